# Optimizing a Trainium2 kernel written in Bass

```python
import jax, jax.numpy as jnp
from jax import lax
import numpy as np

D_MODEL = 1024
BATCH = 32
SEQ = 256
DEPTH = 4
DEC_BATCH = 2
DEC_SEQ = 2048
PAST_LEN = 256

GRID_W = 64
HEAD_DIM = 64
N_HEADS = D_MODEL // HEAD_DIM
N_KV_HEADS = N_HEADS // 4
NA_KH = 8
NA_KW = 16
D_FF = ((8 * D_MODEL // 3 + 127) // 128) * 128
CONV_W = 3
Q_BLOCK = 128
ROPE_BASE = 10000.0
EPS = 1e-6
N_MIXERS = 2

kernel_name = 'hybrid_na_gqa_diffusion_step'


def rmsnorm(x, g):
    xf = x.astype(jnp.float32)
    y = xf * lax.rsqrt(jnp.mean(xf * xf, axis=-1, keepdims=True) + EPS)
    return (y * g.astype(jnp.float32)).astype(x.dtype)


def adaln(cond, w, b):
    m = jax.nn.silu(cond) @ w + b
    return jnp.split(m[..., None, :], 6, axis=-1)


def axial_rope(x):
    t = x.shape[1]
    pos = jnp.arange(t)
    rows = (pos // GRID_W).astype(jnp.float32)
    cols = (pos % GRID_W).astype(jnp.float32)
    half = HEAD_DIM // 2
    quarter = half // 2
    freqs = ROPE_BASE ** (-jnp.arange(quarter, dtype=jnp.float32) / quarter)

    def rot(xp, p):
        ang = p[:, None] * freqs[None, :]
        cos = jnp.cos(ang)[None, :, None, :].astype(x.dtype)
        sin = jnp.sin(ang)[None, :, None, :].astype(x.dtype)
        x1, x2 = xp[..., :quarter], xp[..., quarter:]
        return jnp.concatenate([x1 * cos - x2 * sin, x1 * sin + x2 * cos], axis=-1)

    return jnp.concatenate([rot(x[..., :half], rows), rot(x[..., half:], cols)], axis=-1)


def blocked_attention(q, k, v):
    b, t, h, dh = q.shape
    hkv = k.shape[2]
    g = h // hkv
    nblk = t // Q_BLOCK
    qb = q.reshape(b, nblk, Q_BLOCK, hkv, g, dh).transpose(1, 0, 2, 3, 4, 5)
    scale = dh ** -0.5

    def one(qi):
        s = jnp.einsum('bqkgd,bskd->bkgqs', qi, k).astype(jnp.float32) * scale
        p = jax.nn.softmax(s, axis=-1).astype(v.dtype)
        return jnp.einsum('bkgqs,bskd->bqkgd', p, v)

    o = lax.map(one, qb)
    return o.transpose(1, 0, 2, 3, 4, 5).reshape(b, t, h, dh)


def neighbourhood_attention(q, k, v, k_ctx, v_ctx, rpb):
    b, t, h, dh = q.shape
    rows = t // GRID_W
    kh = min(NA_KH, rows)
    kw = NA_KW
    r = jnp.arange(rows)
    col = jnp.arange(GRID_W)
    r_start = jnp.clip(r - kh // 2, 0, rows - kh)
    row_idx = r_start[:, None] + jnp.arange(kh)[None, :]
    c_start = jnp.clip(col - kw // 2, 0, GRID_W - kw)
    col_ok = (col[None, :] >= c_start[:, None]) & (col[None, :] < c_start[:, None] + kw)
    qg = q.reshape(b, rows, GRID_W, h, dh)
    kg = k.reshape(b, rows, GRID_W, h, dh)[:, row_idx]
    vg = v.reshape(b, rows, GRID_W, h, dh)[:, row_idx]
    scale = dh ** -0.5
    s_lat = jnp.einsum('brqhd,brkwhd->bhrqkw', qg, kg).astype(jnp.float32) * scale
    dr = row_idx - r[:, None] + (NA_KH - 1)
    dc = jnp.clip(col[None, :] - col[:, None] + (NA_KW - 1), 0, 2 * NA_KW - 2)
    bias = rpb[:, dr[:, None, :, None], dc[None, :, None, :]].astype(jnp.float32)
    s_lat = jnp.where(col_ok[None, None, None, :, None, :], s_lat + bias[None], -jnp.inf)
    s_ctx = jnp.einsum('brqhd,bchd->bhrqc', qg, k_ctx).astype(jnp.float32) * scale
    n_lat = kh * GRID_W
    s = jnp.concatenate([s_lat.reshape(b, h, rows, GRID_W, n_lat), s_ctx], axis=-1)
    p = jax.nn.softmax(s, axis=-1).astype(v.dtype)
    p_lat = p[..., :n_lat].reshape(b, h, rows, GRID_W, kh, GRID_W)
    p_ctx = p[..., n_lat:]
    o = jnp.einsum('bhrqkw,brkwhd->brqhd', p_lat, vg) + jnp.einsum('bhrqc,bchd->brqhd', p_ctx, v_ctx)
    return o.reshape(b, t, h, dh)


def na_qkv(h, w_qkv):
    b, t, _ = h.shape
    q, k, v = jnp.split(h @ w_qkv, 3, axis=-1)
    return (q.reshape(b, t, N_HEADS, HEAD_DIM), k.reshape(b, t, N_HEADS, HEAD_DIM),
            v.reshape(b, t, N_HEADS, HEAD_DIM))


def gqa_qkv(h, w_qkv, q_norm, k_norm):
    b, t, _ = h.shape
    qk = N_HEADS * HEAD_DIM
    kv = N_KV_HEADS * HEAD_DIM
    q, k, v = jnp.split(h @ w_qkv, [qk, qk + kv], axis=-1)
    q = rmsnorm(q.reshape(b, t, N_HEADS, HEAD_DIM), q_norm)
    k = rmsnorm(k.reshape(b, t, N_KV_HEADS, HEAD_DIM), k_norm)
    return q, k, v.reshape(b, t, N_KV_HEADS, HEAD_DIM)


def conv_ffn(h, w_up, conv_w, conv_b, w_down):
    u = h @ w_up
    up = jnp.pad(u, ((0, 0), (1, 1), (0, 0)))
    u = up[:, :-2] * conv_w[0] + up[:, 1:-1] * conv_w[1] + up[:, 2:] * conv_w[2] + conv_b
    a, gt = jnp.split(u, 2, axis=-1)
    return (jax.nn.silu(a) * gt) @ w_down


def setup_inputs(seed: int = 0) -> dict:
    key = jax.random.key(seed)
    ks = jax.random.split(key, 24)
    n_a = (DEPTH + N_MIXERS - 1) // N_MIXERS
    n_b = DEPTH // N_MIXERS
    f32 = jnp.float32
    D = D_MODEL

    def nrm(k, shape, scale):
        return jax.random.normal(k, shape, f32) * scale

    def gain(k, shape):
        return 1.0 + 0.1 * jax.random.normal(k, shape, f32)

    return {
        'x_prompt': nrm(ks[0], (BATCH, SEQ, D), 1.0),
        'x_sample': nrm(ks[1], (DEC_BATCH, DEC_SEQ, D), 1.0),
        'cache_na_k': nrm(ks[2], (DEC_BATCH, n_a, PAST_LEN, N_HEADS, HEAD_DIM), 1.0),
        'cache_na_v': nrm(ks[3], (DEC_BATCH, n_a, PAST_LEN, N_HEADS, HEAD_DIM), 1.0),
        'cache_gqa_k': nrm(ks[4], (DEC_BATCH, n_b, PAST_LEN, N_KV_HEADS, HEAD_DIM), 1.0),
        'cache_gqa_v': nrm(ks[5], (DEC_BATCH, n_b, PAST_LEN, N_KV_HEADS, HEAD_DIM), 1.0),
        'c': nrm(ks[6], (DEC_BATCH, D), 1.0),
        'c_ctx': nrm(ks[7], (D,), 1.0),
        'ada_w': nrm(ks[8], (DEPTH, D, 6 * D), 0.5 * D ** -0.5),
        'ada_b': nrm(ks[9], (DEPTH, 6 * D), 0.02),
        'norm_mix_pre': gain(ks[10], (DEPTH, D)),
        'norm_mix_post': gain(ks[11], (DEPTH, D)),
        'norm_ffn_pre': gain(ks[12], (DEPTH, D)),
        'norm_ffn_post': gain(ks[13], (DEPTH, D)),
        'na_w_qkv': nrm(ks[14], (n_a, D, 3 * N_HEADS * HEAD_DIM), D ** -0.5),
        'na_w_o': nrm(ks[15], (n_a, N_HEADS * HEAD_DIM, D), (N_HEADS * HEAD_DIM) ** -0.5),
        'na_rpb': nrm(ks[16], (n_a, N_HEADS, 2 * NA_KH - 1, 2 * NA_KW - 1), 0.5),
        'gqa_w_qkv': nrm(ks[17], (n_b, D, (N_HEADS + 2 * N_KV_HEADS) * HEAD_DIM), D ** -0.5),
        'gqa_w_o': nrm(ks[18], (n_b, N_HEADS * HEAD_DIM, D), (N_HEADS * HEAD_DIM) ** -0.5),
        'gqa_q_norm': gain(ks[19], (n_b, HEAD_DIM)),
        'gqa_k_norm': gain(ks[20], (n_b, HEAD_DIM)),
        'ffn_w_up': nrm(ks[21], (DEPTH, D, 2 * D_FF), D ** -0.5),
        'ffn_conv_w': nrm(ks[22], (DEPTH, CONV_W, 2 * D_FF), CONV_W ** -0.5),
        'ffn_conv_b': nrm(jax.random.fold_in(ks[22], 1), (DEPTH, 2 * D_FF), 0.02),
        'ffn_w_down': nrm(ks[23], (DEPTH, D_FF, D), D_FF ** -0.5),
    }


def reference(x_prompt, x_sample, cache_na_k, cache_na_v, cache_gqa_k, cache_gqa_v, c, c_ctx,
              ada_w, ada_b, norm_mix_pre, norm_mix_post, norm_ffn_pre, norm_ffn_post,
              na_w_qkv, na_w_o, na_rpb, gqa_w_qkv, gqa_w_o, gqa_q_norm, gqa_k_norm,
              ffn_w_up, ffn_conv_w, ffn_conv_b, ffn_w_down):
    xp, xs = x_prompt, x_sample
    bp, tp, _ = xp.shape
    bs, ts, _ = xs.shape
    new_na_k, new_na_v, new_gqa_k, new_gqa_v = [], [], [], []
    for l in range(DEPTH):
        i = l // N_MIXERS
        sh_ap, sc_ap, g_ap, sh_fp, sc_fp, g_fp = adaln(c_ctx, ada_w[l], ada_b[l])
        sh_as, sc_as, g_as, sh_fs, sc_fs, g_fs = adaln(c, ada_w[l], ada_b[l])
        hp = rmsnorm(xp, norm_mix_pre[l]) * (1.0 + sc_ap) + sh_ap
        hs = rmsnorm(xs, norm_mix_pre[l]) * (1.0 + sc_as) + sh_as
        if l % N_MIXERS == 0:
            qp, kp, vp = na_qkv(hp, na_w_qkv[i])
            op = blocked_attention(qp, kp, vp)
            new_na_k.append(kp)
            new_na_v.append(vp)
            qs, ks_, vs = na_qkv(hs, na_w_qkv[i])
            os_ = neighbourhood_attention(qs, ks_, vs, cache_na_k[:, i], cache_na_v[:, i], na_rpb[i])
            w_o = na_w_o[i]
        else:
            qp, kp, vp = gqa_qkv(hp, gqa_w_qkv[i], gqa_q_norm[i], gqa_k_norm[i])
            op = blocked_attention(qp, kp, vp)
            new_gqa_k.append(kp)
            new_gqa_v.append(vp)
            qs, ks_, vs = gqa_qkv(hs, gqa_w_qkv[i], gqa_q_norm[i], gqa_k_norm[i])
            qs = axial_rope(qs)
            ks_ = axial_rope(ks_)
            k_all = jnp.concatenate([ks_, cache_gqa_k[:, i]], axis=1)
            v_all = jnp.concatenate([vs, cache_gqa_v[:, i]], axis=1)
            os_ = blocked_attention(qs, k_all, v_all)
            w_o = gqa_w_o[i]
        xp = xp + g_ap * rmsnorm(op.reshape(bp, tp, -1) @ w_o, norm_mix_post[l])
        xs = xs + g_as * rmsnorm(os_.reshape(bs, ts, -1) @ w_o, norm_mix_post[l])
        hp = rmsnorm(xp, norm_ffn_pre[l]) * (1.0 + sc_fp) + sh_fp
        hs = rmsnorm(xs, norm_ffn_pre[l]) * (1.0 + sc_fs) + sh_fs
        fp = conv_ffn(hp, ffn_w_up[l], ffn_conv_w[l], ffn_conv_b[l], ffn_w_down[l])
        fs = conv_ffn(hs, ffn_w_up[l], ffn_conv_w[l], ffn_conv_b[l], ffn_w_down[l])
        xp = xp + g_fp * rmsnorm(fp, norm_ffn_post[l])
        xs = xs + g_fs * rmsnorm(fs, norm_ffn_post[l])
    na_k_out = jnp.stack(new_na_k, axis=1)
    na_v_out = jnp.stack(new_na_v, axis=1)
    gqa_k_out = jnp.stack(new_gqa_k, axis=1)
    gqa_v_out = jnp.stack(new_gqa_v, axis=1)
    return (xp, xs, na_k_out, na_v_out, gqa_k_out, gqa_v_out)
```

```python
import numpy as np
import ml_dtypes
import concourse.bass as bass
import concourse.mybir as mybir
from concourse.bass_utils import run_bass_kernel_spmd

F32 = mybir.dt.float32
BF16 = mybir.dt.bfloat16
AF = mybir.ActivationFunctionType
ALU = mybir.AluOpType

D = 1024; NC8 = 8; T = 2048; NB = 4; BW = 512; DFF = 2816; NJ = 22; DEPTH = 4
EPS = 1e-6
NEG = -30000.0


class Em:
    def __init__(self, nc):
        self.nc = nc
        self.q = {e: [] for e in ("pe", "act", "dve", "sp", "pool")}
        self.cnt = {e: 0 for e in ("pe", "act", "dve")}
        self.sem = {e: nc.alloc_semaphore("s_" + e) for e in ("pe", "act", "dve")}
        self.P = 8
        self.dq = {}
        for qn in ("sp", "pool"):
            self.dq[qn] = {"sems": [nc.alloc_semaphore("d_%s%d" % (qn, i)) for i in range(self.P)], "k": 0}
        self.waited = {}
        self.lastw = {}
        self.readers = {}

    def _deps(self, reads, writes):
        d = []
        for r in reads:
            if r in self.lastw:
                d.append(self.lastw[r])
        for w in writes:
            if w in self.lastw:
                d.append(self.lastw[w])
            d.extend(self.readers.get(w, ()))
        return d

    def _waits(self, eng, deps):
        for (sem, val, name, src) in deps:
            if src == eng and eng == "pe":
                continue
            key = (eng, name)
            if self.waited.get(key, 0) >= val:
                continue
            self.waited[key] = val
            self.q[eng].append(lambda e, sem=sem, val=val: e.wait_ge(sem, val))

    def _record(self, tok, reads, writes):
        for w in writes:
            self.lastw[w] = tok
            self.readers[w] = []
        for r in reads:
            if r not in writes:
                self.readers.setdefault(r, []).append(tok)

    PSUM_KEYS = frozenset(['pA', 'pB', 'pS0', 'pS1', 'pO', 'pD', 'pN', 'pT'])

    def op(self, eng, fn, reads=(), writes=()):
        px = [r for r in reads if r in self.PSUM_KEYS and r not in writes]
        if px:
            writes = list(writes) + px
        self._waits(eng, self._deps(reads, writes))
        self.cnt[eng] += 1
        sem = self.sem[eng]
        self.q[eng].append(lambda e, fn=fn, sem=sem: fn(e).then_inc(sem, 1))
        self._record((sem, self.cnt[eng], "s_" + eng, eng), reads, writes)

    def dma(self, qn, fn, reads=(), writes=()):
        dq = self.dq[qn]
        k = dq["k"]; dq["k"] += 1
        slot = k % self.P
        sem = dq["sems"][slot]
        name = "d_%s%d" % (qn, slot)
        deps = self._deps(reads, writes)
        need = 16 * (k // self.P)
        if need > 0:
            deps.append((sem, need, name, "dma"))
        self._waits(qn, deps)
        self.q[qn].append(lambda e, fn=fn, sem=sem: fn(e).then_inc(sem, 16))
        self._record((sem, need + 16, name, "dma"), reads, writes)

    def finish(self):
        deps = []
        for qn in ("sp", "pool"):
            dq = self.dq[qn]
            for slot in range(self.P):
                n = (dq["k"] - slot + self.P - 1) // self.P if dq["k"] > slot else 0
                if n > 0:
                    deps.append((dq["sems"][slot], 16 * n, "d_%s%d" % (qn, slot), "dma"))
        for e in ("pe", "act", "dve"):
            if self.cnt[e]:
                deps.append((self.sem[e], self.cnt[e], "s_" + e, e))
        self._waits("sp", deps)

    def build(self):
        nc = self.nc
        with nc.Block() as block:
            @block.sync
            def _(e):
                for f in self.q["sp"]:
                    f(e)

            @block.gpsimd
            def _(e):
                for f in self.q["pool"]:
                    f(e)

            @block.tensor
            def _(e):
                for f in self.q["pe"]:
                    f(e)

            @block.scalar
            def _(e):
                for f in self.q["act"]:
                    f(e)

            @block.vector
            def _(e):
                for f in self.q["dve"]:
                    f(e)


class Rot:
    def __init__(self, items):
        self.items = items; self.i = 0

    def next(self):
        it = self.items[self.i % len(self.items)]; self.i += 1
        return it


def build_program(n_layers=DEPTH, stop_after=None):
    nc = bass.Bass("TRN2", target_bir_lowering=False)
    em = Em(nc)

    def din(name, shape, dt=F32):
        return nc.dram_tensor(name, list(shape), dt, kind="ExternalInput").ap()

    def dout(name, shape):
        return nc.dram_tensor(name, list(shape), F32, kind="ExternalOutput").ap()

    x_d = din("x", [T, D]); cond_d = din("cond", [128, 8])
    adaw_d = din("ada_w", [DEPTH, D, 6 * D]); adab_d = din("ada_b", [128, DEPTH * 48])
    gains_d = din("gains", [128, DEPTH * 32])
    wqkvn_d = din("wqkv_na", [2, D, 3072]); won_d = din("wo_na", [2, D, D])
    wqkvg_d = din("wqkv_g", [2, D, 2048]); wog_d = din("wo_g", [2, D, D])
    qkg_d = din("qkg", [128, 4])
    wup_d = din("w_up", [DEPTH, D, 2 * DFF]); wdn_d = din("w_down", [DEPTH, DFF, D])
    convp_d = din("convp", [DEPTH, 128, 44 * 4])
    bias_d = din("biasT", [2, 16, 128, 1408], BF16)
    indA_d = din("indA", [2, 32, 2304], BF16); indB_d = din("indB", [2, 32, 2048], BF16)
    ckn_d = din("ctxkT_na", [2, 128, 8 * 256]); cvn_d = din("ctxv_na", [2, 256, 1024])
    ckg_d = din("ctxkT_g", [2, 128, 4 * 256]); cvg_d = din("ctxv_g", [2, 256, 512])
    cos_d = din("cosT", [128, T]); sin_d = din("sinT", [128, T])
    cneg_d = din("cneg", [128, 1])
    idf_d = din("ident_f", [128, 128]); cb_d = din("constb", [128, 4 * 128], BF16)
    y_d = dout("y", [T, D])
    nkn_d = dout("nk_na", [2, T, D]); nvn_d = dout("nv_na", [2, T, D])
    nkg_d = dout("nk_g", [2, T, 256]); nvg_d = dout("nv_g", [2, T, 256])
    qT_d = nc.dram_tensor("qT_s", [8, 128, T], BF16, kind="ExternalOutput").ap()
    kT_d = nc.dram_tensor("kT_s", [8, 128, T], BF16, kind="ExternalOutput").ap()
    vS_d = nc.dram_tensor("vS_s", [T, D], BF16, kind="ExternalOutput").ap()
    oT_d = nc.dram_tensor("oT_s", [8, 128, T], BF16, kind="ExternalOutput").ap()
    h2_d = nc.dram_tensor("h2_s", [8, 128, T], BF16, kind="ExternalOutput").ap()

    def sb(name, shape, dt=F32):
        return nc.alloc_sbuf_tensor("sb_" + name, list(shape), dt).ap()

    xT = sb("xT", [128, 8, T])
    big2 = sb("big2", [128, NJ, BW], BF16)
    hT = big2[:, 0:8, :]
    blkin = sb("blkin", [128, 8, BW + 2], BF16)
    yT = sb("yT", [128, 8, BW])
    idf = sb("idf", [128, 128]); cb = sb("cb", [128, 4 * 128], BF16)
    idb = cb[:, 0:128]; onesb = cb[:, 128:256]; blkb = cb[:, 256:384]; rotb = cb[:, 384:512]
    cond_s = sb("cond_s", [128, 8]); silc = sb("silc", [128, 8], BF16)
    adab = sb("adab", [128, DEPTH * 48]); gains = sb("gains", [128, DEPTH * 32])
    qkg = sb("qkg", [128, 4]); cneg = sb("cneg", [128, 1])
    convp = sb("convp", [128, 44 * 4]); wcor = sb("wcor", [128, 44 * 2])
    mod = sb("mod", [128, 48]); drv = sb("drv", [128, 32])
    epsD = sb("epsD", [128, 1]); zerob = sb("zerob", [128, 8], BF16)
    indA = sb("indA", [32, 2304], BF16); indB = sb("indB", [32, 2048], BF16)
    kt_sb = sb("kt_sb", [128, 2304], BF16); v_sb = sb("v_sb", [128, 18, 128], BF16)
    qe = [sb("qe%d" % i, [128, BW], BF16) for i in range(2)]
    qo = [sb("qo%d" % i, [128, BW], BF16) for i in range(2)]
    bias_sb = [sb("bias%d" % i, [128, 1408], BF16) for i in range(2)]
    p_rot = Rot([("p%d" % i, sb("p%d" % i, [128, BW], BF16)) for i in range(3)])
    wsm_rot = Rot([("wsm%d" % i, sb("wsm%d" % i, [128, 8, 128], BF16)) for i in range(4)])
    wbg_rot = Rot([("wbg%d" % i, sb("wbg%d" % i, [128, 8, 512], BF16)) for i in range(2)])
    f_rot = Rot([("f%d" % i, sb("f%d" % i, [128, BW])) for i in range(6)])
    b_rot = Rot([("b%d" % i, sb("b%d" % i, [128, BW], BF16)) for i in range(4)])
    u_rot = Rot([("u%d" % i, sb("u%d" % i, [128, BW + 2])) for i in range(2)])
    acc_rot = Rot([("acc%d" % i, sb("acc%d" % i, [128, BW])) for i in range(3)])
    cos_sb = sb("cos_sb", [128, BW]); sin_sb = sb("sin_sb", [128, BW])
    rstd = sb("rstd", [128, BW])
    ostg_rot = Rot([("ostg%d" % i, sb("ostg%d" % i, [128, BW], BF16)) for i in range(2)])
    nkst = sb("nkst", [128, 4, 64])

    def ps(name):
        return nc.alloc_psum_tensor("ps_" + name, [128, 512], F32).ap()
    pA_rot = Rot([("pA", ps("pA")), ("pB", ps("pB"))])
    pS_rot = Rot([("pS0", ps("pS0")), ("pS1", ps("pS1"))])
    pO = ps("pO"); pD = ps("pD"); pN = ps("pN"); pT = ps("pT")
    pF_rot = Rot(pA_rot.items + [("pO", pO), ("pD", pD)])
    pH_rot = Rot([("pT", pT)] + pS_rot.items)
    oacc_rot = Rot([("pO", pO, "pD", pD), ("pN", pN, "pT", pT)])

    def blkc(b):
        return slice(b * BW, (b + 1) * BW)

    def wpiece_small(src_ap):
        k, t = wsm_rot.next()
        em.dma("pool", lambda e: e.dma_start(out=t, in_=src_ap.rearrange("(kc p) n -> p kc n", p=128)), writes=[k])
        return k, t

    def wpiece_big(src_ap, nk=8, ncol=512):
        k, t = wbg_rot.next()
        if nk == 8 and ncol == 512:
            view = t
        else:
            view = nc_view(t, nk, ncol)
        em.dma("pool", lambda e: e.dma_start(out=view, in_=src_ap.rearrange("(kc p) n -> p kc n", p=128)), writes=[k])
        return k, view

    def nc_view(t, nk, ncol):
        flat = t.rearrange("p a b -> p (a b)")
        return flat[:, 0:nk * ncol].rearrange("p (a b) -> p a b", b=ncol)

    def rms_rstd(src_fn, src_keys, inv_n):
        for c in range(8):
            kq, sq = b_rot.next()
            em.op("act", lambda e, c=c, sq=sq: e.activation(out=sq, in_=src_fn(c), func=AF.Square), reads=src_keys, writes=[kq])
            em.op("pe", lambda e, c=c, sq=sq: e.matmul(pN, onesb, sq, start=(c == 0), stop=(c == 7)), reads=[kq, "cb"], writes=["pN"])
        kf, tf = f_rot.next()
        em.op("act", lambda e: e.activation(out=tf, in_=pN, func=AF.Ln, bias=epsD[:, 0:1], scale=inv_n), reads=["pN", "epsD"], writes=[kf])
        em.op("act", lambda e: e.activation(out=rstd, in_=tf, func=AF.Exp, scale=-0.5), reads=[kf], writes=["rstd"])

    def modulate(b, Acol, Bcol, dst_fn, dst_key):
        for c in range(8):
            kf, tf = f_rot.next()
            em.op("dve", lambda e, c=c, tf=tf: e.scalar_tensor_tensor(out=tf, in0=xT[:, c, blkc(b)], scalar=drv[:, Acol + c:Acol + c + 1], in1=rstd, op0=ALU.mult, op1=ALU.mult),
                  reads=["xT", "drv", "rstd"], writes=[kf])
            em.op("act", lambda e, c=c, tf=tf: e.activation(out=dst_fn(c), in_=tf, func=AF.Identity, bias=mod[:, Bcol + c:Bcol + c + 1], scale=1.0),
                  reads=[kf, "mod"], writes=[dst_key])

    def postnorm_residual(b, Gcol):
        rms_rstd(lambda c: yT[:, c, :], ["yT"], 1.0 / D)
        for c in range(8):
            kf, tf = f_rot.next()
            em.op("dve", lambda e, c=c, tf=tf: e.scalar_tensor_tensor(out=tf, in0=yT[:, c, :], scalar=drv[:, Gcol + c:Gcol + c + 1], in1=rstd, op0=ALU.mult, op1=ALU.mult),
                  reads=["yT", "drv", "rstd"], writes=[kf])
            em.op("dve", lambda e, c=c, tf=tf: e.tensor_tensor(out=xT[:, c, blkc(b)], in0=xT[:, c, blkc(b)], in1=tf, op=ALU.add),
                  reads=[kf, "xT"], writes=["xT"])

    for (dst, src, key) in [(idf, idf_d, "idf"), (cb, cb_d, "cb"), (cond_s, cond_d, "cond"), (adab, adab_d, "adab"),
                            (gains, gains_d, "gains"), (qkg, qkg_d, "qkg"), (cneg, cneg_d, "cneg")]:
        em.dma("sp", lambda e, dst=dst, src=src: e.dma_start(out=dst, in_=src), writes=[key])
    em.op("dve", lambda e: e.memset(epsD, EPS), writes=["epsD"])
    em.op("dve", lambda e: e.memset(zerob, 0.0), writes=["zerob"])
    for i in range(2):
        em.op("dve", lambda e, i=i: e.memset(qe[i], 0.0), writes=["qe%d" % i])
        em.op("dve", lambda e, i=i: e.memset(qo[i], 0.0), writes=["qo%d" % i])
    em.op("act", lambda e: e.activation(out=silc, in_=cond_s, func=AF.Silu), reads=["cond"], writes=["silc"])
    yflat = yT.rearrange("p a b -> p (a b)")
    xin = [yflat[:, 0:1024], yflat[:, 1024:2048]]
    for t in range(16):
        xi = xin[t % 2]; kx = "xin%d" % (t % 2)
        em.dma("sp", lambda e, t=t, xi=xi: e.dma_start(out=xi, in_=x_d[t * 128:(t + 1) * 128, :]), writes=[kx, "yT"])
        for g4 in range(2):
            for c4 in range(4):
                c = g4 * 4 + c4
                em.op("pe", lambda e, c=c, c4=c4, xi=xi: e.transpose(pT[:, c4 * 128:(c4 + 1) * 128], xi[:, c * 128:(c + 1) * 128], idf), reads=[kx, "idf"], writes=["pT"])
            em.op("dve", lambda e, g4=g4, t=t: e.tensor_copy(out=xT[:, g4 * 4:(g4 + 1) * 4, t * 128:(t + 1) * 128], in_=pT.rearrange("p (a b) -> p a b", b=128)),
                  reads=["pT"], writes=["xT"])

    def do_layer(l):
        i = l // 2
        is_na = (l % 2 == 0)
        for pi in range(12):
            kw, wv = wpiece_big(adaw_d[l][:, pi * 512:(pi + 1) * 512])
            for oc4 in range(4):
                ch = pi * 4 + oc4
                for kc in range(8):
                    em.op("pe", lambda e, wv=wv, oc4=oc4, kc=kc, ch=ch: e.matmul(pT[:, ch:ch + 1], wv[:, kc, oc4 * 128:(oc4 + 1) * 128], silc[:, kc:kc + 1], start=(kc == 0), stop=(kc == 7)),
                          reads=[kw, "silc"], writes=["pT"])
        em.op("dve", lambda e, l=l: e.tensor_tensor(out=mod, in0=pT[:, 0:48], in1=adab[:, l * 48:(l + 1) * 48], op=ALU.add), reads=["pT", "adab"], writes=["mod"])
        g0 = l * 32
        em.op("dve", lambda e: e.scalar_tensor_tensor(out=drv[:, 0:8], in0=mod[:, 8:16], scalar=1.0, in1=gains[:, g0:g0 + 8], op0=ALU.add, op1=ALU.mult), reads=["mod", "gains"], writes=["drv"])
        em.op("dve", lambda e: e.tensor_tensor(out=drv[:, 8:16], in0=mod[:, 16:24], in1=gains[:, g0 + 8:g0 + 16], op=ALU.mult), reads=["mod", "gains"], writes=["drv"])
        em.op("dve", lambda e: e.scalar_tensor_tensor(out=drv[:, 16:24], in0=mod[:, 32:40], scalar=1.0, in1=gains[:, g0 + 16:g0 + 24], op0=ALU.add, op1=ALU.mult), reads=["mod", "gains"], writes=["drv"])
        em.op("dve", lambda e: e.tensor_tensor(out=drv[:, 24:32], in0=mod[:, 40:48], in1=gains[:, g0 + 24:g0 + 32], op=ALU.mult), reads=["mod", "gains"], writes=["drv"])
        em.dma("sp", lambda e, l=l: e.dma_start(out=convp, in_=convp_d[l]), writes=["convp"])
        ty = 0 if is_na else 1
        em.dma("sp", lambda e, ty=ty: e.dma_start(out=indA, in_=indA_d[ty]), writes=["indA"])
        em.dma("sp", lambda e, ty=ty: e.dma_start(out=indB, in_=indB_d[ty]), writes=["indB"])
        cp3 = convp.rearrange("p (j f) -> p j f", f=4)
        wc3 = wcor.rearrange("p (j f) -> p j f", f=2)
        em.op("dve", lambda e: e.tensor_scalar(out=wc3[:, :, 0:1], in0=cp3[:, :, 0:1], scalar1=cneg[:, 0:1], scalar2=None, op0=ALU.mult), reads=["convp", "cneg"], writes=["wcor"])
        em.op("dve", lambda e: e.tensor_scalar(out=wc3[:, :, 1:2], in0=cp3[:, :, 2:3], scalar1=cneg[:, 0:1], scalar2=None, op0=ALU.mult), reads=["convp", "cneg"], writes=["wcor"])

        if stop_after == 'ada':
            return True
        wqkv = (wqkvn_d if is_na else wqkvg_d)[i]
        wo = (won_d if is_na else wog_d)[i]
        nqk = 16 if is_na else 12

        for b in range(NB):
            rms_rstd(lambda c, b=b: xT[:, c, blkc(b)], ["xT"], 1.0 / D)
            if stop_after == 'rms':
                break
            modulate(b, 0, 0, lambda c: hT[:, c, :], "big2")
            if stop_after == 'mod':
                break
            if not is_na:
                em.dma("sp", lambda e, b=b: e.dma_start(out=cos_sb, in_=cos_d[:, blkc(b)]), writes=["cos"])
                em.dma("sp", lambda e, b=b: e.dma_start(out=sin_sb, in_=sin_d[:, blkc(b)]), writes=["sin"])
            for oc in range(nqk):
                kw, wv = wpiece_small(wqkv[:, oc * 128:(oc + 1) * 128])
                kp, pa = pA_rot.next()
                for kc in range(8):
                    em.op("pe", lambda e, wv=wv, kc=kc, pa=pa: e.matmul(pa, wv[:, kc, :], hT[:, kc, :], start=(kc == 0), stop=(kc == 7)), reads=[kw, "big2"], writes=[kp])
                is_q = oc < 8
                dst = (qT_d if is_q else kT_d)[oc if is_q else oc - 8][:, blkc(b)]
                dkey = "qTd" if is_q else "kTd"
                kb, tb = b_rot.next()
                if is_na:
                    em.op("act", lambda e, pa=pa, tb=tb, is_q=is_q: e.activation(out=tb, in_=pa, func=AF.Identity, scale=(0.125 if is_q else 1.0)), reads=[kp], writes=[kb])
                else:
                    gcol = 0 if is_q else 1
                    kxf, xf = f_rot.next()
                    em.op("act", lambda e, pa=pa, xf=xf: e.activation(out=xf, in_=pa, func=AF.Identity), reads=[kp], writes=[kxf])
                    ksq, sq = b_rot.next()
                    em.op("act", lambda e, pa=pa, sq=sq: e.activation(out=sq, in_=pa, func=AF.Square), reads=[kp], writes=[ksq])
                    em.op("pe", lambda e, sq=sq: e.matmul(pN, blkb, sq, start=True, stop=True), reads=[ksq, "cb"], writes=["pN"])
                    kr, rr = f_rot.next()
                    em.op("act", lambda e, rr=rr: e.activation(out=rr, in_=pN, func=AF.Ln, bias=epsD[:, 0:1], scale=1.0 / 64.0), reads=["pN", "epsD"], writes=[kr])
                    em.op("act", lambda e, rr=rr: e.activation(out=rr, in_=rr, func=AF.Exp, scale=-0.5), reads=[kr], writes=[kr])
                    em.op("dve", lambda e, xf=xf, rr=rr, gcol=gcol: e.scalar_tensor_tensor(out=xf, in0=xf, scalar=qkg[:, 2 * i + gcol:2 * i + gcol + 1], in1=rr, op0=ALU.mult, op1=ALU.mult),
                          reads=[kxf, kr, "qkg"], writes=[kxf])
                    kxb, xb = b_rot.next()
                    em.op("act", lambda e, xf=xf, xb=xb: e.activation(out=xb, in_=xf, func=AF.Identity), reads=[kxf], writes=[kxb])
                    kp2, pr = pA_rot.next()
                    em.op("pe", lambda e, xb=xb, pr=pr: e.matmul(pr, rotb, xb, start=True, stop=True), reads=[kxb, "cb"], writes=[kp2])
                    em.op("dve", lambda e, rr=rr, pr=pr: e.tensor_tensor(out=rr, in0=pr, in1=sin_sb, op=ALU.mult), reads=[kp2, "sin"], writes=[kr])
                    em.op("dve", lambda e, xf=xf: e.tensor_tensor(out=xf, in0=xf, in1=cos_sb, op=ALU.mult), reads=[kxf, "cos"], writes=[kxf])
                    em.op("dve", lambda e, xf=xf, rr=rr: e.tensor_tensor(out=xf, in0=xf, in1=rr, op=ALU.add), reads=[kxf, kr], writes=[kxf])
                    em.op("act", lambda e, xf=xf, tb=tb, is_q=is_q: e.activation(out=tb, in_=xf, func=AF.Identity, scale=(0.125 if is_q else 1.0)), reads=[kxf], writes=[kb])
                    if not is_q:
                        g = oc - 8
                        for tt in range(4):
                            em.op("pe", lambda e, xf=xf, tt=tt: e.transpose(pT[:, tt * 128:(tt + 1) * 128], xf[:, tt * 128:(tt + 1) * 128], idf), reads=[kxf, "idf"], writes=["pT"])
                        em.op("dve", lambda e: e.tensor_copy(out=nkst, in_=pT.rearrange("p (a b) -> p a b", b=128)[:, :, 0:64]), reads=["pT"], writes=["nkst"])
                        em.dma("sp", lambda e, g=g, b=b: e.dma_start(out=nkg_d[i][b * BW:(b + 1) * BW, g * 64:(g + 1) * 64].rearrange("(a p) n -> p a n", p=128), in_=nkst),
                               reads=["nkst"], writes=["nkg_out"])
                em.dma("sp", lambda e, dst=dst, tb=tb: e.dma_start(out=dst, in_=tb), reads=[kb], writes=[dkey])
            if stop_after == 'qk':
                break
            if is_na:
                pieces = [(1024, nkn_d, 0, False), (1536, nkn_d, 512, False), (2048, nvn_d, 0, True), (2560, nvn_d, 512, True)]
            else:
                pieces = [(1536, None, 0, True)]
            for (col0, od, ocol, isv) in pieces:
                kw, wv = wpiece_big(wqkv[:, col0:col0 + 512])
                for tt in range(4):
                    kp, pa = pA_rot.next()
                    for kc in range(8):
                        em.op("pe", lambda e, wv=wv, kc=kc, pa=pa, tt=tt: e.matmul(pa, hT[:, kc, tt * 128:(tt + 1) * 128], wv[:, kc, :], start=(kc == 0), stop=(kc == 7)), reads=[kw, "big2"], writes=[kp])
                    r0 = b * BW + tt * 128
                    kf, tf = f_rot.next()
                    em.op("act", lambda e, pa=pa, tf=tf: e.activation(out=tf, in_=pa, func=AF.Identity), reads=[kp], writes=[kf])
                    if is_na:
                        em.dma("sp", lambda e, od=od, r0=r0, ocol=ocol, tf=tf: e.dma_start(out=od[i][r0:r0 + 128, ocol:ocol + 512], in_=tf), reads=[kf], writes=["nkv_out"])
                    else:
                        em.dma("sp", lambda e, r0=r0, tf=tf: e.dma_start(out=nvg_d[i][r0:r0 + 128, :].rearrange("p (g n) -> p g n", n=64), in_=tf.rearrange("p (g n) -> p g n", n=128)[:, :, 0:64]),
                               reads=[kf], writes=["nkv_out"])
                    if isv:
                        kb, tb = b_rot.next()
                        em.op("dve", lambda e, tf=tf, tb=tb: e.tensor_copy(out=tb, in_=tf), reads=[kf], writes=[kb])
                        em.dma("sp", lambda e, r0=r0, ocol=ocol, tb=tb: e.dma_start(out=vS_d[r0:r0 + 128, ocol:ocol + 512], in_=tb), reads=[kb], writes=["vSd"])
            if stop_after == 'kv0':
                break

        if stop_after in ('proj', 'rms', 'mod', 'qk', 'kv0'):
            return True
        for c in range(8):
            kvc = c if is_na else c // 2
            em.dma("sp", lambda e, kvc=kvc: e.dma_start(out=kt_sb[:, 0:T], in_=kT_d[kvc]), reads=["kTd"], writes=["kt_sb"])
            if is_na:
                em.dma("pool", lambda e, c=c: e.dma_start(out=kt_sb[:, T:T + 256], in_=ckn_d[i][:, c * 256:(c + 1) * 256]), writes=["kt_sb"])
                em.dma("pool", lambda e, c=c: e.dma_start(out=v_sb[:, 16:18, :], in_=cvn_d[i][:, c * 128:(c + 1) * 128].rearrange("(a p) n -> p a n", p=128)), writes=["v_sb"])
            else:
                em.dma("pool", lambda e, kvc=kvc: e.dma_start(out=kt_sb[:, T:T + 256], in_=ckg_d[i][:, kvc * 256:(kvc + 1) * 256]), writes=["kt_sb"])
                em.dma("pool", lambda e, kvc=kvc: e.dma_start(out=v_sb[:, 16:18, :], in_=cvg_d[i][:, kvc * 128:(kvc + 1) * 128].rearrange("(a p) n -> p a n", p=128)), writes=["v_sb"])
            em.dma("sp", lambda e, kvc=kvc: e.dma_start(out=v_sb[:, 0:16, :], in_=vS_d[:, kvc * 128:(kvc + 1) * 128].rearrange("(a p) n -> p a n", p=128)), reads=["vSd"], writes=["v_sb"])
            if is_na:
                for par in range(2):
                    em.dma("sp", lambda e, par=par, c=c: e.dma_start(out=bias_sb[par], in_=bias_d[i][2 * c + par]), writes=["bias%d" % par])
            for b in range(NB):
                qi = (c * NB + b) % 2
                em.dma("sp", lambda e, c=c, b=b, qi=qi: e.dma_start(out=qe[qi][0:64, :], in_=qT_d[c][0:64, blkc(b)]), reads=["qTd"], writes=["qe%d" % qi])
                em.dma("sp", lambda e, c=c, b=b, qi=qi: e.dma_start(out=qo[qi][64:128, :], in_=qT_d[c][64:128, blkc(b)]), reads=["qTd"], writes=["qo%d" % qi])
                if is_na:
                    tiles = [t for t in range(4 * b - 2, 4 * b + 6) if 0 <= t < 16] + [16, 17]
                else:
                    tiles = list(range(18))
                for par in range(2):
                    qm = (qe if par == 0 else qo)[qi]; kq = ("qe%d" if par == 0 else "qo%d") % qi

                    def issue_S(kt, qm=qm, kq=kq, par=par, b=b):
                        ksn, psn = pS_rot.next()
                        has_bias = is_na and kt < 16
                        em.op("pe", lambda e, psn=psn, kt=kt, qm=qm: e.matmul(psn, kt_sb[:, kt * 128:(kt + 1) * 128], qm, start=True, stop=False), reads=["kt_sb", kq], writes=[ksn])
                        em.op("pe", lambda e, psn=psn, kt=kt, b=b, has_bias=has_bias: e.matmul(psn, indA[:, kt * 128:(kt + 1) * 128], indB[:, blkc(b)], start=False, stop=(not has_bias)),
                              reads=["indA", "indB"], writes=[ksn])
                        if has_bias:
                            e0 = 10 - 2 * (kt - 4 * b)
                            em.op("pe", lambda e, psn=psn, e0=e0, par=par: e.matmul(psn, idb, bias_sb[par][:, e0 * 64:e0 * 64 + 512], start=False, stop=True), reads=["cb", "bias%d" % par], writes=[ksn])
                        return ksn, psn
                    (kO, pOa, kDn, pDa) = oacc_rot.next()
                    nxt = issue_S(tiles[0])
                    for ti, kt in enumerate(tiles):
                        ksn, psn = nxt
                        if ti + 1 < len(tiles):
                            nxt = issue_S(tiles[ti + 1])
                        kpb, pb = p_rot.next()
                        em.op("act", lambda e, psn=psn, pb=pb: e.activation(out=pb, in_=psn, func=AF.Exp), reads=[ksn], writes=[kpb])
                        first = (ti == 0); last = (ti == len(tiles) - 1)
                        em.op("pe", lambda e, pb=pb, kt=kt, first=first, last=last, pOa=pOa: e.matmul(pOa, v_sb[:, kt, :], pb, start=first, stop=last), reads=[kpb, "v_sb"], writes=[kO])
                        em.op("pe", lambda e, pb=pb, first=first, last=last, pDa=pDa: e.matmul(pDa, onesb, pb, start=first, stop=last), reads=[kpb, "cb"], writes=[kDn])
                    hs = slice(par * 64, par * 64 + 64)
                    kf, tf = f_rot.next()
                    em.op("act", lambda e, tf=tf, hs=hs, pDa=pDa: e.activation(out=tf[hs, :], in_=pDa[hs, :], func=AF.Ln), reads=[kDn], writes=[kf])
                    em.op("act", lambda e, tf=tf, hs=hs: e.activation(out=tf[hs, :], in_=tf[hs, :], func=AF.Exp, scale=-1.0), reads=[kf], writes=[kf])
                    ko, to = ostg_rot.next()
                    em.op("dve", lambda e, tf=tf, to=to, hs=hs, pOa=pOa: e.tensor_tensor(out=to[hs, :], in0=pOa[hs, :], in1=tf[hs, :], op=ALU.mult), reads=[kO, kf], writes=[ko])
                    em.dma("sp", lambda e, to=to, hs=hs, c=c, b=b: e.dma_start(out=oT_d[c][hs, blkc(b)], in_=to[hs, :]), reads=[ko], writes=["oTd"])

        if stop_after == 'attn':
            return True
        for b in range(NB):
            em.dma("sp", lambda e, b=b: e.dma_start(out=blkin[:, :, 0:BW], in_=oT_d[:, :, blkc(b)].rearrange("c p n -> p c n")), reads=["oTd"], writes=["blkin"])
            for oc in range(8):
                kw, wv = wpiece_small(wo[:, oc * 128:(oc + 1) * 128])
                kp, pa = pA_rot.next()
                for kc in range(8):
                    em.op("pe", lambda e, wv=wv, kc=kc, pa=pa: e.matmul(pa, wv[:, kc, :], blkin[:, kc, 0:BW], start=(kc == 0), stop=(kc == 7)), reads=[kw, "blkin"], writes=[kp])
                em.op("act", lambda e, pa=pa, oc=oc: e.activation(out=yT[:, oc, :], in_=pa, func=AF.Identity), reads=[kp], writes=["yT"])
            postnorm_residual(b, 8)

        if stop_after == 'wo':
            return True
        for b in range(NB):
            rms_rstd(lambda c, b=b: xT[:, c, blkc(b)], ["xT"], 1.0 / D)
            modulate(b, 16, 24, lambda c: hT[:, c, :], "big2")
            em.dma("sp", lambda e, b=b: e.dma_start(out=h2_d[:, :, blkc(b)].rearrange("c p n -> p c n"), in_=hT), reads=["big2"], writes=["h2d"])
        for b in range(NB):
            lo = max(b * BW - 1, 0); hi = min((b + 1) * BW + 1, T)
            o0 = lo - (b * BW - 1)
            em.dma("sp", lambda e, lo=lo, hi=hi, o0=o0: e.dma_start(out=blkin[:, :, o0:o0 + hi - lo], in_=h2_d[:, :, lo:hi].rearrange("c p n -> p c n")), reads=["h2d"], writes=["blkin"])
            if b == 0:
                em.op("dve", lambda e: e.memset(blkin[:, :, 0:1], 0.0), writes=["blkin"])
            if b == NB - 1:
                em.op("dve", lambda e: e.memset(blkin[:, :, BW + 1:BW + 2], 0.0), writes=["blkin"])
            for j in range(NJ):
                accs = []
                for half in range(2):
                    jj = half * NJ + j
                    col0 = half * DFF + j * 128
                    kw, wv = wpiece_small(wup_d[l][:, col0:col0 + 128])
                    kp, pa = pF_rot.next()
                    kh, ph = pH_rot.next()
                    for kc in range(8):
                        em.op("pe", lambda e, wv=wv, kc=kc, pa=pa: e.matmul(pa, wv[:, kc, :], blkin[:, kc, 1:BW + 1], start=(kc == 0), stop=(kc == 7)), reads=[kw, "blkin"], writes=[kp])
                    for kc in range(8):
                        em.op("pe", lambda e, wv=wv, kc=kc, ph=ph: e.matmul(ph[:, 0:2], wv[:, kc, :], blkin[:, kc, 0:BW + 2:BW + 1], start=(kc == 0), stop=(kc == 7)), reads=[kw, "blkin"], writes=[kh])
                    ka, ac = acc_rot.next()
                    ku, us = u_rot.next()
                    em.op("act", lambda e, pa=pa, ac=ac, jj=jj: e.activation(out=ac, in_=pa, func=AF.Identity, bias=convp[:, jj * 4 + 3:jj * 4 + 4], scale=convp[:, jj * 4 + 1:jj * 4 + 2]), reads=[kp, "convp"], writes=[ka])
                    em.op("act", lambda e, pa=pa, us=us: e.activation(out=us[:, 1:BW + 1], in_=pa, func=AF.Identity), reads=[kp], writes=[ku])
                    em.op("dve", lambda e, us=us, ph=ph: e.tensor_copy(out=us[:, 0:BW + 2:BW + 1], in_=ph[:, 0:2]), reads=[kh], writes=[ku])
                    em.op("dve", lambda e, us=us, ac=ac, jj=jj: e.scalar_tensor_tensor(out=ac, in0=us[:, 0:BW], scalar=convp[:, jj * 4:jj * 4 + 1], in1=ac, op0=ALU.mult, op1=ALU.add), reads=[ku, ka, "convp"], writes=[ka])
                    em.op("dve", lambda e, us=us, ac=ac, jj=jj: e.scalar_tensor_tensor(out=ac, in0=us[:, 2:BW + 2], scalar=convp[:, jj * 4 + 2:jj * 4 + 3], in1=ac, op0=ALU.mult, op1=ALU.add), reads=[ku, ka, "convp"], writes=[ka])
                    em.op("dve", lambda e, us=us, ac=ac, jj=jj: e.scalar_tensor_tensor(out=ac[:, 0:BW:256], in0=us[:, 0:BW:256], scalar=wcor[:, jj * 2:jj * 2 + 1], in1=ac[:, 0:BW:256], op0=ALU.mult, op1=ALU.add), reads=[ku, ka, "wcor"], writes=[ka])
                    em.op("dve", lambda e, us=us, ac=ac, jj=jj: e.scalar_tensor_tensor(out=ac[:, 255:BW:256], in0=us[:, 257:BW + 2:256], scalar=wcor[:, jj * 2 + 1:jj * 2 + 2], in1=ac[:, 255:BW:256], op0=ALU.mult, op1=ALU.add), reads=[ku, ka, "wcor"], writes=[ka])
                    accs.append((ka, ac))
                (kaa, aa), (kag, ag) = accs
                em.op("act", lambda e, aa=aa: e.activation(out=aa, in_=aa, func=AF.Silu), reads=[kaa], writes=[kaa])
                em.op("dve", lambda e, aa=aa, ag=ag, j=j: e.tensor_tensor(out=big2[:, j, :], in0=aa, in1=ag, op=ALU.mult), reads=[kaa, kag], writes=["big2"])
            for oc in range(8):
                kw, wv = wpiece_big(wdn_d[l][:, oc * 128:(oc + 1) * 128], nk=NJ, ncol=128)
                kp, pa = pA_rot.next()
                for j in range(NJ):
                    em.op("pe", lambda e, wv=wv, j=j, pa=pa: e.matmul(pa, wv[:, j, :], big2[:, j, :], start=(j == 0), stop=(j == NJ - 1)), reads=[kw, "big2"], writes=[kp])
                em.op("act", lambda e, pa=pa, oc=oc: e.activation(out=yT[:, oc, :], in_=pa, func=AF.Identity), reads=[kp], writes=["yT"])
            postnorm_residual(b, 24)

    for l in range(n_layers):
        if do_layer(l):
            break

    for t in range(16):
        yo = xin[t % 2]
        for g4 in range(2):
            for c4 in range(4):
                c = g4 * 4 + c4
                em.op("pe", lambda e, c=c, c4=c4, t=t: e.transpose(pT[:, c4 * 128:(c4 + 1) * 128], xT[:, c, t * 128:(t + 1) * 128], idf), reads=["xT", "idf"], writes=["pT"])
            em.op("dve", lambda e, g4=g4, yo=yo: e.tensor_copy(out=yo[:, g4 * 512:(g4 + 1) * 512], in_=pT), reads=["pT"], writes=["yT"])
        em.dma("sp", lambda e, t=t, yo=yo: e.dma_start(out=y_d[t * 128:(t + 1) * 128, :], in_=yo), reads=["yT"], writes=["y_out"])

    em.finish()
    em.build()
    return nc


def _fm(v):
    return np.ascontiguousarray(np.asarray(v, np.float32).reshape(-1, 128).T)


def prepare(x_prompt, x_sample, cache_na_k, cache_na_v, cache_gqa_k, cache_gqa_v, c, c_ctx,
           ada_w, ada_b, norm_mix_pre, norm_mix_post, norm_ffn_pre, norm_ffn_post,
           na_w_qkv, na_w_o, na_rpb, gqa_w_qkv, gqa_w_o, gqa_q_norm, gqa_k_norm,
           ffn_w_up, ffn_conv_w, ffn_conv_b, ffn_w_down):
    f32 = np.float32
    bf = ml_dtypes.bfloat16
    A = lambda a: np.ascontiguousarray(np.asarray(a, f32))
    x_prompt = A(x_prompt); x_sample = A(x_sample)
    shared = {}
    shared["ada_w"] = A(ada_w)
    shared["ada_b"] = np.concatenate([_fm(np.asarray(ada_b)[l]) for l in range(DEPTH)], axis=1)
    gl = []
    for l in range(DEPTH):
        gl += [_fm(np.asarray(norm_mix_pre)[l]), _fm(np.asarray(norm_mix_post)[l]), _fm(np.asarray(norm_ffn_pre)[l]), _fm(np.asarray(norm_ffn_post)[l])]
    shared["gains"] = np.ascontiguousarray(np.concatenate(gl, axis=1))
    shared["wqkv_na"] = A(na_w_qkv); shared["wo_na"] = A(na_w_o); shared["wo_g"] = A(gqa_w_o)
    wg = np.asarray(gqa_w_qkv, f32)
    wq = wg[:, :, :1024]
    wk = wg[:, :, 1024:1280].reshape(2, 1024, 4, 1, 64)
    wv = wg[:, :, 1280:1536].reshape(2, 1024, 4, 1, 64)
    wkd = np.broadcast_to(wk, (2, 1024, 4, 2, 64)).reshape(2, 1024, 512)
    wvd = np.broadcast_to(wv, (2, 1024, 4, 2, 64)).reshape(2, 1024, 512)
    shared["wqkv_g"] = np.ascontiguousarray(np.concatenate([wq, wkd, wvd], axis=2))
    qn = np.asarray(gqa_q_norm, f32); kn = np.asarray(gqa_k_norm, f32)
    qkg = np.zeros((128, 4), f32)
    for i in range(2):
        qkg[:, 2 * i] = np.tile(qn[i], 2); qkg[:, 2 * i + 1] = np.tile(kn[i], 2)
    shared["qkg"] = qkg
    shared["w_up"] = A(ffn_w_up); shared["w_down"] = A(ffn_w_down)
    cw = np.asarray(ffn_conv_w, f32); cbias = np.asarray(ffn_conv_b, f32)
    convp = np.zeros((DEPTH, 128, 44, 4), f32)
    for l in range(DEPTH):
        for k in range(3):
            convp[l, :, :, k] = cw[l, k].reshape(44, 128).T
        convp[l, :, :, 3] = cbias[l].reshape(44, 128).T
    shared["convp"] = convp.reshape(DEPTH, 128, 176)
    shared["ident_f"] = np.eye(128, dtype=f32)
    blk = np.zeros((128, 128), f32); blk[:64, :64] = 1; blk[64:, 64:] = 1
    rot = np.zeros((128, 128), f32)
    for p in range(128):
        d = p % 32
        rot[p, p + 16 if d < 16 else p - 16] = 1.0
    shared["constb"] = np.ascontiguousarray(np.concatenate([np.eye(128, dtype=f32), np.ones((128, 128), f32), blk, rot], axis=1)).astype(bf)
    rpb = np.asarray(na_rpb, f32)
    p = np.arange(128); hi = (p >= 64).astype(int); kc = p % 64
    ep = np.arange(22); qc = np.arange(64)
    dr = 17 - ep[None, :] + hi[:, None]
    dc = np.clip(kc[:, None] - qc[None, :] + 15, 0, 30)
    cst = np.clip(qc - 8, 0, 48)
    colok = (kc[:, None] >= cst[None, :]) & (kc[:, None] < cst[None, :] + 16)
    drv_ok = (dr >= 0) & (dr <= 14)
    drc = np.clip(dr, 0, 14)
    tab = rpb[:, :, drc[:, :, None], dc[:, None, :]]
    tab = np.where(drv_ok[None, None, :, :, None], tab, 0.0)
    tab = np.where(colok[None, None, :, None, :], tab, NEG)
    bias_sample = np.ascontiguousarray(tab.reshape(2, 16, 128, 1408)).astype(bf)
    bias_prompt = np.zeros((2, 16, 128, 1408), bf)
    tok = np.arange(T)
    indA_p = np.zeros((2, 32, 2304), f32); indB_p = np.zeros((2, 32, 2048), f32)
    seq = tok // 256
    for j in range(8):
        indA_p[:, j, :T] = (seq == j)
        indA_p[:, j, T:] = 1.0
        indB_p[:, j, :] = np.where(seq == j, 0.0, NEG)
    indA_s = np.zeros((2, 32, 2304), f32); indB_s = np.zeros((2, 32, 2048), f32)
    row = tok // 64
    rs = np.clip(row - 4, 0, 24)
    for j in range(32):
        indA_s[0, j, :T] = (row == j)
        indB_s[0, j, :] = np.where((j >= rs) & (j < rs + 8), 0.0, NEG)
    d = np.arange(128) % 64
    fidx = d % 16
    freqs = (10000.0 ** (-(np.arange(16, dtype=f32)) / 16)).astype(f32)
    posr = (tok // 64).astype(f32); posc = (tok % 64).astype(f32)
    pos = np.where((d < 32)[:, None], posr[None, :], posc[None, :]).astype(f32)
    ang = (pos * freqs[fidx][:, None]).astype(f32)
    cos_s = np.cos(ang).astype(f32)
    sgn = np.where((d % 32) < 16, -1.0, 1.0).astype(f32)
    sin_s = (np.sin(ang).astype(f32) * sgn[:, None]).astype(f32)
    cos_p = np.ones((128, T), f32); sin_p = np.zeros((128, T), f32)
    cnk = np.asarray(cache_na_k, f32); cnv = np.asarray(cache_na_v, f32)
    cgk = np.asarray(cache_gqa_k, f32); cgv = np.asarray(cache_gqa_v, f32)

    def ctx_for(bb):
        out = {}
        k = cnk[bb].reshape(2, 256, 8, 128)
        out["ctxkT_na"] = np.ascontiguousarray(k.transpose(0, 3, 2, 1).reshape(2, 128, 2048))
        out["ctxv_na"] = np.ascontiguousarray(cnv[bb].reshape(2, 256, 1024))
        kg = cgk[bb]
        kgd = np.broadcast_to(kg[:, :, :, None, :], (2, 256, 4, 2, 64)).reshape(2, 256, 4, 128)
        out["ctxkT_g"] = np.ascontiguousarray(kgd.transpose(0, 3, 2, 1).reshape(2, 128, 1024))
        vg = cgv[bb]
        out["ctxv_g"] = np.ascontiguousarray(np.broadcast_to(vg[:, :, :, None, :], (2, 256, 4, 2, 64)).reshape(2, 256, 512))
        return out
    zero_ctx = {"ctxkT_na": np.zeros((2, 128, 2048), f32), "ctxv_na": np.zeros((2, 256, 1024), f32),
                "ctxkT_g": np.zeros((2, 128, 1024), f32), "ctxv_g": np.zeros((2, 256, 512), f32)}

    in_maps = []
    for core in range(NC8):
        m = dict(shared)
        role = core if core < 6 else core - 6
        if role < 4:
            m["x"] = np.ascontiguousarray(x_prompt[role * 8:(role + 1) * 8].reshape(T, D))
            m["cond"] = _fm(c_ctx)
            m["biasT"] = bias_prompt
            m["indA"] = indA_p.astype(bf); m["indB"] = indB_p.astype(bf)
            m["cosT"] = cos_p; m["sinT"] = sin_p
            m["cneg"] = np.full((128, 1), -1.0, f32)
            m.update(zero_ctx)
        else:
            bb = role - 4
            m["x"] = np.ascontiguousarray(x_sample[bb])
            m["cond"] = _fm(np.asarray(c)[bb])
            m["biasT"] = bias_sample
            m["indA"] = indA_s.astype(bf); m["indB"] = indB_s.astype(bf)
            m["cosT"] = cos_s; m["sinT"] = sin_s
            m["cneg"] = np.zeros((128, 1), f32)
            m.update(ctx_for(bb))
        in_maps.append(m)

    return in_maps


def assemble(R):
    f32 = np.float32
    y_prompt = np.concatenate([np.asarray(R[k]["y"], f32).reshape(8, 256, D) for k in range(4)], axis=0)
    y_sample = np.stack([np.asarray(R[4]["y"], f32), np.asarray(R[5]["y"], f32)], axis=0)

    def gather(name, feat):
        parts = [np.asarray(R[k][name], f32).reshape(2, 8, 256, feat).transpose(1, 0, 2, 3) for k in range(4)]
        return np.concatenate(parts, axis=0)
    na_k = gather("nk_na", 1024).reshape(32, 2, 256, 16, 64)
    na_v = gather("nv_na", 1024).reshape(32, 2, 256, 16, 64)
    g_k = gather("nk_g", 256).reshape(32, 2, 256, 4, 64)
    g_v = gather("nv_g", 256).reshape(32, 2, 256, 4, 64)
    return (y_prompt, y_sample, na_k, na_v, g_k, g_v)


def kernel(**inputs):
    in_maps = prepare(**inputs)
    nc = build_program()
    res = run_bass_kernel_spmd(nc, in_maps, core_ids=list(range(NC8)))
    return assemble(res.results)
```

```python
import numpy as np
import ml_dtypes
import concourse.bass as bass
import concourse.mybir as mybir
from concourse.bass_utils import run_bass_kernel_spmd

F32 = mybir.dt.float32
BF16 = mybir.dt.bfloat16
AF = mybir.ActivationFunctionType
ALU = mybir.AluOpType

D = 1024; NC8 = 8; T = 2048; NB = 4; BW = 512; DFF = 2816; NJ = 22; DEPTH = 4
EPS = 1e-6
NEG = -30000.0


class Em:
    def __init__(self, nc):
        self.nc = nc
        self.q = {e: [] for e in ("pe", "act", "dve", "sp", "pool")}
        self.cnt = {e: 0 for e in ("pe", "act", "dve")}
        self.sem = {e: nc.alloc_semaphore("s_" + e) for e in ("pe", "act", "dve")}
        self.P = 8
        self.dq = {}
        for qn in ("sp", "pool"):
            self.dq[qn] = {"sems": [nc.alloc_semaphore("d_%s%d" % (qn, i)) for i in range(self.P)], "k": 0}
        self.waited = {}
        self.lastw = {}
        self.readers = {}

    def _deps(self, reads, writes):
        d = []
        for r in reads:
            if r in self.lastw:
                d.append(self.lastw[r])
        for w in writes:
            if w in self.lastw:
                d.append(self.lastw[w])
            d.extend(self.readers.get(w, ()))
        return d

    def _waits(self, eng, deps):
        for (sem, val, name, src) in deps:
            if src == eng and eng == "pe":
                continue
            key = (eng, name)
            if self.waited.get(key, 0) >= val:
                continue
            self.waited[key] = val
            self.q[eng].append(lambda e, sem=sem, val=val: e.wait_ge(sem, val))

    def _record(self, tok, reads, writes):
        for w in writes:
            self.lastw[w] = tok
            self.readers[w] = []
        for r in reads:
            if r not in writes:
                self.readers.setdefault(r, []).append(tok)

    PSUM_KEYS = frozenset(['pA', 'pB', 'pS0', 'pS1', 'pO', 'pD', 'pN', 'pT'])

    def op(self, eng, fn, reads=(), writes=()):
        px = [r for r in reads if r in self.PSUM_KEYS and r not in writes]
        if px:
            writes = list(writes) + px
        self._waits(eng, self._deps(reads, writes))
        self.cnt[eng] += 1
        sem = self.sem[eng]
        self.q[eng].append(lambda e, fn=fn, sem=sem: fn(e).then_inc(sem, 1))
        self._record((sem, self.cnt[eng], "s_" + eng, eng), reads, writes)

    def dma(self, qn, fn, reads=(), writes=()):
        dq = self.dq[qn]
        k = dq["k"]; dq["k"] += 1
        slot = k % self.P
        sem = dq["sems"][slot]
        name = "d_%s%d" % (qn, slot)
        deps = self._deps(reads, writes)
        need = 16 * (k // self.P)
        if need > 0:
            deps.append((sem, need, name, "dma"))
        self._waits(qn, deps)
        self.q[qn].append(lambda e, fn=fn, sem=sem: fn(e).then_inc(sem, 16))
        self._record((sem, need + 16, name, "dma"), reads, writes)

    def finish(self):
        deps = []
        for qn in ("sp", "pool"):
            dq = self.dq[qn]
            for slot in range(self.P):
                n = (dq["k"] - slot + self.P - 1) // self.P if dq["k"] > slot else 0
                if n > 0:
                    deps.append((dq["sems"][slot], 16 * n, "d_%s%d" % (qn, slot), "dma"))
        for e in ("pe", "act", "dve"):
            if self.cnt[e]:
                deps.append((self.sem[e], self.cnt[e], "s_" + e, e))
        self._waits("sp", deps)

    def build(self):
        nc = self.nc
        with nc.Block() as block:
            @block.sync
            def _(e):
                for f in self.q["sp"]:
                    f(e)

            @block.gpsimd
            def _(e):
                for f in self.q["pool"]:
                    f(e)

            @block.tensor
            def _(e):
                for f in self.q["pe"]:
                    f(e)

            @block.scalar
            def _(e):
                for f in self.q["act"]:
                    f(e)

            @block.vector
            def _(e):
                for f in self.q["dve"]:
                    f(e)


class Rot:
    def __init__(self, items):
        self.items = items; self.i = 0

    def next(self):
        it = self.items[self.i % len(self.items)]; self.i += 1
        return it


def build_program(n_layers=DEPTH, stop_after=None):
    nc = bass.Bass("TRN2", target_bir_lowering=False)
    em = Em(nc)

    def din(name, shape, dt=F32):
        return nc.dram_tensor(name, list(shape), dt, kind="ExternalInput").ap()

    def dout(name, shape):
        return nc.dram_tensor(name, list(shape), F32, kind="ExternalOutput").ap()

    x_d = din("x", [T, D]); cond_d = din("cond", [128, 8])
    adaw_d = din("ada_w", [DEPTH, D, 6 * D]); adab_d = din("ada_b", [128, DEPTH * 48])
    gains_d = din("gains", [128, DEPTH * 32])
    wqkvn_d = din("wqkv_na", [2, D, 3072]); won_d = din("wo_na", [2, D, D])
    wqkvg_d = din("wqkv_g", [2, D, 2048]); wog_d = din("wo_g", [2, D, D])
    qkg_d = din("qkg", [128, 4])
    wup_d = din("w_up", [DEPTH, D, 2 * DFF]); wdn_d = din("w_down", [DEPTH, DFF, D])
    convp_d = din("convp", [DEPTH, 128, 44 * 4])
    bias_d = din("biasT", [2, 16, 128, 1408], BF16)
    indA_d = din("indA", [2, 32, 2304], BF16); indB_d = din("indB", [2, 32, 2048], BF16)
    ckn_d = din("ctxkT_na", [2, 128, 8 * 256]); cvn_d = din("ctxv_na", [2, 256, 1024])
    ckg_d = din("ctxkT_g", [2, 128, 4 * 256]); cvg_d = din("ctxv_g", [2, 256, 512])
    cos_d = din("cosT", [128, T]); sin_d = din("sinT", [128, T])
    cneg_d = din("cneg", [128, 1])
    idf_d = din("ident_f", [128, 128]); cb_d = din("constb", [128, 4 * 128], BF16)
    y_d = dout("y", [T, D])
    nkn_d = dout("nk_na", [2, T, D]); nvn_d = dout("nv_na", [2, T, D])
    nkg_d = dout("nk_g", [2, T, 256]); nvg_d = dout("nv_g", [2, T, 256])
    qT_d = nc.dram_tensor("qT_s", [8, 128, T], BF16, kind="ExternalOutput").ap()
    kT_d = nc.dram_tensor("kT_s", [8, 128, T], BF16, kind="ExternalOutput").ap()
    vS_d = nc.dram_tensor("vS_s", [T, D], BF16, kind="ExternalOutput").ap()
    oT_d = nc.dram_tensor("oT_s", [8, 128, T], BF16, kind="ExternalOutput").ap()
    h2_d = nc.dram_tensor("h2_s", [8, 128, T], BF16, kind="ExternalOutput").ap()

    def sb(name, shape, dt=F32):
        return nc.alloc_sbuf_tensor("sb_" + name, list(shape), dt).ap()

    xT = sb("xT", [128, 8, T])
    big2 = sb("big2", [128, NJ, BW], BF16)
    hT = big2[:, 0:8, :]
    blkin = sb("blkin", [128, 8, BW + 2], BF16)
    yT = sb("yT", [128, 8, BW])
    idf = sb("idf", [128, 128]); cb = sb("cb", [128, 4 * 128], BF16)
    idb = cb[:, 0:128]; onesb = cb[:, 128:256]; blkb = cb[:, 256:384]; rotb = cb[:, 384:512]
    cond_s = sb("cond_s", [128, 8]); silc = sb("silc", [128, 8], BF16)
    adab = sb("adab", [128, DEPTH * 48]); gains = sb("gains", [128, DEPTH * 32])
    qkg = sb("qkg", [128, 4]); cneg = sb("cneg", [128, 1])
    convp = sb("convp", [128, 44 * 4]); wcor = sb("wcor", [128, 44 * 2])
    mod = sb("mod", [128, 48]); drv = sb("drv", [128, 32])
    epsD = sb("epsD", [128, 1]); zerob = sb("zerob", [128, 8], BF16)
    kt2 = [sb("kt2_%d" % i, [128, 2304], BF16) for i in range(2)]; v_sb = sb("v_sb", [128, 18, 128], BF16)
    qe = [sb("qe%d" % i, [128, BW], BF16) for i in range(2)]
    qo = [sb("qo%d" % i, [128, BW], BF16) for i in range(2)]
    bias_sb = [sb("bias%d" % i, [128, 1408], BF16) for i in range(2)]
    p_rot = Rot([("p%d" % i, sb("p%d" % i, [128, BW], BF16)) for i in range(3)])
    wsm_rot = Rot([("wsm%d" % i, sb("wsm%d" % i, [128, 8, 128], BF16)) for i in range(4)])
    wbg_rot = Rot([("wbg%d" % i, sb("wbg%d" % i, [128, 8, 512], BF16)) for i in range(2)])
    f_rot = Rot([("f%d" % i, sb("f%d" % i, [128, BW])) for i in range(6)])
    b_rot = Rot([("b%d" % i, sb("b%d" % i, [128, BW], BF16)) for i in range(4)])
    u_rot = Rot([("u%d" % i, sb("u%d" % i, [128, BW + 2])) for i in range(2)])
    acc_rot = Rot([("acc%d" % i, sb("acc%d" % i, [128, BW])) for i in range(3)])
    cos_sb = sb("cos_sb", [128, BW]); sin_sb = sb("sin_sb", [128, BW])
    rstd = sb("rstd", [128, BW])
    ostg_rot = Rot([("ostg%d" % i, sb("ostg%d" % i, [128, BW], BF16)) for i in range(2)])
    nkst = sb("nkst", [128, 4, 64])

    def ps(name):
        return nc.alloc_psum_tensor("ps_" + name, [128, 512], F32).ap()
    pA_rot = Rot([("pA", ps("pA")), ("pB", ps("pB"))])
    pS_rot = Rot([("pS0", ps("pS0")), ("pS1", ps("pS1"))])
    pO = ps("pO"); pD = ps("pD"); pN = ps("pN"); pT = ps("pT")
    pF_rot = Rot(pA_rot.items + [("pO", pO), ("pD", pD)])
    pH_rot = Rot([("pT", pT)] + pS_rot.items)
    oacc_rot = Rot([("pO", pO, "pD", pD), ("pN", pN, "pT", pT)])

    def blkc(b):
        return slice(b * BW, (b + 1) * BW)

    def wpiece_small(src_ap):
        k, t = wsm_rot.next()
        em.dma("pool", lambda e: e.dma_start(out=t, in_=src_ap.rearrange("(kc p) n -> p kc n", p=128)), writes=[k])
        return k, t

    def wpiece_big(src_ap, nk=8, ncol=512):
        k, t = wbg_rot.next()
        if nk == 8 and ncol == 512:
            view = t
        else:
            view = nc_view(t, nk, ncol)
        em.dma("pool", lambda e: e.dma_start(out=view, in_=src_ap.rearrange("(kc p) n -> p kc n", p=128)), writes=[k])
        return k, view

    def nc_view(t, nk, ncol):
        flat = t.rearrange("p a b -> p (a b)")
        return flat[:, 0:nk * ncol].rearrange("p (a b) -> p a b", b=ncol)

    def rms_rstd(src_fn, src_keys, inv_n):
        for c in range(8):
            kq, sq = b_rot.next()
            em.op("act", lambda e, c=c, sq=sq: e.activation(out=sq, in_=src_fn(c), func=AF.Square), reads=src_keys, writes=[kq])
            em.op("pe", lambda e, c=c, sq=sq: e.matmul(pN, onesb, sq, start=(c == 0), stop=(c == 7)), reads=[kq, "cb"], writes=["pN"])
        kf, tf = f_rot.next()
        em.op("act", lambda e: e.activation(out=tf, in_=pN, func=AF.Ln, bias=epsD[:, 0:1], scale=inv_n), reads=["pN", "epsD"], writes=[kf])
        em.op("act", lambda e: e.activation(out=rstd, in_=tf, func=AF.Exp, scale=-0.5), reads=[kf], writes=["rstd"])

    def modulate(b, Acol, Bcol, dst_fn, dst_key):
        for c in range(8):
            kf, tf = f_rot.next()
            em.op("dve", lambda e, c=c, tf=tf: e.scalar_tensor_tensor(out=tf, in0=xT[:, c, blkc(b)], scalar=drv[:, Acol + c:Acol + c + 1], in1=rstd, op0=ALU.mult, op1=ALU.mult),
                  reads=["xT", "drv", "rstd"], writes=[kf])
            em.op("act", lambda e, c=c, tf=tf: e.activation(out=dst_fn(c), in_=tf, func=AF.Identity, bias=mod[:, Bcol + c:Bcol + c + 1], scale=1.0),
                  reads=[kf, "mod"], writes=[dst_key])

    def postnorm_residual(b, Gcol):
        rms_rstd(lambda c: yT[:, c, :], ["yT"], 1.0 / D)
        for c in range(8):
            kf, tf = f_rot.next()
            em.op("dve", lambda e, c=c, tf=tf: e.scalar_tensor_tensor(out=tf, in0=yT[:, c, :], scalar=drv[:, Gcol + c:Gcol + c + 1], in1=rstd, op0=ALU.mult, op1=ALU.mult),
                  reads=["yT", "drv", "rstd"], writes=[kf])
            em.op("dve", lambda e, c=c, tf=tf: e.tensor_tensor(out=xT[:, c, blkc(b)], in0=xT[:, c, blkc(b)], in1=tf, op=ALU.add),
                  reads=[kf, "xT"], writes=["xT"])

    for (dst, src, key) in [(idf, idf_d, "idf"), (cb, cb_d, "cb"), (cond_s, cond_d, "cond"), (adab, adab_d, "adab"),
                            (gains, gains_d, "gains"), (qkg, qkg_d, "qkg"), (cneg, cneg_d, "cneg")]:
        em.dma("sp", lambda e, dst=dst, src=src: e.dma_start(out=dst, in_=src), writes=[key])
    em.op("dve", lambda e: e.memset(epsD, EPS), writes=["epsD"])
    em.op("dve", lambda e: e.memset(zerob, 0.0), writes=["zerob"])
    for i in range(2):
        em.op("dve", lambda e, i=i: e.memset(qe[i], 0.0), writes=["qe%d" % i])
        em.op("dve", lambda e, i=i: e.memset(qo[i], 0.0), writes=["qo%d" % i])
    em.op("act", lambda e: e.activation(out=silc, in_=cond_s, func=AF.Silu), reads=["cond"], writes=["silc"])
    for i in range(2):
        em.op("dve", lambda e, i=i: e.memset(kt2[i], 0.0), writes=["kt2_%d" % i])
    yflat = yT.rearrange("p a b -> p (a b)")
    xin = [yflat[:, 0:1024], yflat[:, 1024:2048]]
    for t in range(16):
        xi = xin[t % 2]; kx = "xin%d" % (t % 2)
        em.dma("sp", lambda e, t=t, xi=xi: e.dma_start(out=xi, in_=x_d[t * 128:(t + 1) * 128, :]), writes=[kx, "yT"])
        for g4 in range(2):
            for c4 in range(4):
                c = g4 * 4 + c4
                em.op("pe", lambda e, c=c, c4=c4, xi=xi: e.transpose(pT[:, c4 * 128:(c4 + 1) * 128], xi[:, c * 128:(c + 1) * 128], idf), reads=[kx, "idf"], writes=["pT"])
            em.op("dve", lambda e, g4=g4, t=t: e.tensor_copy(out=xT[:, g4 * 4:(g4 + 1) * 4, t * 128:(t + 1) * 128], in_=pT.rearrange("p (a b) -> p a b", b=128)),
                  reads=["pT"], writes=["xT"])

    def do_layer(l):
        i = l // 2
        is_na = (l % 2 == 0)
        for pi in range(12):
            kw, wv = wpiece_big(adaw_d[l][:, pi * 512:(pi + 1) * 512])
            for oc4 in range(4):
                ch = pi * 4 + oc4
                for kc in range(8):
                    em.op("pe", lambda e, wv=wv, oc4=oc4, kc=kc, ch=ch: e.matmul(pT[:, ch:ch + 1], wv[:, kc, oc4 * 128:(oc4 + 1) * 128], silc[:, kc:kc + 1], start=(kc == 0), stop=(kc == 7)),
                          reads=[kw, "silc"], writes=["pT"])
        em.op("dve", lambda e, l=l: e.tensor_tensor(out=mod, in0=pT[:, 0:48], in1=adab[:, l * 48:(l + 1) * 48], op=ALU.add), reads=["pT", "adab"], writes=["mod"])
        g0 = l * 32
        em.op("dve", lambda e: e.scalar_tensor_tensor(out=drv[:, 0:8], in0=mod[:, 8:16], scalar=1.0, in1=gains[:, g0:g0 + 8], op0=ALU.add, op1=ALU.mult), reads=["mod", "gains"], writes=["drv"])
        em.op("dve", lambda e: e.tensor_tensor(out=drv[:, 8:16], in0=mod[:, 16:24], in1=gains[:, g0 + 8:g0 + 16], op=ALU.mult), reads=["mod", "gains"], writes=["drv"])
        em.op("dve", lambda e: e.scalar_tensor_tensor(out=drv[:, 16:24], in0=mod[:, 32:40], scalar=1.0, in1=gains[:, g0 + 16:g0 + 24], op0=ALU.add, op1=ALU.mult), reads=["mod", "gains"], writes=["drv"])
        em.op("dve", lambda e: e.tensor_tensor(out=drv[:, 24:32], in0=mod[:, 40:48], in1=gains[:, g0 + 24:g0 + 32], op=ALU.mult), reads=["mod", "gains"], writes=["drv"])
        em.dma("sp", lambda e, l=l: e.dma_start(out=convp, in_=convp_d[l]), writes=["convp"])
        ty = 0 if is_na else 1
        for p2 in range(2):
            em.dma("sp", lambda e, ty=ty, p2=p2: e.dma_start(out=kt2[p2][64:96, :], in_=indA_d[ty]), writes=["kt2_%d" % p2])
        cp3 = convp.rearrange("p (j f) -> p j f", f=4)
        wc3 = wcor.rearrange("p (j f) -> p j f", f=2)
        em.op("dve", lambda e: e.tensor_scalar(out=wc3[:, :, 0:1], in0=cp3[:, :, 0:1], scalar1=cneg[:, 0:1], scalar2=None, op0=ALU.mult), reads=["convp", "cneg"], writes=["wcor"])
        em.op("dve", lambda e: e.tensor_scalar(out=wc3[:, :, 1:2], in0=cp3[:, :, 2:3], scalar1=cneg[:, 0:1], scalar2=None, op0=ALU.mult), reads=["convp", "cneg"], writes=["wcor"])

        if stop_after == 'ada':
            return True
        wqkv = (wqkvn_d if is_na else wqkvg_d)[i]
        wo = (won_d if is_na else wog_d)[i]
        nqk = 16 if is_na else 12

        for b in range(NB):
            rms_rstd(lambda c, b=b: xT[:, c, blkc(b)], ["xT"], 1.0 / D)
            if stop_after == 'rms':
                break
            modulate(b, 0, 0, lambda c: hT[:, c, :], "big2")
            if stop_after == 'mod':
                break
            if not is_na:
                em.dma("sp", lambda e, b=b: e.dma_start(out=cos_sb, in_=cos_d[:, blkc(b)]), writes=["cos"])
                em.dma("sp", lambda e, b=b: e.dma_start(out=sin_sb, in_=sin_d[:, blkc(b)]), writes=["sin"])
            for oc in range(nqk):
                kw, wv = wpiece_small(wqkv[:, oc * 128:(oc + 1) * 128])
                kp, pa = pA_rot.next()
                for kc in range(8):
                    em.op("pe", lambda e, wv=wv, kc=kc, pa=pa: e.matmul(pa, wv[:, kc, :], hT[:, kc, :], start=(kc == 0), stop=(kc == 7)), reads=[kw, "big2"], writes=[kp])
                is_q = oc < 8
                dst = (qT_d if is_q else kT_d)[oc if is_q else oc - 8][:, blkc(b)]
                dkey = "qTd" if is_q else "kTd"
                kb, tb = b_rot.next()
                if is_na:
                    em.op("act", lambda e, pa=pa, tb=tb, is_q=is_q: e.activation(out=tb, in_=pa, func=AF.Identity, scale=(0.125 if is_q else 1.0)), reads=[kp], writes=[kb])
                else:
                    gcol = 0 if is_q else 1
                    kxf, xf = f_rot.next()
                    em.op("act", lambda e, pa=pa, xf=xf: e.activation(out=xf, in_=pa, func=AF.Identity), reads=[kp], writes=[kxf])
                    ksq, sq = b_rot.next()
                    em.op("act", lambda e, pa=pa, sq=sq: e.activation(out=sq, in_=pa, func=AF.Square), reads=[kp], writes=[ksq])
                    em.op("pe", lambda e, sq=sq: e.matmul(pN, blkb, sq, start=True, stop=True), reads=[ksq, "cb"], writes=["pN"])
                    kr, rr = f_rot.next()
                    em.op("act", lambda e, rr=rr: e.activation(out=rr, in_=pN, func=AF.Ln, bias=epsD[:, 0:1], scale=1.0 / 64.0), reads=["pN", "epsD"], writes=[kr])
                    em.op("act", lambda e, rr=rr: e.activation(out=rr, in_=rr, func=AF.Exp, scale=-0.5), reads=[kr], writes=[kr])
                    em.op("dve", lambda e, xf=xf, rr=rr, gcol=gcol: e.scalar_tensor_tensor(out=xf, in0=xf, scalar=qkg[:, 2 * i + gcol:2 * i + gcol + 1], in1=rr, op0=ALU.mult, op1=ALU.mult),
                          reads=[kxf, kr, "qkg"], writes=[kxf])
                    kxb, xb = b_rot.next()
                    em.op("act", lambda e, xf=xf, xb=xb: e.activation(out=xb, in_=xf, func=AF.Identity), reads=[kxf], writes=[kxb])
                    kp2, pr = pA_rot.next()
                    em.op("pe", lambda e, xb=xb, pr=pr: e.matmul(pr, rotb, xb, start=True, stop=True), reads=[kxb, "cb"], writes=[kp2])
                    em.op("dve", lambda e, rr=rr, pr=pr: e.tensor_tensor(out=rr, in0=pr, in1=sin_sb, op=ALU.mult), reads=[kp2, "sin"], writes=[kr])
                    em.op("dve", lambda e, xf=xf: e.tensor_tensor(out=xf, in0=xf, in1=cos_sb, op=ALU.mult), reads=[kxf, "cos"], writes=[kxf])
                    em.op("dve", lambda e, xf=xf, rr=rr: e.tensor_tensor(out=xf, in0=xf, in1=rr, op=ALU.add), reads=[kxf, kr], writes=[kxf])
                    em.op("act", lambda e, xf=xf, tb=tb, is_q=is_q: e.activation(out=tb, in_=xf, func=AF.Identity, scale=(0.125 if is_q else 1.0)), reads=[kxf], writes=[kb])
                    if not is_q:
                        g = oc - 8
                        for tt in range(4):
                            em.op("pe", lambda e, xf=xf, tt=tt: e.transpose(pT[:, tt * 128:(tt + 1) * 128], xf[:, tt * 128:(tt + 1) * 128], idf), reads=[kxf, "idf"], writes=["pT"])
                        em.op("dve", lambda e: e.tensor_copy(out=nkst, in_=pT.rearrange("p (a b) -> p a b", b=128)[:, :, 0:64]), reads=["pT"], writes=["nkst"])
                        em.dma("sp", lambda e, g=g, b=b: e.dma_start(out=nkg_d[i][b * BW:(b + 1) * BW, g * 64:(g + 1) * 64].rearrange("(a p) n -> p a n", p=128), in_=nkst),
                               reads=["nkst"], writes=["nkg_out"])
                em.dma("sp", lambda e, dst=dst, tb=tb: e.dma_start(out=dst, in_=tb), reads=[kb], writes=[dkey])
            if stop_after == 'qk':
                break
            if is_na:
                pieces = [(1024, nkn_d, 0, False), (1536, nkn_d, 512, False), (2048, nvn_d, 0, True), (2560, nvn_d, 512, True)]
            else:
                pieces = [(1536, None, 0, True)]
            for (col0, od, ocol, isv) in pieces:
                kw, wv = wpiece_big(wqkv[:, col0:col0 + 512])
                for tt in range(4):
                    kp, pa = pA_rot.next()
                    for kc in range(8):
                        em.op("pe", lambda e, wv=wv, kc=kc, pa=pa, tt=tt: e.matmul(pa, hT[:, kc, tt * 128:(tt + 1) * 128], wv[:, kc, :], start=(kc == 0), stop=(kc == 7)), reads=[kw, "big2"], writes=[kp])
                    r0 = b * BW + tt * 128
                    kf, tf = f_rot.next()
                    em.op("act", lambda e, pa=pa, tf=tf: e.activation(out=tf, in_=pa, func=AF.Identity), reads=[kp], writes=[kf])
                    if is_na:
                        em.dma("sp", lambda e, od=od, r0=r0, ocol=ocol, tf=tf: e.dma_start(out=od[i][r0:r0 + 128, ocol:ocol + 512], in_=tf), reads=[kf], writes=["nkv_out"])
                    else:
                        em.dma("sp", lambda e, r0=r0, tf=tf: e.dma_start(out=nvg_d[i][r0:r0 + 128, :].rearrange("p (g n) -> p g n", n=64), in_=tf.rearrange("p (g n) -> p g n", n=128)[:, :, 0:64]),
                               reads=[kf], writes=["nkv_out"])
                    if isv:
                        kb, tb = b_rot.next()
                        em.op("dve", lambda e, tf=tf, tb=tb: e.tensor_copy(out=tb, in_=tf), reads=[kf], writes=[kb])
                        em.dma("sp", lambda e, r0=r0, ocol=ocol, tb=tb: e.dma_start(out=vS_d[r0:r0 + 128, ocol:ocol + 512], in_=tb), reads=[kb], writes=["vSd"])
            if stop_after == 'kv0':
                break

        if stop_after in ('proj', 'rms', 'mod', 'qk', 'kv0'):
            return True
        for c in range(8):
            kvc = c if is_na else c // 2
            if is_na:
                for p2 in range(2):
                    em.dma("sp", lambda e, c=c, p2=p2: e.dma_start(out=kt2[p2][0:64, 0:T], in_=kT_d[c][p2 * 64:(p2 + 1) * 64, :]), reads=["kTd"], writes=["kt2_%d" % p2])
                    em.dma("pool", lambda e, c=c, p2=p2: e.dma_start(out=kt2[p2][0:64, T:T + 256], in_=ckn_d[i][p2 * 64:(p2 + 1) * 64, c * 256:(c + 1) * 256]), writes=["kt2_%d" % p2])
            else:
                em.dma("sp", lambda e, kvc=kvc: e.dma_start(out=kt2[0][0:64, 0:T], in_=kT_d[kvc][0:64, :]), reads=["kTd"], writes=["kt2_0"])
            if is_na:
                em.dma("pool", lambda e, c=c: e.dma_start(out=v_sb[:, 16:18, :], in_=cvn_d[i][:, c * 128:(c + 1) * 128].rearrange("(a p) n -> p a n", p=128)), writes=["v_sb"])
            else:
                em.dma("pool", lambda e, kvc=kvc: e.dma_start(out=kt2[0][0:64, T:T + 256], in_=ckg_d[i][0:64, kvc * 256:(kvc + 1) * 256]), writes=["kt2_0"])
                em.dma("pool", lambda e, kvc=kvc: e.dma_start(out=v_sb[:, 16:18, :], in_=cvg_d[i][:, kvc * 128:(kvc + 1) * 128].rearrange("(a p) n -> p a n", p=128)), writes=["v_sb"])
            em.dma("sp", lambda e, kvc=kvc: e.dma_start(out=v_sb[:, 0:16, :], in_=vS_d[:, kvc * 128:(kvc + 1) * 128].rearrange("(a p) n -> p a n", p=128)), reads=["vSd"], writes=["v_sb"])
            if is_na:
                for par in range(2):
                    em.dma("sp", lambda e, par=par, c=c: e.dma_start(out=bias_sb[par], in_=bias_d[i][2 * c + par]), writes=["bias%d" % par])
            for b in range(NB):
                qi = (c * NB + b) % 2
                em.dma("sp", lambda e, c=c, b=b, qi=qi: e.dma_start(out=qe[qi][0:64, :], in_=qT_d[c][0:64, blkc(b)]), reads=["qTd"], writes=["qe%d" % qi])
                em.dma("sp", lambda e, c=c, b=b, qi=qi: e.dma_start(out=qo[qi][0:64, :], in_=qT_d[c][64:128, blkc(b)]), reads=["qTd"], writes=["qo%d" % qi])
                em.dma("sp", lambda e, b=b, qi=qi: e.dma_start(out=qe[qi][64:96, :], in_=indB_d[ty][:, blkc(b)]), writes=["qe%d" % qi])
                em.dma("sp", lambda e, b=b, qi=qi: e.dma_start(out=qo[qi][64:96, :], in_=indB_d[ty][:, blkc(b)]), writes=["qo%d" % qi])
                if is_na:
                    tiles = [t for t in range(4 * b - 2, 4 * b + 6) if 0 <= t < 16] + [16, 17]
                else:
                    tiles = list(range(18))
                for par in range(2):
                    qm = (qe if par == 0 else qo)[qi]; kq = ("qe%d" if par == 0 else "qo%d") % qi

                    def issue_S(kt, qm=qm, kq=kq, par=par, b=b):
                        ksn, psn = pS_rot.next()
                        has_bias = is_na and kt < 16
                        ktile = kt2[par] if is_na else kt2[0]
                        kkey = ("kt2_%d" % par) if is_na else "kt2_0"
                        em.op("pe", lambda e, psn=psn, kt=kt, qm=qm, ktile=ktile, has_bias=has_bias: e.matmul(psn, ktile[:, kt * 128:(kt + 1) * 128], qm, start=True, stop=(not has_bias)), reads=[kkey, kq], writes=[ksn])
                        if has_bias:
                            e0 = 10 - 2 * (kt - 4 * b)
                            em.op("pe", lambda e, psn=psn, e0=e0, par=par: e.matmul(psn, idb, bias_sb[par][:, e0 * 64:e0 * 64 + 512], start=False, stop=True), reads=["cb", "bias%d" % par], writes=[ksn])
                        return ksn, psn
                    (kO, pOa, kDn, pDa) = oacc_rot.next()
                    nxt = issue_S(tiles[0])
                    for ti, kt in enumerate(tiles):
                        ksn, psn = nxt
                        if ti + 1 < len(tiles):
                            nxt = issue_S(tiles[ti + 1])
                        kpb, pb = p_rot.next()
                        em.op("act", lambda e, psn=psn, pb=pb: e.activation(out=pb, in_=psn, func=AF.Exp), reads=[ksn], writes=[kpb])
                        first = (ti == 0); last = (ti == len(tiles) - 1)
                        em.op("pe", lambda e, pb=pb, kt=kt, first=first, last=last, pOa=pOa: e.matmul(pOa, v_sb[:, kt, :], pb, start=first, stop=last), reads=[kpb, "v_sb"], writes=[kO])
                        em.op("pe", lambda e, pb=pb, first=first, last=last, pDa=pDa: e.matmul(pDa, onesb, pb, start=first, stop=last), reads=[kpb, "cb"], writes=[kDn])
                    hs = slice(par * 64, par * 64 + 64)
                    kf, tf = f_rot.next()
                    em.op("act", lambda e, tf=tf, hs=hs, pDa=pDa: e.activation(out=tf[hs, :], in_=pDa[hs, :], func=AF.Ln), reads=[kDn], writes=[kf])
                    em.op("act", lambda e, tf=tf, hs=hs: e.activation(out=tf[hs, :], in_=tf[hs, :], func=AF.Exp, scale=-1.0), reads=[kf], writes=[kf])
                    ko, to = ostg_rot.next()
                    em.op("dve", lambda e, tf=tf, to=to, hs=hs, pOa=pOa: e.tensor_tensor(out=to[hs, :], in0=pOa[hs, :], in1=tf[hs, :], op=ALU.mult), reads=[kO, kf], writes=[ko])
                    em.dma("sp", lambda e, to=to, hs=hs, c=c, b=b: e.dma_start(out=oT_d[c][hs, blkc(b)], in_=to[hs, :]), reads=[ko], writes=["oTd"])

        if stop_after == 'attn':
            return True
        for b in range(NB):
            em.dma("sp", lambda e, b=b: e.dma_start(out=blkin[:, :, 0:BW], in_=oT_d[:, :, blkc(b)].rearrange("c p n -> p c n")), reads=["oTd"], writes=["blkin"])
            for oc in range(8):
                kw, wv = wpiece_small(wo[:, oc * 128:(oc + 1) * 128])
                kp, pa = pA_rot.next()
                for kc in range(8):
                    em.op("pe", lambda e, wv=wv, kc=kc, pa=pa: e.matmul(pa, wv[:, kc, :], blkin[:, kc, 0:BW], start=(kc == 0), stop=(kc == 7)), reads=[kw, "blkin"], writes=[kp])
                em.op("act", lambda e, pa=pa, oc=oc: e.activation(out=yT[:, oc, :], in_=pa, func=AF.Identity), reads=[kp], writes=["yT"])
            postnorm_residual(b, 8)

        if stop_after == 'wo':
            return True
        for b in range(NB):
            rms_rstd(lambda c, b=b: xT[:, c, blkc(b)], ["xT"], 1.0 / D)
            modulate(b, 16, 24, lambda c: hT[:, c, :], "big2")
            em.dma("sp", lambda e, b=b: e.dma_start(out=h2_d[:, :, blkc(b)].rearrange("c p n -> p c n"), in_=hT), reads=["big2"], writes=["h2d"])
        for b in range(NB):
            lo = max(b * BW - 1, 0); hi = min((b + 1) * BW + 1, T)
            o0 = lo - (b * BW - 1)
            em.dma("sp", lambda e, lo=lo, hi=hi, o0=o0: e.dma_start(out=blkin[:, :, o0:o0 + hi - lo], in_=h2_d[:, :, lo:hi].rearrange("c p n -> p c n")), reads=["h2d"], writes=["blkin"])
            if b == 0:
                em.op("dve", lambda e: e.memset(blkin[:, :, 0:1], 0.0), writes=["blkin"])
            if b == NB - 1:
                em.op("dve", lambda e: e.memset(blkin[:, :, BW + 1:BW + 2], 0.0), writes=["blkin"])
            for j in range(NJ):
                accs = []
                for half in range(2):
                    jj = half * NJ + j
                    col0 = half * DFF + j * 128
                    kw, wv = wpiece_small(wup_d[l][:, col0:col0 + 128])
                    kp, pa = pF_rot.next()
                    kh, ph = pH_rot.next()
                    for kc in range(8):
                        em.op("pe", lambda e, wv=wv, kc=kc, pa=pa: e.matmul(pa, wv[:, kc, :], blkin[:, kc, 1:BW + 1], start=(kc == 0), stop=(kc == 7)), reads=[kw, "blkin"], writes=[kp])
                    for kc in range(8):
                        em.op("pe", lambda e, wv=wv, kc=kc, ph=ph: e.matmul(ph[:, 0:2], wv[:, kc, :], blkin[:, kc, 0:BW + 2:BW + 1], start=(kc == 0), stop=(kc == 7)), reads=[kw, "blkin"], writes=[kh])
                    ka, ac = acc_rot.next()
                    ku, us = u_rot.next()
                    em.op("act", lambda e, pa=pa, ac=ac, jj=jj: e.activation(out=ac, in_=pa, func=AF.Identity, bias=convp[:, jj * 4 + 3:jj * 4 + 4], scale=convp[:, jj * 4 + 1:jj * 4 + 2]), reads=[kp, "convp"], writes=[ka])
                    em.op("act", lambda e, pa=pa, us=us: e.activation(out=us[:, 1:BW + 1], in_=pa, func=AF.Identity), reads=[kp], writes=[ku])
                    em.op("dve", lambda e, us=us, ph=ph: e.tensor_copy(out=us[:, 0:BW + 2:BW + 1], in_=ph[:, 0:2]), reads=[kh], writes=[ku])
                    em.op("dve", lambda e, us=us, ac=ac, jj=jj: e.scalar_tensor_tensor(out=ac, in0=us[:, 0:BW], scalar=convp[:, jj * 4:jj * 4 + 1], in1=ac, op0=ALU.mult, op1=ALU.add), reads=[ku, ka, "convp"], writes=[ka])
                    em.op("dve", lambda e, us=us, ac=ac, jj=jj: e.scalar_tensor_tensor(out=ac, in0=us[:, 2:BW + 2], scalar=convp[:, jj * 4 + 2:jj * 4 + 3], in1=ac, op0=ALU.mult, op1=ALU.add), reads=[ku, ka, "convp"], writes=[ka])
                    em.op("dve", lambda e, us=us, ac=ac, jj=jj: e.scalar_tensor_tensor(out=ac[:, 0:BW:256], in0=us[:, 0:BW:256], scalar=wcor[:, jj * 2:jj * 2 + 1], in1=ac[:, 0:BW:256], op0=ALU.mult, op1=ALU.add), reads=[ku, ka, "wcor"], writes=[ka])
                    em.op("dve", lambda e, us=us, ac=ac, jj=jj: e.scalar_tensor_tensor(out=ac[:, 255:BW:256], in0=us[:, 257:BW + 2:256], scalar=wcor[:, jj * 2 + 1:jj * 2 + 2], in1=ac[:, 255:BW:256], op0=ALU.mult, op1=ALU.add), reads=[ku, ka, "wcor"], writes=[ka])
                    accs.append((ka, ac))
                (kaa, aa), (kag, ag) = accs
                em.op("act", lambda e, aa=aa: e.activation(out=aa, in_=aa, func=AF.Silu), reads=[kaa], writes=[kaa])
                em.op("dve", lambda e, aa=aa, ag=ag, j=j: e.tensor_tensor(out=big2[:, j, :], in0=aa, in1=ag, op=ALU.mult), reads=[kaa, kag], writes=["big2"])
            for oc in range(8):
                kw, wv = wpiece_big(wdn_d[l][:, oc * 128:(oc + 1) * 128], nk=NJ, ncol=128)
                kp, pa = pA_rot.next()
                for j in range(NJ):
                    em.op("pe", lambda e, wv=wv, j=j, pa=pa: e.matmul(pa, wv[:, j, :], big2[:, j, :], start=(j == 0), stop=(j == NJ - 1)), reads=[kw, "big2"], writes=[kp])
                em.op("act", lambda e, pa=pa, oc=oc: e.activation(out=yT[:, oc, :], in_=pa, func=AF.Identity), reads=[kp], writes=["yT"])
            postnorm_residual(b, 24)

    for l in range(n_layers):
        if do_layer(l):
            break

    for t in range(16):
        yo = xin[t % 2]
        for g4 in range(2):
            for c4 in range(4):
                c = g4 * 4 + c4
                em.op("pe", lambda e, c=c, c4=c4, t=t: e.transpose(pT[:, c4 * 128:(c4 + 1) * 128], xT[:, c, t * 128:(t + 1) * 128], idf), reads=["xT", "idf"], writes=["pT"])
            em.op("dve", lambda e, g4=g4, yo=yo: e.tensor_copy(out=yo[:, g4 * 512:(g4 + 1) * 512], in_=pT), reads=["pT"], writes=["yT"])
        em.dma("sp", lambda e, t=t, yo=yo: e.dma_start(out=y_d[t * 128:(t + 1) * 128, :], in_=yo), reads=["yT"], writes=["y_out"])

    em.finish()
    em.build()
    return nc


def _fm(v):
    return np.ascontiguousarray(np.asarray(v, np.float32).reshape(-1, 128).T)


def prepare(x_prompt, x_sample, cache_na_k, cache_na_v, cache_gqa_k, cache_gqa_v, c, c_ctx,
           ada_w, ada_b, norm_mix_pre, norm_mix_post, norm_ffn_pre, norm_ffn_post,
           na_w_qkv, na_w_o, na_rpb, gqa_w_qkv, gqa_w_o, gqa_q_norm, gqa_k_norm,
           ffn_w_up, ffn_conv_w, ffn_conv_b, ffn_w_down):
    f32 = np.float32
    bf = ml_dtypes.bfloat16
    A = lambda a: np.ascontiguousarray(np.asarray(a, f32))
    x_prompt = A(x_prompt); x_sample = A(x_sample)
    shared = {}
    shared["ada_w"] = A(ada_w)
    shared["ada_b"] = np.concatenate([_fm(np.asarray(ada_b)[l]) for l in range(DEPTH)], axis=1)
    gl = []
    for l in range(DEPTH):
        gl += [_fm(np.asarray(norm_mix_pre)[l]), _fm(np.asarray(norm_mix_post)[l]), _fm(np.asarray(norm_ffn_pre)[l]), _fm(np.asarray(norm_ffn_post)[l])]
    shared["gains"] = np.ascontiguousarray(np.concatenate(gl, axis=1))
    shared["wqkv_na"] = A(na_w_qkv); shared["wo_na"] = A(na_w_o); shared["wo_g"] = A(gqa_w_o)
    wg = np.asarray(gqa_w_qkv, f32)
    wq = wg[:, :, :1024]
    wk = wg[:, :, 1024:1280].reshape(2, 1024, 4, 1, 64)
    wv = wg[:, :, 1280:1536].reshape(2, 1024, 4, 1, 64)
    wkd = np.broadcast_to(wk, (2, 1024, 4, 2, 64)).reshape(2, 1024, 512)
    wvd = np.broadcast_to(wv, (2, 1024, 4, 2, 64)).reshape(2, 1024, 512)
    shared["wqkv_g"] = np.ascontiguousarray(np.concatenate([wq, wkd, wvd], axis=2))
    qn = np.asarray(gqa_q_norm, f32); kn = np.asarray(gqa_k_norm, f32)
    qkg = np.zeros((128, 4), f32)
    for i in range(2):
        qkg[:, 2 * i] = np.tile(qn[i], 2); qkg[:, 2 * i + 1] = np.tile(kn[i], 2)
    shared["qkg"] = qkg
    shared["w_up"] = A(ffn_w_up); shared["w_down"] = A(ffn_w_down)
    cw = np.asarray(ffn_conv_w, f32); cbias = np.asarray(ffn_conv_b, f32)
    convp = np.zeros((DEPTH, 128, 44, 4), f32)
    for l in range(DEPTH):
        for k in range(3):
            convp[l, :, :, k] = cw[l, k].reshape(44, 128).T
        convp[l, :, :, 3] = cbias[l].reshape(44, 128).T
    shared["convp"] = convp.reshape(DEPTH, 128, 176)
    shared["ident_f"] = np.eye(128, dtype=f32)
    blk = np.zeros((128, 128), f32); blk[:64, :64] = 1; blk[64:, 64:] = 1
    rot = np.zeros((128, 128), f32)
    for p in range(128):
        d = p % 32
        rot[p, p + 16 if d < 16 else p - 16] = 1.0
    shared["constb"] = np.ascontiguousarray(np.concatenate([np.eye(128, dtype=f32), np.ones((128, 128), f32), blk, rot], axis=1)).astype(bf)
    rpb = np.asarray(na_rpb, f32)
    p = np.arange(128); hi = (p >= 64).astype(int); kc = p % 64
    ep = np.arange(22); qc = np.arange(64)
    dr = 17 - ep[None, :] + hi[:, None]
    dc = np.clip(kc[:, None] - qc[None, :] + 15, 0, 30)
    cst = np.clip(qc - 8, 0, 48)
    colok = (kc[:, None] >= cst[None, :]) & (kc[:, None] < cst[None, :] + 16)
    drv_ok = (dr >= 0) & (dr <= 14)
    drc = np.clip(dr, 0, 14)
    tab = rpb[:, :, drc[:, :, None], dc[:, None, :]]
    tab = np.where(drv_ok[None, None, :, :, None], tab, 0.0)
    tab = np.where(colok[None, None, :, None, :], tab, NEG)
    bias_sample = np.ascontiguousarray(tab.reshape(2, 16, 128, 1408)).astype(bf)
    bias_prompt = np.zeros((2, 16, 128, 1408), bf)
    tok = np.arange(T)
    indA_p = np.zeros((2, 32, 2304), f32); indB_p = np.zeros((2, 32, 2048), f32)
    seq = tok // 256
    for j in range(8):
        indA_p[:, j, :T] = (seq == j)
        indA_p[:, j, T:] = 1.0
        indB_p[:, j, :] = np.where(seq == j, 0.0, NEG)
    indA_s = np.zeros((2, 32, 2304), f32); indB_s = np.zeros((2, 32, 2048), f32)
    row = tok // 64
    rs = np.clip(row - 4, 0, 24)
    for j in range(32):
        indA_s[0, j, :T] = (row == j)
        indB_s[0, j, :] = np.where((j >= rs) & (j < rs + 8), 0.0, NEG)
    d = np.arange(128) % 64
    fidx = d % 16
    freqs = (10000.0 ** (-(np.arange(16, dtype=f32)) / 16)).astype(f32)
    posr = (tok // 64).astype(f32); posc = (tok % 64).astype(f32)
    pos = np.where((d < 32)[:, None], posr[None, :], posc[None, :]).astype(f32)
    ang = (pos * freqs[fidx][:, None]).astype(f32)
    cos_s = np.cos(ang).astype(f32)
    sgn = np.where((d % 32) < 16, -1.0, 1.0).astype(f32)
    sin_s = (np.sin(ang).astype(f32) * sgn[:, None]).astype(f32)
    cos_p = np.ones((128, T), f32); sin_p = np.zeros((128, T), f32)
    cnk = np.asarray(cache_na_k, f32); cnv = np.asarray(cache_na_v, f32)
    cgk = np.asarray(cache_gqa_k, f32); cgv = np.asarray(cache_gqa_v, f32)

    def ctx_for(bb):
        out = {}
        k = cnk[bb].reshape(2, 256, 8, 128)
        out["ctxkT_na"] = np.ascontiguousarray(k.transpose(0, 3, 2, 1).reshape(2, 128, 2048))
        out["ctxv_na"] = np.ascontiguousarray(cnv[bb].reshape(2, 256, 1024))
        kg = cgk[bb]
        kgd = np.broadcast_to(kg[:, :, :, None, :], (2, 256, 4, 2, 64)).reshape(2, 256, 4, 128)
        out["ctxkT_g"] = np.ascontiguousarray(kgd.transpose(0, 3, 2, 1).reshape(2, 128, 1024))
        vg = cgv[bb]
        out["ctxv_g"] = np.ascontiguousarray(np.broadcast_to(vg[:, :, :, None, :], (2, 256, 4, 2, 64)).reshape(2, 256, 512))
        return out
    zero_ctx = {"ctxkT_na": np.zeros((2, 128, 2048), f32), "ctxv_na": np.zeros((2, 256, 1024), f32),
                "ctxkT_g": np.zeros((2, 128, 1024), f32), "ctxv_g": np.zeros((2, 256, 512), f32)}

    in_maps = []
    for core in range(NC8):
        m = dict(shared)
        role = core if core < 6 else core - 6
        if role < 4:
            m["x"] = np.ascontiguousarray(x_prompt[role * 8:(role + 1) * 8].reshape(T, D))
            m["cond"] = _fm(c_ctx)
            m["biasT"] = bias_prompt
            m["indA"] = indA_p.astype(bf); m["indB"] = indB_p.astype(bf)
            m["cosT"] = cos_p; m["sinT"] = sin_p
            m["cneg"] = np.full((128, 1), -1.0, f32)
            m.update(zero_ctx)
        else:
            bb = role - 4
            m["x"] = np.ascontiguousarray(x_sample[bb])
            m["cond"] = _fm(np.asarray(c)[bb])
            m["biasT"] = bias_sample
            m["indA"] = indA_s.astype(bf); m["indB"] = indB_s.astype(bf)
            m["cosT"] = cos_s; m["sinT"] = sin_s
            m["cneg"] = np.zeros((128, 1), f32)
            m.update(ctx_for(bb))
        in_maps.append(m)

    return in_maps


def assemble(R):
    f32 = np.float32
    y_prompt = np.concatenate([np.asarray(R[k]["y"], f32).reshape(8, 256, D) for k in range(4)], axis=0)
    y_sample = np.stack([np.asarray(R[4]["y"], f32), np.asarray(R[5]["y"], f32)], axis=0)

    def gather(name, feat):
        parts = [np.asarray(R[k][name], f32).reshape(2, 8, 256, feat).transpose(1, 0, 2, 3) for k in range(4)]
        return np.concatenate(parts, axis=0)
    na_k = gather("nk_na", 1024).reshape(32, 2, 256, 16, 64)
    na_v = gather("nv_na", 1024).reshape(32, 2, 256, 16, 64)
    g_k = gather("nk_g", 256).reshape(32, 2, 256, 4, 64)
    g_v = gather("nv_g", 256).reshape(32, 2, 256, 4, 64)
    return (y_prompt, y_sample, na_k, na_v, g_k, g_v)


def kernel(**inputs):
    in_maps = prepare(**inputs)
    nc = build_program()
    res = run_bass_kernel_spmd(nc, in_maps, core_ids=list(range(NC8)))
    return assemble(res.results)
```

```python
import numpy as np
import ml_dtypes
import concourse.bass as bass
import concourse.mybir as mybir
from concourse.bass_utils import run_bass_kernel_spmd

F32 = mybir.dt.float32
BF16 = mybir.dt.bfloat16
AF = mybir.ActivationFunctionType
ALU = mybir.AluOpType

D = 1024; NC8 = 8; T = 2048; NB = 4; BW = 512; DFF = 2816; NJ = 22; DEPTH = 4
EPS = 1e-6
NEG = -30000.0


class Em:
    def __init__(self, nc):
        self.nc = nc
        self.q = {e: [] for e in ("pe", "act", "dve", "sp", "pool")}
        self.cnt = {e: 0 for e in ("pe", "act", "dve")}
        self.sem = {e: nc.alloc_semaphore("s_" + e) for e in ("pe", "act", "dve")}
        self.P = 8
        self.dq = {}
        for qn in ("sp", "pool"):
            self.dq[qn] = {"sems": [nc.alloc_semaphore("d_%s%d" % (qn, i)) for i in range(self.P)], "k": 0}
        self.waited = {}
        self.lastw = {}
        self.readers = {}

    def _deps(self, reads, writes):
        d = []
        for r in reads:
            if r in self.lastw:
                d.append(self.lastw[r])
        for w in writes:
            if w in self.lastw:
                d.append(self.lastw[w])
            d.extend(self.readers.get(w, ()))
        return d

    def _waits(self, eng, deps):
        for (sem, val, name, src) in deps:
            if src == eng and eng == "pe":
                continue
            key = (eng, name)
            if self.waited.get(key, 0) >= val:
                continue
            self.waited[key] = val
            self.q[eng].append(lambda e, sem=sem, val=val: e.wait_ge(sem, val))

    def _record(self, tok, reads, writes):
        for w in writes:
            self.lastw[w] = tok
            self.readers[w] = []
        for r in reads:
            if r not in writes:
                self.readers.setdefault(r, []).append(tok)

    PSUM_KEYS = frozenset(['pA', 'pB', 'pS0', 'pS1', 'pO', 'pD', 'pN', 'pT'])

    def op(self, eng, fn, reads=(), writes=()):
        px = [r for r in reads if r in self.PSUM_KEYS and r not in writes]
        if px:
            writes = list(writes) + px
        self._waits(eng, self._deps(reads, writes))
        self.cnt[eng] += 1
        sem = self.sem[eng]
        self.q[eng].append(lambda e, fn=fn, sem=sem: fn(e).then_inc(sem, 1))
        self._record((sem, self.cnt[eng], "s_" + eng, eng), reads, writes)

    def dma(self, qn, fn, reads=(), writes=()):
        dq = self.dq[qn]
        k = dq["k"]; dq["k"] += 1
        slot = k % self.P
        sem = dq["sems"][slot]
        name = "d_%s%d" % (qn, slot)
        deps = self._deps(reads, writes)
        need = 16 * (k // self.P)
        if need > 0:
            deps.append((sem, need, name, "dma"))
        self._waits(qn, deps)
        self.q[qn].append(lambda e, fn=fn, sem=sem: fn(e).then_inc(sem, 16))
        self._record((sem, need + 16, name, "dma"), reads, writes)

    def finish(self):
        deps = []
        for qn in ("sp", "pool"):
            dq = self.dq[qn]
            for slot in range(self.P):
                n = (dq["k"] - slot + self.P - 1) // self.P if dq["k"] > slot else 0
                if n > 0:
                    deps.append((dq["sems"][slot], 16 * n, "d_%s%d" % (qn, slot), "dma"))
        for e in ("pe", "act", "dve"):
            if self.cnt[e]:
                deps.append((self.sem[e], self.cnt[e], "s_" + e, e))
        self._waits("sp", deps)

    def build(self):
        nc = self.nc
        with nc.Block() as block:
            @block.sync
            def _(e):
                for f in self.q["sp"]:
                    f(e)

            @block.gpsimd
            def _(e):
                for f in self.q["pool"]:
                    f(e)

            @block.tensor
            def _(e):
                for f in self.q["pe"]:
                    f(e)

            @block.scalar
            def _(e):
                for f in self.q["act"]:
                    f(e)

            @block.vector
            def _(e):
                for f in self.q["dve"]:
                    f(e)


class Rot:
    def __init__(self, items):
        self.items = items; self.i = 0

    def next(self):
        it = self.items[self.i % len(self.items)]; self.i += 1
        return it


def build_program(n_layers=DEPTH, stop_after=None):
    nc = bass.Bass("TRN2", target_bir_lowering=False)
    em = Em(nc)

    def din(name, shape, dt=F32):
        return nc.dram_tensor(name, list(shape), dt, kind="ExternalInput").ap()

    def dout(name, shape):
        return nc.dram_tensor(name, list(shape), F32, kind="ExternalOutput").ap()

    x_d = din("x", [T, D]); cond_d = din("cond", [128, 8])
    adaw_d = din("ada_w", [DEPTH, D, 6 * D]); adab_d = din("ada_b", [128, DEPTH * 48])
    gains_d = din("gains", [128, DEPTH * 32])
    wqkvn_d = din("wqkv_na", [2, D, 3072]); won_d = din("wo_na", [2, D, D])
    wqkvg_d = din("wqkv_g", [2, D, 2048]); wog_d = din("wo_g", [2, D, D])
    qkg_d = din("qkg", [128, 4])
    wup_d = din("w_up", [DEPTH, D, 2 * DFF]); wdn_d = din("w_down", [DEPTH, DFF, D])
    convp_d = din("convp", [DEPTH, 128, 44 * 4])
    bias_d = din("biasT", [2, 16, 128, 1408], BF16)
    indA_d = din("indA", [2, 32, 2304], BF16); indB_d = din("indB", [2, 32, 2048], BF16)
    ckn_d = din("ctxkT_na", [2, 128, 8 * 256]); cvn_d = din("ctxv_na", [2, 256, 1024])
    ckg_d = din("ctxkT_g", [2, 128, 4 * 256]); cvg_d = din("ctxv_g", [2, 256, 512])
    cos_d = din("cosT", [128, T]); sin_d = din("sinT", [128, T])
    cneg_d = din("cneg", [128, 1])
    idf_d = din("ident_f", [128, 128]); cb_d = din("constb", [128, 4 * 128], BF16)
    y_d = dout("y", [T, D])
    nkn_d = dout("nk_na", [2, T, D]); nvn_d = dout("nv_na", [2, T, D])
    nkg_d = dout("nk_g", [2, T, 256]); nvg_d = dout("nv_g", [2, T, 256])
    qT_d = nc.dram_tensor("qT_s", [8, 128, T], BF16, kind="ExternalOutput").ap()
    kT_d = nc.dram_tensor("kT_s", [8, 128, T], BF16, kind="ExternalOutput").ap()
    vS_d = nc.dram_tensor("vS_s", [T, D], BF16, kind="ExternalOutput").ap()
    oT_d = nc.dram_tensor("oT_s", [8, 128, T], BF16, kind="ExternalOutput").ap()
    h2_d = nc.dram_tensor("h2_s", [8, 128, T], BF16, kind="ExternalOutput").ap()

    def sb(name, shape, dt=F32):
        return nc.alloc_sbuf_tensor("sb_" + name, list(shape), dt).ap()

    xT = sb("xT", [128, 8, T])
    big2 = sb("big2", [128, NJ, BW], BF16)
    hT = big2[:, 0:8, :]
    blkin = sb("blkin", [128, 8, BW + 2], BF16)
    yT = sb("yT", [128, 8, BW])
    idf = sb("idf", [128, 128]); cb = sb("cb", [128, 4 * 128], BF16)
    idb = cb[:, 0:128]; onesb = cb[:, 128:256]; blkb = cb[:, 256:384]; rotb = cb[:, 384:512]
    cond_s = sb("cond_s", [128, 8]); silc = sb("silc", [128, 8], BF16)
    adab = sb("adab", [128, DEPTH * 48]); gains = sb("gains", [128, DEPTH * 32])
    qkg = sb("qkg", [128, 4]); cneg = sb("cneg", [128, 1])
    convp = sb("convp", [128, 44 * 4]); wcor = sb("wcor", [128, 44 * 2])
    mod = sb("mod", [128, 48]); drv = sb("drv", [128, 32])
    epsD = sb("epsD", [128, 1]); zerob = sb("zerob", [128, 8], BF16)
    kt2 = [sb("kt2_%d" % i, [128, 2304], BF16) for i in range(2)]; v_sb = sb("v_sb", [128, 18, 128], BF16)
    qe = [sb("qe%d" % i, [128, BW], BF16) for i in range(2)]
    qo = [sb("qo%d" % i, [128, BW], BF16) for i in range(2)]
    bias_sb = [sb("bias%d" % i, [128, 1408], BF16) for i in range(2)]
    p_rot = Rot([("p%d" % i, sb("p%d" % i, [128, BW], BF16)) for i in range(4)])
    wsm_rot = Rot([("wsm%d" % i, sb("wsm%d" % i, [128, 8, 128], BF16)) for i in range(4)])
    wbg_rot = Rot([("wbg%d" % i, sb("wbg%d" % i, [128, 8, 512], BF16)) for i in range(2)])
    f_rot = Rot([("f%d" % i, sb("f%d" % i, [128, BW])) for i in range(6)])
    b_rot = Rot([("b%d" % i, sb("b%d" % i, [128, BW], BF16)) for i in range(4)])
    u_rot = Rot([("u%d" % i, sb("u%d" % i, [128, BW + 2])) for i in range(2)])
    acc_rot = Rot([("acc%d" % i, sb("acc%d" % i, [128, BW])) for i in range(3)])
    cos_sb = sb("cos_sb", [128, BW]); sin_sb = sb("sin_sb", [128, BW])
    rstd = sb("rstd", [128, BW])
    ostg_rot = Rot([("ostg%d" % i, sb("ostg%d" % i, [128, BW], BF16)) for i in range(2)])
    nkst = sb("nkst", [128, 4, 64])

    def ps(name):
        return nc.alloc_psum_tensor("ps_" + name, [128, 512], F32).ap()
    pA_rot = Rot([("pA", ps("pA")), ("pB", ps("pB"))])
    pS_rot = Rot([("pS0", ps("pS0")), ("pS1", ps("pS1"))])
    pO = ps("pO"); pD = ps("pD"); pN = ps("pN"); pT = ps("pT")
    pF_rot = Rot(pA_rot.items + [("pO", pO), ("pD", pD)])
    pH_rot = Rot([("pT", pT)] + pS_rot.items)
    pS4_rot = Rot(pS_rot.items + pA_rot.items)
    oacc_rot = Rot([("pO", pO, "pD", pD), ("pN", pN, "pT", pT)])

    def blkc(b):
        return slice(b * BW, (b + 1) * BW)

    def wpiece_small(src_ap):
        k, t = wsm_rot.next()
        em.dma("pool", lambda e: e.dma_start(out=t, in_=src_ap.rearrange("(kc p) n -> p kc n", p=128)), writes=[k])
        return k, t

    def wpiece_big(src_ap, nk=8, ncol=512):
        k, t = wbg_rot.next()
        if nk == 8 and ncol == 512:
            view = t
        else:
            view = nc_view(t, nk, ncol)
        em.dma("pool", lambda e: e.dma_start(out=view, in_=src_ap.rearrange("(kc p) n -> p kc n", p=128)), writes=[k])
        return k, view

    def nc_view(t, nk, ncol):
        flat = t.rearrange("p a b -> p (a b)")
        return flat[:, 0:nk * ncol].rearrange("p (a b) -> p a b", b=ncol)

    def rms_rstd(src_fn, src_keys, inv_n):
        for c in range(8):
            kq, sq = b_rot.next()
            em.op("act", lambda e, c=c, sq=sq: e.activation(out=sq, in_=src_fn(c), func=AF.Square), reads=src_keys, writes=[kq])
            em.op("pe", lambda e, c=c, sq=sq: e.matmul(pN, onesb, sq, start=(c == 0), stop=(c == 7)), reads=[kq, "cb"], writes=["pN"])
        kf, tf = f_rot.next()
        em.op("act", lambda e: e.activation(out=tf, in_=pN, func=AF.Ln, bias=epsD[:, 0:1], scale=inv_n), reads=["pN", "epsD"], writes=[kf])
        em.op("act", lambda e: e.activation(out=rstd, in_=tf, func=AF.Exp, scale=-0.5), reads=[kf], writes=["rstd"])

    def modulate(b, Acol, Bcol, dst_fn, dst_key):
        for c in range(8):
            kf, tf = f_rot.next()
            em.op("dve", lambda e, c=c, tf=tf: e.scalar_tensor_tensor(out=tf, in0=xT[:, c, blkc(b)], scalar=drv[:, Acol + c:Acol + c + 1], in1=rstd, op0=ALU.mult, op1=ALU.mult),
                  reads=["xT", "drv", "rstd"], writes=[kf])
            em.op("act", lambda e, c=c, tf=tf: e.activation(out=dst_fn(c), in_=tf, func=AF.Identity, bias=mod[:, Bcol + c:Bcol + c + 1], scale=1.0),
                  reads=[kf, "mod"], writes=[dst_key])

    def postnorm_residual(b, Gcol):
        rms_rstd(lambda c: yT[:, c, :], ["yT"], 1.0 / D)
        for c in range(8):
            kf, tf = f_rot.next()
            em.op("dve", lambda e, c=c, tf=tf: e.scalar_tensor_tensor(out=tf, in0=yT[:, c, :], scalar=drv[:, Gcol + c:Gcol + c + 1], in1=rstd, op0=ALU.mult, op1=ALU.mult),
                  reads=["yT", "drv", "rstd"], writes=[kf])
            em.op("dve", lambda e, c=c, tf=tf: e.tensor_tensor(out=xT[:, c, blkc(b)], in0=xT[:, c, blkc(b)], in1=tf, op=ALU.add),
                  reads=[kf, "xT"], writes=["xT"])

    for (dst, src, key) in [(idf, idf_d, "idf"), (cb, cb_d, "cb"), (cond_s, cond_d, "cond"), (adab, adab_d, "adab"),
                            (gains, gains_d, "gains"), (qkg, qkg_d, "qkg"), (cneg, cneg_d, "cneg")]:
        em.dma("sp", lambda e, dst=dst, src=src: e.dma_start(out=dst, in_=src), writes=[key])
    em.op("dve", lambda e: e.memset(epsD, EPS), writes=["epsD"])
    em.op("dve", lambda e: e.memset(zerob, 0.0), writes=["zerob"])
    for i in range(2):
        em.op("dve", lambda e, i=i: e.memset(qe[i], 0.0), writes=["qe%d" % i])
        em.op("dve", lambda e, i=i: e.memset(qo[i], 0.0), writes=["qo%d" % i])
    em.op("act", lambda e: e.activation(out=silc, in_=cond_s, func=AF.Silu), reads=["cond"], writes=["silc"])
    for i in range(2):
        em.op("dve", lambda e, i=i: e.memset(kt2[i], 0.0), writes=["kt2_%d" % i])
    yflat = yT.rearrange("p a b -> p (a b)")
    xin = [yflat[:, 0:1024], yflat[:, 1024:2048]]
    for t in range(16):
        xi = xin[t % 2]; kx = "xin%d" % (t % 2)
        em.dma("sp", lambda e, t=t, xi=xi: e.dma_start(out=xi, in_=x_d[t * 128:(t + 1) * 128, :]), writes=[kx, "yT"])
        for g4 in range(2):
            for c4 in range(4):
                c = g4 * 4 + c4
                em.op("pe", lambda e, c=c, c4=c4, xi=xi: e.transpose(pT[:, c4 * 128:(c4 + 1) * 128], xi[:, c * 128:(c + 1) * 128], idf), reads=[kx, "idf"], writes=["pT"])
            em.op("dve", lambda e, g4=g4, t=t: e.tensor_copy(out=xT[:, g4 * 4:(g4 + 1) * 4, t * 128:(t + 1) * 128], in_=pT.rearrange("p (a b) -> p a b", b=128)),
                  reads=["pT"], writes=["xT"])

    def do_layer(l):
        i = l // 2
        is_na = (l % 2 == 0)
        for pi in range(12):
            kw, wv = wpiece_big(adaw_d[l][:, pi * 512:(pi + 1) * 512])
            for oc4 in range(4):
                ch = pi * 4 + oc4
                for kc in range(8):
                    em.op("pe", lambda e, wv=wv, oc4=oc4, kc=kc, ch=ch: e.matmul(pT[:, ch:ch + 1], wv[:, kc, oc4 * 128:(oc4 + 1) * 128], silc[:, kc:kc + 1], start=(kc == 0), stop=(kc == 7)),
                          reads=[kw, "silc"], writes=["pT"])
        em.op("dve", lambda e, l=l: e.tensor_tensor(out=mod, in0=pT[:, 0:48], in1=adab[:, l * 48:(l + 1) * 48], op=ALU.add), reads=["pT", "adab"], writes=["mod"])
        g0 = l * 32
        em.op("dve", lambda e: e.scalar_tensor_tensor(out=drv[:, 0:8], in0=mod[:, 8:16], scalar=1.0, in1=gains[:, g0:g0 + 8], op0=ALU.add, op1=ALU.mult), reads=["mod", "gains"], writes=["drv"])
        em.op("dve", lambda e: e.tensor_tensor(out=drv[:, 8:16], in0=mod[:, 16:24], in1=gains[:, g0 + 8:g0 + 16], op=ALU.mult), reads=["mod", "gains"], writes=["drv"])
        em.op("dve", lambda e: e.scalar_tensor_tensor(out=drv[:, 16:24], in0=mod[:, 32:40], scalar=1.0, in1=gains[:, g0 + 16:g0 + 24], op0=ALU.add, op1=ALU.mult), reads=["mod", "gains"], writes=["drv"])
        em.op("dve", lambda e: e.tensor_tensor(out=drv[:, 24:32], in0=mod[:, 40:48], in1=gains[:, g0 + 24:g0 + 32], op=ALU.mult), reads=["mod", "gains"], writes=["drv"])
        em.dma("sp", lambda e, l=l: e.dma_start(out=convp, in_=convp_d[l]), writes=["convp"])
        ty = 0 if is_na else 1
        for p2 in range(2):
            em.dma("sp", lambda e, ty=ty, p2=p2: e.dma_start(out=kt2[p2][64:96, :], in_=indA_d[ty]), writes=["kt2_%d" % p2])
        cp3 = convp.rearrange("p (j f) -> p j f", f=4)
        wc3 = wcor.rearrange("p (j f) -> p j f", f=2)
        em.op("dve", lambda e: e.tensor_scalar(out=wc3[:, :, 0:1], in0=cp3[:, :, 0:1], scalar1=cneg[:, 0:1], scalar2=None, op0=ALU.mult), reads=["convp", "cneg"], writes=["wcor"])
        em.op("dve", lambda e: e.tensor_scalar(out=wc3[:, :, 1:2], in0=cp3[:, :, 2:3], scalar1=cneg[:, 0:1], scalar2=None, op0=ALU.mult), reads=["convp", "cneg"], writes=["wcor"])

        if stop_after == 'ada':
            return True
        wqkv = (wqkvn_d if is_na else wqkvg_d)[i]
        wo = (won_d if is_na else wog_d)[i]
        nqk = 16 if is_na else 12

        for b in range(NB):
            rms_rstd(lambda c, b=b: xT[:, c, blkc(b)], ["xT"], 1.0 / D)
            if stop_after == 'rms':
                break
            modulate(b, 0, 0, lambda c: hT[:, c, :], "big2")
            if stop_after == 'mod':
                break
            if not is_na:
                em.dma("sp", lambda e, b=b: e.dma_start(out=cos_sb, in_=cos_d[:, blkc(b)]), writes=["cos"])
                em.dma("sp", lambda e, b=b: e.dma_start(out=sin_sb, in_=sin_d[:, blkc(b)]), writes=["sin"])
            for oc in range(nqk):
                kw, wv = wpiece_small(wqkv[:, oc * 128:(oc + 1) * 128])
                kp, pa = pA_rot.next()
                for kc in range(8):
                    em.op("pe", lambda e, wv=wv, kc=kc, pa=pa: e.matmul(pa, wv[:, kc, :], hT[:, kc, :], start=(kc == 0), stop=(kc == 7)), reads=[kw, "big2"], writes=[kp])
                is_q = oc < 8
                dst = (qT_d if is_q else kT_d)[oc if is_q else oc - 8][:, blkc(b)]
                dkey = "qTd" if is_q else "kTd"
                kb, tb = b_rot.next()
                if is_na:
                    em.op("act", lambda e, pa=pa, tb=tb, is_q=is_q: e.activation(out=tb, in_=pa, func=AF.Identity, scale=(0.125 if is_q else 1.0)), reads=[kp], writes=[kb])
                else:
                    gcol = 0 if is_q else 1
                    kxf, xf = f_rot.next()
                    em.op("act", lambda e, pa=pa, xf=xf: e.activation(out=xf, in_=pa, func=AF.Identity), reads=[kp], writes=[kxf])
                    ksq, sq = b_rot.next()
                    em.op("act", lambda e, pa=pa, sq=sq: e.activation(out=sq, in_=pa, func=AF.Square), reads=[kp], writes=[ksq])
                    em.op("pe", lambda e, sq=sq: e.matmul(pN, blkb, sq, start=True, stop=True), reads=[ksq, "cb"], writes=["pN"])
                    kr, rr = f_rot.next()
                    em.op("act", lambda e, rr=rr: e.activation(out=rr, in_=pN, func=AF.Ln, bias=epsD[:, 0:1], scale=1.0 / 64.0), reads=["pN", "epsD"], writes=[kr])
                    em.op("act", lambda e, rr=rr: e.activation(out=rr, in_=rr, func=AF.Exp, scale=-0.5), reads=[kr], writes=[kr])
                    em.op("dve", lambda e, xf=xf, rr=rr, gcol=gcol: e.scalar_tensor_tensor(out=xf, in0=xf, scalar=qkg[:, 2 * i + gcol:2 * i + gcol + 1], in1=rr, op0=ALU.mult, op1=ALU.mult),
                          reads=[kxf, kr, "qkg"], writes=[kxf])
                    kxb, xb = b_rot.next()
                    em.op("act", lambda e, xf=xf, xb=xb: e.activation(out=xb, in_=xf, func=AF.Identity), reads=[kxf], writes=[kxb])
                    kp2, pr = pA_rot.next()
                    em.op("pe", lambda e, xb=xb, pr=pr: e.matmul(pr, rotb, xb, start=True, stop=True), reads=[kxb, "cb"], writes=[kp2])
                    em.op("dve", lambda e, rr=rr, pr=pr: e.tensor_tensor(out=rr, in0=pr, in1=sin_sb, op=ALU.mult), reads=[kp2, "sin"], writes=[kr])
                    em.op("dve", lambda e, xf=xf: e.tensor_tensor(out=xf, in0=xf, in1=cos_sb, op=ALU.mult), reads=[kxf, "cos"], writes=[kxf])
                    em.op("dve", lambda e, xf=xf, rr=rr: e.tensor_tensor(out=xf, in0=xf, in1=rr, op=ALU.add), reads=[kxf, kr], writes=[kxf])
                    em.op("act", lambda e, xf=xf, tb=tb, is_q=is_q: e.activation(out=tb, in_=xf, func=AF.Identity, scale=(0.125 if is_q else 1.0)), reads=[kxf], writes=[kb])
                    if not is_q:
                        g = oc - 8
                        for tt in range(4):
                            em.op("pe", lambda e, xf=xf, tt=tt: e.transpose(pT[:, tt * 128:(tt + 1) * 128], xf[:, tt * 128:(tt + 1) * 128], idf), reads=[kxf, "idf"], writes=["pT"])
                        em.op("dve", lambda e: e.tensor_copy(out=nkst, in_=pT.rearrange("p (a b) -> p a b", b=128)[:, :, 0:64]), reads=["pT"], writes=["nkst"])
                        em.dma("sp", lambda e, g=g, b=b: e.dma_start(out=nkg_d[i][b * BW:(b + 1) * BW, g * 64:(g + 1) * 64].rearrange("(a p) n -> p a n", p=128), in_=nkst),
                               reads=["nkst"], writes=["nkg_out"])
                em.dma("sp", lambda e, dst=dst, tb=tb: e.dma_start(out=dst, in_=tb), reads=[kb], writes=[dkey])
            if stop_after == 'qk':
                break
            if is_na:
                pieces = [(1024, nkn_d, 0, False), (1536, nkn_d, 512, False), (2048, nvn_d, 0, True), (2560, nvn_d, 512, True)]
            else:
                pieces = [(1536, None, 0, True)]
            for (col0, od, ocol, isv) in pieces:
                kw, wv = wpiece_big(wqkv[:, col0:col0 + 512])
                for tt in range(4):
                    kp, pa = pA_rot.next()
                    for kc in range(8):
                        em.op("pe", lambda e, wv=wv, kc=kc, pa=pa, tt=tt: e.matmul(pa, hT[:, kc, tt * 128:(tt + 1) * 128], wv[:, kc, :], start=(kc == 0), stop=(kc == 7)), reads=[kw, "big2"], writes=[kp])
                    r0 = b * BW + tt * 128
                    kf, tf = f_rot.next()
                    em.op("act", lambda e, pa=pa, tf=tf: e.activation(out=tf, in_=pa, func=AF.Identity), reads=[kp], writes=[kf])
                    if is_na:
                        em.dma("sp", lambda e, od=od, r0=r0, ocol=ocol, tf=tf: e.dma_start(out=od[i][r0:r0 + 128, ocol:ocol + 512], in_=tf), reads=[kf], writes=["nkv_out"])
                    else:
                        em.dma("sp", lambda e, r0=r0, tf=tf: e.dma_start(out=nvg_d[i][r0:r0 + 128, :].rearrange("p (g n) -> p g n", n=64), in_=tf.rearrange("p (g n) -> p g n", n=128)[:, :, 0:64]),
                               reads=[kf], writes=["nkv_out"])
                    if isv:
                        kb, tb = b_rot.next()
                        em.op("dve", lambda e, tf=tf, tb=tb: e.tensor_copy(out=tb, in_=tf), reads=[kf], writes=[kb])
                        em.dma("sp", lambda e, r0=r0, ocol=ocol, tb=tb: e.dma_start(out=vS_d[r0:r0 + 128, ocol:ocol + 512], in_=tb), reads=[kb], writes=["vSd"])
            if stop_after == 'kv0':
                break

        if stop_after in ('proj', 'rms', 'mod', 'qk', 'kv0'):
            return True
        for c in range(8):
            kvc = c if is_na else c // 2
            if is_na:
                for p2 in range(2):
                    em.dma("sp", lambda e, c=c, p2=p2: e.dma_start(out=kt2[p2][0:64, 0:T], in_=kT_d[c][p2 * 64:(p2 + 1) * 64, :]), reads=["kTd"], writes=["kt2_%d" % p2])
                    em.dma("pool", lambda e, c=c, p2=p2: e.dma_start(out=kt2[p2][0:64, T:T + 256], in_=ckn_d[i][p2 * 64:(p2 + 1) * 64, c * 256:(c + 1) * 256]), writes=["kt2_%d" % p2])
            else:
                em.dma("sp", lambda e, kvc=kvc: e.dma_start(out=kt2[0][0:64, 0:T], in_=kT_d[kvc][0:64, :]), reads=["kTd"], writes=["kt2_0"])
            if is_na:
                em.dma("pool", lambda e, c=c: e.dma_start(out=v_sb[:, 16:18, :], in_=cvn_d[i][:, c * 128:(c + 1) * 128].rearrange("(a p) n -> p a n", p=128)), writes=["v_sb"])
            else:
                em.dma("pool", lambda e, kvc=kvc: e.dma_start(out=kt2[0][0:64, T:T + 256], in_=ckg_d[i][0:64, kvc * 256:(kvc + 1) * 256]), writes=["kt2_0"])
                em.dma("pool", lambda e, kvc=kvc: e.dma_start(out=v_sb[:, 16:18, :], in_=cvg_d[i][:, kvc * 128:(kvc + 1) * 128].rearrange("(a p) n -> p a n", p=128)), writes=["v_sb"])
            em.dma("sp", lambda e, kvc=kvc: e.dma_start(out=v_sb[:, 0:16, :], in_=vS_d[:, kvc * 128:(kvc + 1) * 128].rearrange("(a p) n -> p a n", p=128)), reads=["vSd"], writes=["v_sb"])
            if is_na:
                for par in range(2):
                    em.dma("sp", lambda e, par=par, c=c: e.dma_start(out=bias_sb[par], in_=bias_d[i][2 * c + par]), writes=["bias%d" % par])
            LOOK = 2
            items = []
            for b in range(NB):
                if is_na:
                    tiles = [t for t in range(4 * b - 2, 4 * b + 6) if 0 <= t < 16] + [16, 17]
                else:
                    tiles = list(range(18))
                for par in range(2):
                    for ti, kt in enumerate(tiles):
                        items.append((b, par, ti, kt, len(tiles)))
            qloaded = set()

            def ensure_q(b, c=c):
                if b in qloaded:
                    return
                qloaded.add(b)
                qi = (c * NB + b) % 2
                em.dma("sp", lambda e, c=c, b=b, qi=qi: e.dma_start(out=qe[qi][0:64, :], in_=qT_d[c][0:64, blkc(b)]), reads=["qTd"], writes=["qe%d" % qi])
                em.dma("sp", lambda e, c=c, b=b, qi=qi: e.dma_start(out=qo[qi][0:64, :], in_=qT_d[c][64:128, blkc(b)]), reads=["qTd"], writes=["qo%d" % qi])
                em.dma("sp", lambda e, b=b, qi=qi: e.dma_start(out=qe[qi][64:96, :], in_=indB_d[ty][:, blkc(b)]), writes=["qe%d" % qi])
                em.dma("sp", lambda e, b=b, qi=qi: e.dma_start(out=qo[qi][64:96, :], in_=indB_d[ty][:, blkc(b)]), writes=["qo%d" % qi])

            def issue_S(n, c=c):
                (b, par, ti, kt, nt) = items[n]
                ensure_q(b)
                qi = (c * NB + b) % 2
                qm = (qe if par == 0 else qo)[qi]; kq = ("qe%d" if par == 0 else "qo%d") % qi
                ksn, psn = pS4_rot.next()
                has_bias = is_na and kt < 16
                ktile = kt2[par] if is_na else kt2[0]
                kkey = ("kt2_%d" % par) if is_na else "kt2_0"
                em.op("pe", lambda e, psn=psn, kt=kt, qm=qm, ktile=ktile, has_bias=has_bias: e.matmul(psn, ktile[:, kt * 128:(kt + 1) * 128], qm, start=True, stop=(not has_bias)), reads=[kkey, kq], writes=[ksn])
                if has_bias:
                    e0 = 10 - 2 * (kt - 4 * b)
                    em.op("pe", lambda e, psn=psn, e0=e0, par=par: e.matmul(psn, idb, bias_sb[par][:, e0 * 64:e0 * 64 + 512], start=False, stop=True), reads=["cb", "bias%d" % par], writes=[ksn])
                return ksn, psn
            issued = [issue_S(n) for n in range(min(LOOK, len(items)))]
            cur_acc = None
            for n, (b, par, ti, kt, nt) in enumerate(items):
                if n + LOOK < len(items):
                    issued.append(issue_S(n + LOOK))
                ksn, psn = issued[n]
                if ti == 0:
                    cur_acc = oacc_rot.next()
                (kO, pOa, kDn, pDa) = cur_acc
                kpb, pb = p_rot.next()
                em.op("act", lambda e, psn=psn, pb=pb: e.activation(out=pb, in_=psn, func=AF.Exp), reads=[ksn], writes=[kpb])
                first = (ti == 0); last = (ti == nt - 1)
                em.op("pe", lambda e, pb=pb, kt=kt, first=first, last=last, pOa=pOa: e.matmul(pOa, v_sb[:, kt, :], pb, start=first, stop=last), reads=[kpb, "v_sb"], writes=[kO])
                em.op("pe", lambda e, pb=pb, first=first, last=last, pDa=pDa: e.matmul(pDa, onesb, pb, start=first, stop=last), reads=[kpb, "cb"], writes=[kDn])
                if last:
                    hs = slice(par * 64, par * 64 + 64)
                    kf, tf = f_rot.next()
                    em.op("act", lambda e, tf=tf, hs=hs, pDa=pDa: e.activation(out=tf[hs, :], in_=pDa[hs, :], func=AF.Ln), reads=[kDn], writes=[kf])
                    em.op("act", lambda e, tf=tf, hs=hs: e.activation(out=tf[hs, :], in_=tf[hs, :], func=AF.Exp, scale=-1.0), reads=[kf], writes=[kf])
                    ko, to = ostg_rot.next()
                    em.op("dve", lambda e, tf=tf, to=to, hs=hs, pOa=pOa: e.tensor_tensor(out=to[hs, :], in0=pOa[hs, :], in1=tf[hs, :], op=ALU.mult), reads=[kO, kf], writes=[ko])
                    em.dma("sp", lambda e, to=to, hs=hs, c=c, b=b: e.dma_start(out=oT_d[c][hs, blkc(b)], in_=to[hs, :]), reads=[ko], writes=["oTd"])

        if stop_after == 'attn':
            return True
        for b in range(NB):
            em.dma("sp", lambda e, b=b: e.dma_start(out=blkin[:, :, 0:BW], in_=oT_d[:, :, blkc(b)].rearrange("c p n -> p c n")), reads=["oTd"], writes=["blkin"])
            for oc in range(8):
                kw, wv = wpiece_small(wo[:, oc * 128:(oc + 1) * 128])
                kp, pa = pA_rot.next()
                for kc in range(8):
                    em.op("pe", lambda e, wv=wv, kc=kc, pa=pa: e.matmul(pa, wv[:, kc, :], blkin[:, kc, 0:BW], start=(kc == 0), stop=(kc == 7)), reads=[kw, "blkin"], writes=[kp])
                em.op("act", lambda e, pa=pa, oc=oc: e.activation(out=yT[:, oc, :], in_=pa, func=AF.Identity), reads=[kp], writes=["yT"])
            postnorm_residual(b, 8)

        if stop_after == 'wo':
            return True
        for b in range(NB):
            rms_rstd(lambda c, b=b: xT[:, c, blkc(b)], ["xT"], 1.0 / D)
            modulate(b, 16, 24, lambda c: hT[:, c, :], "big2")
            em.dma("sp", lambda e, b=b: e.dma_start(out=h2_d[:, :, blkc(b)].rearrange("c p n -> p c n"), in_=hT), reads=["big2"], writes=["h2d"])
        for b in range(NB):
            lo = max(b * BW - 1, 0); hi = min((b + 1) * BW + 1, T)
            o0 = lo - (b * BW - 1)
            em.dma("sp", lambda e, lo=lo, hi=hi, o0=o0: e.dma_start(out=blkin[:, :, o0:o0 + hi - lo], in_=h2_d[:, :, lo:hi].rearrange("c p n -> p c n")), reads=["h2d"], writes=["blkin"])
            if b == 0:
                em.op("dve", lambda e: e.memset(blkin[:, :, 0:1], 0.0), writes=["blkin"])
            if b == NB - 1:
                em.op("dve", lambda e: e.memset(blkin[:, :, BW + 1:BW + 2], 0.0), writes=["blkin"])
            for j in range(NJ):
                accs = []
                for half in range(2):
                    jj = half * NJ + j
                    col0 = half * DFF + j * 128
                    kw, wv = wpiece_small(wup_d[l][:, col0:col0 + 128])
                    kp, pa = pF_rot.next()
                    kh, ph = pH_rot.next()
                    for kc in range(8):
                        em.op("pe", lambda e, wv=wv, kc=kc, pa=pa: e.matmul(pa, wv[:, kc, :], blkin[:, kc, 1:BW + 1], start=(kc == 0), stop=(kc == 7)), reads=[kw, "blkin"], writes=[kp])
                    for kc in range(8):
                        em.op("pe", lambda e, wv=wv, kc=kc, ph=ph: e.matmul(ph[:, 0:2], wv[:, kc, :], blkin[:, kc, 0:BW + 2:BW + 1], start=(kc == 0), stop=(kc == 7)), reads=[kw, "blkin"], writes=[kh])
                    ka, ac = acc_rot.next()
                    ku, us = u_rot.next()
                    em.op("act", lambda e, pa=pa, ac=ac, jj=jj: e.activation(out=ac, in_=pa, func=AF.Identity, bias=convp[:, jj * 4 + 3:jj * 4 + 4], scale=convp[:, jj * 4 + 1:jj * 4 + 2]), reads=[kp, "convp"], writes=[ka])
                    em.op("act", lambda e, pa=pa, us=us: e.activation(out=us[:, 1:BW + 1], in_=pa, func=AF.Identity), reads=[kp], writes=[ku])
                    em.op("dve", lambda e, us=us, ph=ph: e.tensor_copy(out=us[:, 0:BW + 2:BW + 1], in_=ph[:, 0:2]), reads=[kh], writes=[ku])
                    em.op("dve", lambda e, us=us, ac=ac, jj=jj: e.scalar_tensor_tensor(out=ac, in0=us[:, 0:BW], scalar=convp[:, jj * 4:jj * 4 + 1], in1=ac, op0=ALU.mult, op1=ALU.add), reads=[ku, ka, "convp"], writes=[ka])
                    em.op("dve", lambda e, us=us, ac=ac, jj=jj: e.scalar_tensor_tensor(out=ac, in0=us[:, 2:BW + 2], scalar=convp[:, jj * 4 + 2:jj * 4 + 3], in1=ac, op0=ALU.mult, op1=ALU.add), reads=[ku, ka, "convp"], writes=[ka])
                    em.op("dve", lambda e, us=us, ac=ac, jj=jj: e.scalar_tensor_tensor(out=ac[:, 0:BW:256], in0=us[:, 0:BW:256], scalar=wcor[:, jj * 2:jj * 2 + 1], in1=ac[:, 0:BW:256], op0=ALU.mult, op1=ALU.add), reads=[ku, ka, "wcor"], writes=[ka])
                    em.op("dve", lambda e, us=us, ac=ac, jj=jj: e.scalar_tensor_tensor(out=ac[:, 255:BW:256], in0=us[:, 257:BW + 2:256], scalar=wcor[:, jj * 2 + 1:jj * 2 + 2], in1=ac[:, 255:BW:256], op0=ALU.mult, op1=ALU.add), reads=[ku, ka, "wcor"], writes=[ka])
                    accs.append((ka, ac))
                (kaa, aa), (kag, ag) = accs
                em.op("act", lambda e, aa=aa: e.activation(out=aa, in_=aa, func=AF.Silu), reads=[kaa], writes=[kaa])
                em.op("dve", lambda e, aa=aa, ag=ag, j=j: e.tensor_tensor(out=big2[:, j, :], in0=aa, in1=ag, op=ALU.mult), reads=[kaa, kag], writes=["big2"])
            for oc in range(8):
                kw, wv = wpiece_big(wdn_d[l][:, oc * 128:(oc + 1) * 128], nk=NJ, ncol=128)
                kp, pa = pA_rot.next()
                for j in range(NJ):
                    em.op("pe", lambda e, wv=wv, j=j, pa=pa: e.matmul(pa, wv[:, j, :], big2[:, j, :], start=(j == 0), stop=(j == NJ - 1)), reads=[kw, "big2"], writes=[kp])
                em.op("act", lambda e, pa=pa, oc=oc: e.activation(out=yT[:, oc, :], in_=pa, func=AF.Identity), reads=[kp], writes=["yT"])
            postnorm_residual(b, 24)

    for l in range(n_layers):
        if do_layer(l):
            break

    for t in range(16):
        yo = xin[t % 2]
        for g4 in range(2):
            for c4 in range(4):
                c = g4 * 4 + c4
                em.op("pe", lambda e, c=c, c4=c4, t=t: e.transpose(pT[:, c4 * 128:(c4 + 1) * 128], xT[:, c, t * 128:(t + 1) * 128], idf), reads=["xT", "idf"], writes=["pT"])
            em.op("dve", lambda e, g4=g4, yo=yo: e.tensor_copy(out=yo[:, g4 * 512:(g4 + 1) * 512], in_=pT), reads=["pT"], writes=["yT"])
        em.dma("sp", lambda e, t=t, yo=yo: e.dma_start(out=y_d[t * 128:(t + 1) * 128, :], in_=yo), reads=["yT"], writes=["y_out"])

    em.finish()
    em.build()
    return nc


def _fm(v):
    return np.ascontiguousarray(np.asarray(v, np.float32).reshape(-1, 128).T)


def prepare(x_prompt, x_sample, cache_na_k, cache_na_v, cache_gqa_k, cache_gqa_v, c, c_ctx,
           ada_w, ada_b, norm_mix_pre, norm_mix_post, norm_ffn_pre, norm_ffn_post,
           na_w_qkv, na_w_o, na_rpb, gqa_w_qkv, gqa_w_o, gqa_q_norm, gqa_k_norm,
           ffn_w_up, ffn_conv_w, ffn_conv_b, ffn_w_down):
    f32 = np.float32
    bf = ml_dtypes.bfloat16
    A = lambda a: np.ascontiguousarray(np.asarray(a, f32))
    x_prompt = A(x_prompt); x_sample = A(x_sample)
    shared = {}
    shared["ada_w"] = A(ada_w)
    shared["ada_b"] = np.concatenate([_fm(np.asarray(ada_b)[l]) for l in range(DEPTH)], axis=1)
    gl = []
    for l in range(DEPTH):
        gl += [_fm(np.asarray(norm_mix_pre)[l]), _fm(np.asarray(norm_mix_post)[l]), _fm(np.asarray(norm_ffn_pre)[l]), _fm(np.asarray(norm_ffn_post)[l])]
    shared["gains"] = np.ascontiguousarray(np.concatenate(gl, axis=1))
    shared["wqkv_na"] = A(na_w_qkv); shared["wo_na"] = A(na_w_o); shared["wo_g"] = A(gqa_w_o)
    wg = np.asarray(gqa_w_qkv, f32)
    wq = wg[:, :, :1024]
    wk = wg[:, :, 1024:1280].reshape(2, 1024, 4, 1, 64)
    wv = wg[:, :, 1280:1536].reshape(2, 1024, 4, 1, 64)
    wkd = np.broadcast_to(wk, (2, 1024, 4, 2, 64)).reshape(2, 1024, 512)
    wvd = np.broadcast_to(wv, (2, 1024, 4, 2, 64)).reshape(2, 1024, 512)
    shared["wqkv_g"] = np.ascontiguousarray(np.concatenate([wq, wkd, wvd], axis=2))
    qn = np.asarray(gqa_q_norm, f32); kn = np.asarray(gqa_k_norm, f32)
    qkg = np.zeros((128, 4), f32)
    for i in range(2):
        qkg[:, 2 * i] = np.tile(qn[i], 2); qkg[:, 2 * i + 1] = np.tile(kn[i], 2)
    shared["qkg"] = qkg
    shared["w_up"] = A(ffn_w_up); shared["w_down"] = A(ffn_w_down)
    cw = np.asarray(ffn_conv_w, f32); cbias = np.asarray(ffn_conv_b, f32)
    convp = np.zeros((DEPTH, 128, 44, 4), f32)
    for l in range(DEPTH):
        for k in range(3):
            convp[l, :, :, k] = cw[l, k].reshape(44, 128).T
        convp[l, :, :, 3] = cbias[l].reshape(44, 128).T
    shared["convp"] = convp.reshape(DEPTH, 128, 176)
    shared["ident_f"] = np.eye(128, dtype=f32)
    blk = np.zeros((128, 128), f32); blk[:64, :64] = 1; blk[64:, 64:] = 1
    rot = np.zeros((128, 128), f32)
    for p in range(128):
        d = p % 32
        rot[p, p + 16 if d < 16 else p - 16] = 1.0
    shared["constb"] = np.ascontiguousarray(np.concatenate([np.eye(128, dtype=f32), np.ones((128, 128), f32), blk, rot], axis=1)).astype(bf)
    rpb = np.asarray(na_rpb, f32)
    p = np.arange(128); hi = (p >= 64).astype(int); kc = p % 64
    ep = np.arange(22); qc = np.arange(64)
    dr = 17 - ep[None, :] + hi[:, None]
    dc = np.clip(kc[:, None] - qc[None, :] + 15, 0, 30)
    cst = np.clip(qc - 8, 0, 48)
    colok = (kc[:, None] >= cst[None, :]) & (kc[:, None] < cst[None, :] + 16)
    drv_ok = (dr >= 0) & (dr <= 14)
    drc = np.clip(dr, 0, 14)
    tab = rpb[:, :, drc[:, :, None], dc[:, None, :]]
    tab = np.where(drv_ok[None, None, :, :, None], tab, 0.0)
    tab = np.where(colok[None, None, :, None, :], tab, NEG)
    bias_sample = np.ascontiguousarray(tab.reshape(2, 16, 128, 1408)).astype(bf)
    bias_prompt = np.zeros((2, 16, 128, 1408), bf)
    tok = np.arange(T)
    indA_p = np.zeros((2, 32, 2304), f32); indB_p = np.zeros((2, 32, 2048), f32)
    seq = tok // 256
    for j in range(8):
        indA_p[:, j, :T] = (seq == j)
        indA_p[:, j, T:] = 1.0
        indB_p[:, j, :] = np.where(seq == j, 0.0, NEG)
    indA_s = np.zeros((2, 32, 2304), f32); indB_s = np.zeros((2, 32, 2048), f32)
    row = tok // 64
    rs = np.clip(row - 4, 0, 24)
    for j in range(32):
        indA_s[0, j, :T] = (row == j)
        indB_s[0, j, :] = np.where((j >= rs) & (j < rs + 8), 0.0, NEG)
    d = np.arange(128) % 64
    fidx = d % 16
    freqs = (10000.0 ** (-(np.arange(16, dtype=f32)) / 16)).astype(f32)
    posr = (tok // 64).astype(f32); posc = (tok % 64).astype(f32)
    pos = np.where((d < 32)[:, None], posr[None, :], posc[None, :]).astype(f32)
    ang = (pos * freqs[fidx][:, None]).astype(f32)
    cos_s = np.cos(ang).astype(f32)
    sgn = np.where((d % 32) < 16, -1.0, 1.0).astype(f32)
    sin_s = (np.sin(ang).astype(f32) * sgn[:, None]).astype(f32)
    cos_p = np.ones((128, T), f32); sin_p = np.zeros((128, T), f32)
    cnk = np.asarray(cache_na_k, f32); cnv = np.asarray(cache_na_v, f32)
    cgk = np.asarray(cache_gqa_k, f32); cgv = np.asarray(cache_gqa_v, f32)

    def ctx_for(bb):
        out = {}
        k = cnk[bb].reshape(2, 256, 8, 128)
        out["ctxkT_na"] = np.ascontiguousarray(k.transpose(0, 3, 2, 1).reshape(2, 128, 2048))
        out["ctxv_na"] = np.ascontiguousarray(cnv[bb].reshape(2, 256, 1024))
        kg = cgk[bb]
        kgd = np.broadcast_to(kg[:, :, :, None, :], (2, 256, 4, 2, 64)).reshape(2, 256, 4, 128)
        out["ctxkT_g"] = np.ascontiguousarray(kgd.transpose(0, 3, 2, 1).reshape(2, 128, 1024))
        vg = cgv[bb]
        out["ctxv_g"] = np.ascontiguousarray(np.broadcast_to(vg[:, :, :, None, :], (2, 256, 4, 2, 64)).reshape(2, 256, 512))
        return out
    zero_ctx = {"ctxkT_na": np.zeros((2, 128, 2048), f32), "ctxv_na": np.zeros((2, 256, 1024), f32),
                "ctxkT_g": np.zeros((2, 128, 1024), f32), "ctxv_g": np.zeros((2, 256, 512), f32)}

    in_maps = []
    for core in range(NC8):
        m = dict(shared)
        role = core if core < 6 else core - 6
        if role < 4:
            m["x"] = np.ascontiguousarray(x_prompt[role * 8:(role + 1) * 8].reshape(T, D))
            m["cond"] = _fm(c_ctx)
            m["biasT"] = bias_prompt
            m["indA"] = indA_p.astype(bf); m["indB"] = indB_p.astype(bf)
            m["cosT"] = cos_p; m["sinT"] = sin_p
            m["cneg"] = np.full((128, 1), -1.0, f32)
            m.update(zero_ctx)
        else:
            bb = role - 4
            m["x"] = np.ascontiguousarray(x_sample[bb])
            m["cond"] = _fm(np.asarray(c)[bb])
            m["biasT"] = bias_sample
            m["indA"] = indA_s.astype(bf); m["indB"] = indB_s.astype(bf)
            m["cosT"] = cos_s; m["sinT"] = sin_s
            m["cneg"] = np.zeros((128, 1), f32)
            m.update(ctx_for(bb))
        in_maps.append(m)

    return in_maps


def assemble(R):
    f32 = np.float32
    y_prompt = np.concatenate([np.asarray(R[k]["y"], f32).reshape(8, 256, D) for k in range(4)], axis=0)
    y_sample = np.stack([np.asarray(R[4]["y"], f32), np.asarray(R[5]["y"], f32)], axis=0)

    def gather(name, feat):
        parts = [np.asarray(R[k][name], f32).reshape(2, 8, 256, feat).transpose(1, 0, 2, 3) for k in range(4)]
        return np.concatenate(parts, axis=0)
    na_k = gather("nk_na", 1024).reshape(32, 2, 256, 16, 64)
    na_v = gather("nv_na", 1024).reshape(32, 2, 256, 16, 64)
    g_k = gather("nk_g", 256).reshape(32, 2, 256, 4, 64)
    g_v = gather("nv_g", 256).reshape(32, 2, 256, 4, 64)
    return (y_prompt, y_sample, na_k, na_v, g_k, g_v)


def kernel(**inputs):
    in_maps = prepare(**inputs)
    nc = build_program()
    res = run_bass_kernel_spmd(nc, in_maps, core_ids=list(range(NC8)))
    return assemble(res.results)
```

```python
import numpy as np
import ml_dtypes
import concourse.bass as bass
import concourse.mybir as mybir
from concourse.bass_utils import run_bass_kernel_spmd

F32 = mybir.dt.float32
BF16 = mybir.dt.bfloat16
AF = mybir.ActivationFunctionType
ALU = mybir.AluOpType

D = 1024; NC8 = 8; T = 2048; NB = 4; BW = 512; DFF = 2816; NJ = 22; DEPTH = 4
EPS = 1e-6
NEG = -30000.0


class Em:
    def __init__(self, nc):
        self.nc = nc
        self.q = {e: [] for e in ("pe", "act", "dve", "sp", "pool")}
        self.cnt = {e: 0 for e in ("pe", "act", "dve")}
        self.sem = {e: nc.alloc_semaphore("s_" + e) for e in ("pe", "act", "dve")}
        self.P = 8
        self.dq = {}
        for qn in ("sp", "pool"):
            self.dq[qn] = {"sems": [nc.alloc_semaphore("d_%s%d" % (qn, i)) for i in range(self.P)], "k": 0}
        self.waited = {}
        self.lastw = {}
        self.readers = {}

    def _deps(self, reads, writes):
        d = []
        for r in reads:
            if r in self.lastw:
                d.append(self.lastw[r])
        for w in writes:
            if w in self.lastw:
                d.append(self.lastw[w])
            d.extend(self.readers.get(w, ()))
        return d

    def _waits(self, eng, deps):
        for (sem, val, name, src) in deps:
            if src == eng and eng == "pe":
                continue
            key = (eng, name)
            if self.waited.get(key, 0) >= val:
                continue
            self.waited[key] = val
            self.q[eng].append(lambda e, sem=sem, val=val: e.wait_ge(sem, val))

    def _record(self, tok, reads, writes):
        for w in writes:
            self.lastw[w] = tok
            self.readers[w] = []
        for r in reads:
            if r not in writes:
                self.readers.setdefault(r, []).append(tok)

    PSUM_KEYS = frozenset(['pA', 'pB', 'pS0', 'pS1', 'pO', 'pD', 'pN', 'pT'])

    def op(self, eng, fn, reads=(), writes=()):
        px = [r for r in reads if r in self.PSUM_KEYS and r not in writes]
        if px:
            writes = list(writes) + px
        self._waits(eng, self._deps(reads, writes))
        self.cnt[eng] += 1
        sem = self.sem[eng]
        self.q[eng].append(lambda e, fn=fn, sem=sem: fn(e).then_inc(sem, 1))
        self._record((sem, self.cnt[eng], "s_" + eng, eng), reads, writes)

    def dma(self, qn, fn, reads=(), writes=()):
        dq = self.dq[qn]
        k = dq["k"]; dq["k"] += 1
        slot = k % self.P
        sem = dq["sems"][slot]
        name = "d_%s%d" % (qn, slot)
        deps = self._deps(reads, writes)
        need = 16 * (k // self.P)
        if need > 0:
            deps.append((sem, need, name, "dma"))
        self._waits(qn, deps)
        self.q[qn].append(lambda e, fn=fn, sem=sem: fn(e).then_inc(sem, 16))
        self._record((sem, need + 16, name, "dma"), reads, writes)

    def finish(self):
        deps = []
        for qn in ("sp", "pool"):
            dq = self.dq[qn]
            for slot in range(self.P):
                n = (dq["k"] - slot + self.P - 1) // self.P if dq["k"] > slot else 0
                if n > 0:
                    deps.append((dq["sems"][slot], 16 * n, "d_%s%d" % (qn, slot), "dma"))
        for e in ("pe", "act", "dve"):
            if self.cnt[e]:
                deps.append((self.sem[e], self.cnt[e], "s_" + e, e))
        self._waits("sp", deps)

    def build(self):
        nc = self.nc
        with nc.Block() as block:
            @block.sync
            def _(e):
                for f in self.q["sp"]:
                    f(e)

            @block.gpsimd
            def _(e):
                for f in self.q["pool"]:
                    f(e)

            @block.tensor
            def _(e):
                for f in self.q["pe"]:
                    f(e)

            @block.scalar
            def _(e):
                for f in self.q["act"]:
                    f(e)

            @block.vector
            def _(e):
                for f in self.q["dve"]:
                    f(e)


class Rot:
    def __init__(self, items):
        self.items = items; self.i = 0

    def next(self):
        it = self.items[self.i % len(self.items)]; self.i += 1
        return it


def build_program(n_layers=DEPTH, stop_after=None):
    nc = bass.Bass("TRN2", target_bir_lowering=False)
    em = Em(nc)

    def din(name, shape, dt=F32):
        return nc.dram_tensor(name, list(shape), dt, kind="ExternalInput").ap()

    def dout(name, shape):
        return nc.dram_tensor(name, list(shape), F32, kind="ExternalOutput").ap()

    x_d = din("x", [T, D]); cond_d = din("cond", [128, 8])
    adaw_d = din("ada_w", [DEPTH, D, 6 * D]); adab_d = din("ada_b", [128, DEPTH * 48])
    gains_d = din("gains", [128, DEPTH * 32])
    wqkvn_d = din("wqkv_na", [2, D, 3072]); won_d = din("wo_na", [2, D, D])
    wqkvg_d = din("wqkv_g", [2, D, 2048]); wog_d = din("wo_g", [2, D, D])
    qkg_d = din("qkg", [128, 4])
    wup_d = din("w_up", [DEPTH, D, 2 * DFF]); wdn_d = din("w_down", [DEPTH, DFF, D])
    convp_d = din("convp", [DEPTH, 128, 44 * 4])
    bias_d = din("biasT", [2, 16, 128, 1408], BF16)
    indA_d = din("indA", [2, 32, 2304], BF16); indB_d = din("indB", [2, 32, 2048], BF16)
    ckn_d = din("ctxkT_na", [2, 128, 8 * 256]); cvn_d = din("ctxv_na", [2, 256, 1024])
    ckg_d = din("ctxkT_g", [2, 128, 4 * 256]); cvg_d = din("ctxv_g", [2, 256, 512])
    cos_d = din("cosT", [128, T]); sin_d = din("sinT", [128, T])
    cneg_d = din("cneg", [128, 1])
    idf_d = din("ident_f", [128, 128]); cb_d = din("constb", [128, 4 * 128], BF16)
    y_d = dout("y", [T, D])
    nkn_d = dout("nk_na", [2, T, D]); nvn_d = dout("nv_na", [2, T, D])
    nkg_d = dout("nk_g", [2, T, 256]); nvg_d = dout("nv_g", [2, T, 256])
    qT_d = nc.dram_tensor("qT_s", [8, 128, T], BF16, kind="ExternalOutput").ap()
    kT_d = nc.dram_tensor("kT_s", [8, 128, T], BF16, kind="ExternalOutput").ap()
    vS_d = nc.dram_tensor("vS_s", [T, D], BF16, kind="ExternalOutput").ap()
    oT_d = nc.dram_tensor("oT_s", [8, 128, T], BF16, kind="ExternalOutput").ap()
    h2_d = nc.dram_tensor("h2_s", [8, 128, T], BF16, kind="ExternalOutput").ap()

    def sb(name, shape, dt=F32):
        return nc.alloc_sbuf_tensor("sb_" + name, list(shape), dt).ap()

    xT = sb("xT", [128, 8, T])
    big2 = sb("big2", [128, NJ, BW], BF16)
    hT = big2[:, 0:8, :]
    blkin = sb("blkin", [128, 8, BW + 2], BF16)
    yT = sb("yT", [128, 8, BW])
    idf = sb("idf", [128, 128]); cb = sb("cb", [128, 4 * 128], BF16)
    idb = cb[:, 0:128]; onesb = cb[:, 128:256]; blkb = cb[:, 256:384]; rotb = cb[:, 384:512]
    cond_s = sb("cond_s", [128, 8]); silc = sb("silc", [128, 8], BF16)
    adab = sb("adab", [128, DEPTH * 48]); gains = sb("gains", [128, DEPTH * 32])
    qkg = sb("qkg", [128, 4]); cneg = sb("cneg", [128, 1])
    convp = sb("convp", [128, 44 * 4]); wcor = sb("wcor", [128, 44 * 2])
    mod = sb("mod", [128, 48]); drv = sb("drv", [128, 32])
    epsD = sb("epsD", [128, 1]); zerob = sb("zerob", [128, 8], BF16)
    kt2 = [sb("kt2_%d" % i, [128, 2304], BF16) for i in range(2)]; v_sb = sb("v_sb", [128, 18, 128], BF16)
    qe = [sb("qe%d" % i, [128, BW], BF16) for i in range(2)]
    qo = [sb("qo%d" % i, [128, BW], BF16) for i in range(2)]
    bias_sb = [sb("bias%d" % i, [128, 1408], BF16) for i in range(2)]
    p_rot = Rot([("p%d" % i, sb("p%d" % i, [128, BW], BF16)) for i in range(5)])
    wsm_rot = Rot([("wsm%d" % i, sb("wsm%d" % i, [128, 8, 128], BF16)) for i in range(4)])
    wbg_rot = Rot([("wbg%d" % i, sb("wbg%d" % i, [128, 8, 512], BF16)) for i in range(2)])
    f_rot = Rot([("f%d" % i, sb("f%d" % i, [128, BW])) for i in range(6)])
    b_rot = Rot([("b%d" % i, sb("b%d" % i, [128, BW], BF16)) for i in range(6)])
    u_rot = Rot([("u%d" % i, sb("u%d" % i, [128, BW + 2])) for i in range(3)])
    acc_rot = Rot([("acc%d" % i, sb("acc%d" % i, [128, BW])) for i in range(4)])
    cos_sb = sb("cos_sb", [128, BW]); sin_sb = sb("sin_sb", [128, BW])
    rstd = sb("rstd", [128, BW])
    ostg_rot = Rot([("ostg%d" % i, sb("ostg%d" % i, [128, BW], BF16)) for i in range(2)])
    nkst = sb("nkst", [128, 4, 64])

    def ps(name):
        return nc.alloc_psum_tensor("ps_" + name, [128, 512], F32).ap()
    pA_rot = Rot([("pA", ps("pA")), ("pB", ps("pB"))])
    pS_rot = Rot([("pS0", ps("pS0")), ("pS1", ps("pS1"))])
    pO = ps("pO"); pD = ps("pD"); pN = ps("pN"); pT = ps("pT")
    pF_rot = Rot(pA_rot.items + [("pO", pO), ("pD", pD)])
    pH_rot = Rot([("pT", pT)] + pS_rot.items)
    pS4_rot = Rot(pS_rot.items + pA_rot.items)
    oacc_rot = Rot([("pO", pO, "pD", pD), ("pN", pN, "pT", pT)])

    def blkc(b):
        return slice(b * BW, (b + 1) * BW)

    def wpiece_small(src_ap):
        k, t = wsm_rot.next()
        em.dma("pool", lambda e: e.dma_start(out=t, in_=src_ap.rearrange("(kc p) n -> p kc n", p=128)), writes=[k])
        return k, t

    def wpiece_big(src_ap, nk=8, ncol=512):
        k, t = wbg_rot.next()
        if nk == 8 and ncol == 512:
            view = t
        else:
            view = nc_view(t, nk, ncol)
        em.dma("pool", lambda e: e.dma_start(out=view, in_=src_ap.rearrange("(kc p) n -> p kc n", p=128)), writes=[k])
        return k, view

    def nc_view(t, nk, ncol):
        flat = t.rearrange("p a b -> p (a b)")
        return flat[:, 0:nk * ncol].rearrange("p (a b) -> p a b", b=ncol)

    def rms_rstd(src_fn, src_keys, inv_n):
        for c in range(8):
            kq, sq = b_rot.next()
            em.op("act", lambda e, c=c, sq=sq: e.activation(out=sq, in_=src_fn(c), func=AF.Square), reads=src_keys, writes=[kq])
            em.op("pe", lambda e, c=c, sq=sq: e.matmul(pN, onesb, sq, start=(c == 0), stop=(c == 7)), reads=[kq, "cb"], writes=["pN"])
        kf, tf = f_rot.next()
        em.op("act", lambda e: e.activation(out=tf, in_=pN, func=AF.Ln, bias=epsD[:, 0:1], scale=inv_n), reads=["pN", "epsD"], writes=[kf])
        em.op("act", lambda e: e.activation(out=rstd, in_=tf, func=AF.Exp, scale=-0.5), reads=[kf], writes=["rstd"])

    def modulate(b, Acol, Bcol, dst_fn, dst_key):
        for c in range(8):
            kf, tf = f_rot.next()
            em.op("dve", lambda e, c=c, tf=tf: e.scalar_tensor_tensor(out=tf, in0=xT[:, c, blkc(b)], scalar=drv[:, Acol + c:Acol + c + 1], in1=rstd, op0=ALU.mult, op1=ALU.mult),
                  reads=["xT", "drv", "rstd"], writes=[kf])
            em.op("act", lambda e, c=c, tf=tf: e.activation(out=dst_fn(c), in_=tf, func=AF.Identity, bias=mod[:, Bcol + c:Bcol + c + 1], scale=1.0),
                  reads=[kf, "mod"], writes=[dst_key])

    def postnorm_residual(b, Gcol):
        rms_rstd(lambda c: yT[:, c, :], ["yT"], 1.0 / D)
        for c in range(8):
            kf, tf = f_rot.next()
            em.op("dve", lambda e, c=c, tf=tf: e.scalar_tensor_tensor(out=tf, in0=yT[:, c, :], scalar=drv[:, Gcol + c:Gcol + c + 1], in1=rstd, op0=ALU.mult, op1=ALU.mult),
                  reads=["yT", "drv", "rstd"], writes=[kf])
            em.op("dve", lambda e, c=c, tf=tf: e.tensor_tensor(out=xT[:, c, blkc(b)], in0=xT[:, c, blkc(b)], in1=tf, op=ALU.add),
                  reads=[kf, "xT"], writes=["xT"])

    for (dst, src, key) in [(idf, idf_d, "idf"), (cb, cb_d, "cb"), (cond_s, cond_d, "cond"), (adab, adab_d, "adab"),
                            (gains, gains_d, "gains"), (qkg, qkg_d, "qkg"), (cneg, cneg_d, "cneg")]:
        em.dma("sp", lambda e, dst=dst, src=src: e.dma_start(out=dst, in_=src), writes=[key])
    em.op("dve", lambda e: e.memset(epsD, EPS), writes=["epsD"])
    em.op("dve", lambda e: e.memset(zerob, 0.0), writes=["zerob"])
    for i in range(2):
        em.op("dve", lambda e, i=i: e.memset(qe[i], 0.0), writes=["qe%d" % i])
        em.op("dve", lambda e, i=i: e.memset(qo[i], 0.0), writes=["qo%d" % i])
    em.op("act", lambda e: e.activation(out=silc, in_=cond_s, func=AF.Silu), reads=["cond"], writes=["silc"])
    for i in range(2):
        em.op("dve", lambda e, i=i: e.memset(kt2[i], 0.0), writes=["kt2_%d" % i])
    yflat = yT.rearrange("p a b -> p (a b)")
    xin = [yflat[:, 0:1024], yflat[:, 1024:2048]]
    for t in range(16):
        xi = xin[t % 2]; kx = "xin%d" % (t % 2)
        em.dma("sp", lambda e, t=t, xi=xi: e.dma_start(out=xi, in_=x_d[t * 128:(t + 1) * 128, :]), writes=[kx, "yT"])
        for g4 in range(2):
            for c4 in range(4):
                c = g4 * 4 + c4
                em.op("pe", lambda e, c=c, c4=c4, xi=xi: e.transpose(pT[:, c4 * 128:(c4 + 1) * 128], xi[:, c * 128:(c + 1) * 128], idf), reads=[kx, "idf"], writes=["pT"])
            em.op("dve", lambda e, g4=g4, t=t: e.tensor_copy(out=xT[:, g4 * 4:(g4 + 1) * 4, t * 128:(t + 1) * 128], in_=pT.rearrange("p (a b) -> p a b", b=128)),
                  reads=["pT"], writes=["xT"])

    def do_layer(l):
        i = l // 2
        is_na = (l % 2 == 0)
        for pi in range(12):
            kw, wv = wpiece_big(adaw_d[l][:, pi * 512:(pi + 1) * 512])
            for oc4 in range(4):
                ch = pi * 4 + oc4
                for kc in range(8):
                    em.op("pe", lambda e, wv=wv, oc4=oc4, kc=kc, ch=ch: e.matmul(pT[:, ch:ch + 1], wv[:, kc, oc4 * 128:(oc4 + 1) * 128], silc[:, kc:kc + 1], start=(kc == 0), stop=(kc == 7)),
                          reads=[kw, "silc"], writes=["pT"])
        em.op("dve", lambda e, l=l: e.tensor_tensor(out=mod, in0=pT[:, 0:48], in1=adab[:, l * 48:(l + 1) * 48], op=ALU.add), reads=["pT", "adab"], writes=["mod"])
        g0 = l * 32
        em.op("dve", lambda e: e.scalar_tensor_tensor(out=drv[:, 0:8], in0=mod[:, 8:16], scalar=1.0, in1=gains[:, g0:g0 + 8], op0=ALU.add, op1=ALU.mult), reads=["mod", "gains"], writes=["drv"])
        em.op("dve", lambda e: e.tensor_tensor(out=drv[:, 8:16], in0=mod[:, 16:24], in1=gains[:, g0 + 8:g0 + 16], op=ALU.mult), reads=["mod", "gains"], writes=["drv"])
        em.op("dve", lambda e: e.scalar_tensor_tensor(out=drv[:, 16:24], in0=mod[:, 32:40], scalar=1.0, in1=gains[:, g0 + 16:g0 + 24], op0=ALU.add, op1=ALU.mult), reads=["mod", "gains"], writes=["drv"])
        em.op("dve", lambda e: e.tensor_tensor(out=drv[:, 24:32], in0=mod[:, 40:48], in1=gains[:, g0 + 24:g0 + 32], op=ALU.mult), reads=["mod", "gains"], writes=["drv"])
        em.dma("sp", lambda e, l=l: e.dma_start(out=convp, in_=convp_d[l]), writes=["convp"])
        ty = 0 if is_na else 1
        for p2 in range(2):
            em.dma("sp", lambda e, ty=ty, p2=p2: e.dma_start(out=kt2[p2][64:96, :], in_=indA_d[ty]), writes=["kt2_%d" % p2])
        cp3 = convp.rearrange("p (j f) -> p j f", f=4)
        wc3 = wcor.rearrange("p (j f) -> p j f", f=2)
        em.op("dve", lambda e: e.tensor_scalar(out=wc3[:, :, 0:1], in0=cp3[:, :, 0:1], scalar1=cneg[:, 0:1], scalar2=None, op0=ALU.mult), reads=["convp", "cneg"], writes=["wcor"])
        em.op("dve", lambda e: e.tensor_scalar(out=wc3[:, :, 1:2], in0=cp3[:, :, 2:3], scalar1=cneg[:, 0:1], scalar2=None, op0=ALU.mult), reads=["convp", "cneg"], writes=["wcor"])

        if stop_after == 'ada':
            return True
        wqkv = (wqkvn_d if is_na else wqkvg_d)[i]
        wo = (won_d if is_na else wog_d)[i]
        nqk = 16 if is_na else 12

        for b in range(NB):
            rms_rstd(lambda c, b=b: xT[:, c, blkc(b)], ["xT"], 1.0 / D)
            if stop_after == 'rms':
                break
            modulate(b, 0, 0, lambda c: hT[:, c, :], "big2")
            if stop_after == 'mod':
                break
            if not is_na:
                em.dma("sp", lambda e, b=b: e.dma_start(out=cos_sb, in_=cos_d[:, blkc(b)]), writes=["cos"])
                em.dma("sp", lambda e, b=b: e.dma_start(out=sin_sb, in_=sin_d[:, blkc(b)]), writes=["sin"])
            for oc in range(nqk):
                kw, wv = wpiece_small(wqkv[:, oc * 128:(oc + 1) * 128])
                kp, pa = pA_rot.next()
                for kc in range(8):
                    em.op("pe", lambda e, wv=wv, kc=kc, pa=pa: e.matmul(pa, wv[:, kc, :], hT[:, kc, :], start=(kc == 0), stop=(kc == 7)), reads=[kw, "big2"], writes=[kp])
                is_q = oc < 8
                dst = (qT_d if is_q else kT_d)[oc if is_q else oc - 8][:, blkc(b)]
                dkey = "qTd" if is_q else "kTd"
                kb, tb = b_rot.next()
                if is_na:
                    em.op("act", lambda e, pa=pa, tb=tb, is_q=is_q: e.activation(out=tb, in_=pa, func=AF.Identity, scale=(0.125 if is_q else 1.0)), reads=[kp], writes=[kb])
                else:
                    gcol = 0 if is_q else 1
                    kxf, xf = f_rot.next()
                    em.op("act", lambda e, pa=pa, xf=xf: e.activation(out=xf, in_=pa, func=AF.Identity), reads=[kp], writes=[kxf])
                    ksq, sq = b_rot.next()
                    em.op("act", lambda e, pa=pa, sq=sq: e.activation(out=sq, in_=pa, func=AF.Square), reads=[kp], writes=[ksq])
                    em.op("pe", lambda e, sq=sq: e.matmul(pN, blkb, sq, start=True, stop=True), reads=[ksq, "cb"], writes=["pN"])
                    kr, rr = f_rot.next()
                    em.op("act", lambda e, rr=rr: e.activation(out=rr, in_=pN, func=AF.Ln, bias=epsD[:, 0:1], scale=1.0 / 64.0), reads=["pN", "epsD"], writes=[kr])
                    em.op("act", lambda e, rr=rr: e.activation(out=rr, in_=rr, func=AF.Exp, scale=-0.5), reads=[kr], writes=[kr])
                    em.op("dve", lambda e, xf=xf, rr=rr, gcol=gcol: e.scalar_tensor_tensor(out=xf, in0=xf, scalar=qkg[:, 2 * i + gcol:2 * i + gcol + 1], in1=rr, op0=ALU.mult, op1=ALU.mult),
                          reads=[kxf, kr, "qkg"], writes=[kxf])
                    kxb, xb = b_rot.next()
                    em.op("act", lambda e, xf=xf, xb=xb: e.activation(out=xb, in_=xf, func=AF.Identity), reads=[kxf], writes=[kxb])
                    kp2, pr = pA_rot.next()
                    em.op("pe", lambda e, xb=xb, pr=pr: e.matmul(pr, rotb, xb, start=True, stop=True), reads=[kxb, "cb"], writes=[kp2])
                    em.op("dve", lambda e, rr=rr, pr=pr: e.tensor_tensor(out=rr, in0=pr, in1=sin_sb, op=ALU.mult), reads=[kp2, "sin"], writes=[kr])
                    em.op("dve", lambda e, xf=xf: e.tensor_tensor(out=xf, in0=xf, in1=cos_sb, op=ALU.mult), reads=[kxf, "cos"], writes=[kxf])
                    em.op("dve", lambda e, xf=xf, rr=rr: e.tensor_tensor(out=xf, in0=xf, in1=rr, op=ALU.add), reads=[kxf, kr], writes=[kxf])
                    em.op("act", lambda e, xf=xf, tb=tb, is_q=is_q: e.activation(out=tb, in_=xf, func=AF.Identity, scale=(0.125 if is_q else 1.0)), reads=[kxf], writes=[kb])
                    if not is_q:
                        g = oc - 8
                        for tt in range(4):
                            em.op("pe", lambda e, xf=xf, tt=tt: e.transpose(pT[:, tt * 128:(tt + 1) * 128], xf[:, tt * 128:(tt + 1) * 128], idf), reads=[kxf, "idf"], writes=["pT"])
                        em.op("dve", lambda e: e.tensor_copy(out=nkst, in_=pT.rearrange("p (a b) -> p a b", b=128)[:, :, 0:64]), reads=["pT"], writes=["nkst"])
                        em.dma("sp", lambda e, g=g, b=b: e.dma_start(out=nkg_d[i][b * BW:(b + 1) * BW, g * 64:(g + 1) * 64].rearrange("(a p) n -> p a n", p=128), in_=nkst),
                               reads=["nkst"], writes=["nkg_out"])
                em.dma("sp", lambda e, dst=dst, tb=tb: e.dma_start(out=dst, in_=tb), reads=[kb], writes=[dkey])
            if stop_after == 'qk':
                break
            if is_na:
                pieces = [(1024, nkn_d, 0, False), (1536, nkn_d, 512, False), (2048, nvn_d, 0, True), (2560, nvn_d, 512, True)]
            else:
                pieces = [(1536, None, 0, True)]
            for (col0, od, ocol, isv) in pieces:
                kw, wv = wpiece_big(wqkv[:, col0:col0 + 512])
                for tt in range(4):
                    kp, pa = pA_rot.next()
                    for kc in range(8):
                        em.op("pe", lambda e, wv=wv, kc=kc, pa=pa, tt=tt: e.matmul(pa, hT[:, kc, tt * 128:(tt + 1) * 128], wv[:, kc, :], start=(kc == 0), stop=(kc == 7)), reads=[kw, "big2"], writes=[kp])
                    r0 = b * BW + tt * 128
                    kf, tf = f_rot.next()
                    em.op("act", lambda e, pa=pa, tf=tf: e.activation(out=tf, in_=pa, func=AF.Identity), reads=[kp], writes=[kf])
                    if is_na:
                        em.dma("sp", lambda e, od=od, r0=r0, ocol=ocol, tf=tf: e.dma_start(out=od[i][r0:r0 + 128, ocol:ocol + 512], in_=tf), reads=[kf], writes=["nkv_out"])
                    else:
                        em.dma("sp", lambda e, r0=r0, tf=tf: e.dma_start(out=nvg_d[i][r0:r0 + 128, :].rearrange("p (g n) -> p g n", n=64), in_=tf.rearrange("p (g n) -> p g n", n=128)[:, :, 0:64]),
                               reads=[kf], writes=["nkv_out"])
                    if isv:
                        kb, tb = b_rot.next()
                        em.op("dve", lambda e, tf=tf, tb=tb: e.tensor_copy(out=tb, in_=tf), reads=[kf], writes=[kb])
                        em.dma("sp", lambda e, r0=r0, ocol=ocol, tb=tb: e.dma_start(out=vS_d[r0:r0 + 128, ocol:ocol + 512], in_=tb), reads=[kb], writes=["vSd"])
            if stop_after == 'kv0':
                break

        if stop_after in ('proj', 'rms', 'mod', 'qk', 'kv0'):
            return True
        for c in range(8):
            kvc = c if is_na else c // 2
            if is_na:
                for p2 in range(2):
                    em.dma("sp", lambda e, c=c, p2=p2: e.dma_start(out=kt2[p2][0:64, 0:T], in_=kT_d[c][p2 * 64:(p2 + 1) * 64, :]), reads=["kTd"], writes=["kt2_%d" % p2])
                    em.dma("pool", lambda e, c=c, p2=p2: e.dma_start(out=kt2[p2][0:64, T:T + 256], in_=ckn_d[i][p2 * 64:(p2 + 1) * 64, c * 256:(c + 1) * 256]), writes=["kt2_%d" % p2])
            else:
                em.dma("sp", lambda e, kvc=kvc: e.dma_start(out=kt2[0][0:64, 0:T], in_=kT_d[kvc][0:64, :]), reads=["kTd"], writes=["kt2_0"])
            if is_na:
                em.dma("pool", lambda e, c=c: e.dma_start(out=v_sb[:, 16:18, :], in_=cvn_d[i][:, c * 128:(c + 1) * 128].rearrange("(a p) n -> p a n", p=128)), writes=["v_sb"])
            else:
                em.dma("pool", lambda e, kvc=kvc: e.dma_start(out=kt2[0][0:64, T:T + 256], in_=ckg_d[i][0:64, kvc * 256:(kvc + 1) * 256]), writes=["kt2_0"])
                em.dma("pool", lambda e, kvc=kvc: e.dma_start(out=v_sb[:, 16:18, :], in_=cvg_d[i][:, kvc * 128:(kvc + 1) * 128].rearrange("(a p) n -> p a n", p=128)), writes=["v_sb"])
            em.dma("sp", lambda e, kvc=kvc: e.dma_start(out=v_sb[:, 0:16, :], in_=vS_d[:, kvc * 128:(kvc + 1) * 128].rearrange("(a p) n -> p a n", p=128)), reads=["vSd"], writes=["v_sb"])
            if is_na:
                for par in range(2):
                    em.dma("sp", lambda e, par=par, c=c: e.dma_start(out=bias_sb[par], in_=bias_d[i][2 * c + par]), writes=["bias%d" % par])
            LOOK = 3
            items = []
            for b in range(NB):
                if is_na:
                    tiles = [t for t in range(4 * b - 2, 4 * b + 6) if 0 <= t < 16] + [16, 17]
                else:
                    tiles = list(range(18))
                for par in range(2):
                    for ti, kt in enumerate(tiles):
                        items.append((b, par, ti, kt, len(tiles)))
            qloaded = set()

            def ensure_q(b, c=c):
                if b in qloaded:
                    return
                qloaded.add(b)
                qi = (c * NB + b) % 2
                em.dma("sp", lambda e, c=c, b=b, qi=qi: e.dma_start(out=qe[qi][0:64, :], in_=qT_d[c][0:64, blkc(b)]), reads=["qTd"], writes=["qe%d" % qi])
                em.dma("sp", lambda e, c=c, b=b, qi=qi: e.dma_start(out=qo[qi][0:64, :], in_=qT_d[c][64:128, blkc(b)]), reads=["qTd"], writes=["qo%d" % qi])
                em.dma("sp", lambda e, b=b, qi=qi: e.dma_start(out=qe[qi][64:96, :], in_=indB_d[ty][:, blkc(b)]), writes=["qe%d" % qi])
                em.dma("sp", lambda e, b=b, qi=qi: e.dma_start(out=qo[qi][64:96, :], in_=indB_d[ty][:, blkc(b)]), writes=["qo%d" % qi])

            def issue_S(n, c=c):
                (b, par, ti, kt, nt) = items[n]
                ensure_q(b)
                qi = (c * NB + b) % 2
                qm = (qe if par == 0 else qo)[qi]; kq = ("qe%d" if par == 0 else "qo%d") % qi
                ksn, psn = pS4_rot.next()
                has_bias = is_na and kt < 16
                ktile = kt2[par] if is_na else kt2[0]
                kkey = ("kt2_%d" % par) if is_na else "kt2_0"
                em.op("pe", lambda e, psn=psn, kt=kt, qm=qm, ktile=ktile, has_bias=has_bias: e.matmul(psn, ktile[:, kt * 128:(kt + 1) * 128], qm, start=True, stop=(not has_bias)), reads=[kkey, kq], writes=[ksn])
                if has_bias:
                    e0 = 10 - 2 * (kt - 4 * b)
                    em.op("pe", lambda e, psn=psn, e0=e0, par=par: e.matmul(psn, idb, bias_sb[par][:, e0 * 64:e0 * 64 + 512], start=False, stop=True), reads=["cb", "bias%d" % par], writes=[ksn])
                return ksn, psn
            issued = [issue_S(n) for n in range(min(LOOK, len(items)))]
            cur_acc = None
            for n, (b, par, ti, kt, nt) in enumerate(items):
                if n + LOOK < len(items):
                    issued.append(issue_S(n + LOOK))
                ksn, psn = issued[n]
                if ti == 0:
                    cur_acc = oacc_rot.next()
                (kO, pOa, kDn, pDa) = cur_acc
                kpb, pb = p_rot.next()
                em.op("act", lambda e, psn=psn, pb=pb: e.activation(out=pb, in_=psn, func=AF.Exp), reads=[ksn], writes=[kpb])
                first = (ti == 0); last = (ti == nt - 1)
                em.op("pe", lambda e, pb=pb, kt=kt, first=first, last=last, pOa=pOa: e.matmul(pOa, v_sb[:, kt, :], pb, start=first, stop=last), reads=[kpb, "v_sb"], writes=[kO])
                em.op("pe", lambda e, pb=pb, first=first, last=last, pDa=pDa: e.matmul(pDa, onesb, pb, start=first, stop=last), reads=[kpb, "cb"], writes=[kDn])
                if last:
                    hs = slice(par * 64, par * 64 + 64)
                    kf, tf = f_rot.next()
                    em.op("act", lambda e, tf=tf, hs=hs, pDa=pDa: e.activation(out=tf[hs, :], in_=pDa[hs, :], func=AF.Ln), reads=[kDn], writes=[kf])
                    em.op("act", lambda e, tf=tf, hs=hs: e.activation(out=tf[hs, :], in_=tf[hs, :], func=AF.Exp, scale=-1.0), reads=[kf], writes=[kf])
                    ko, to = ostg_rot.next()
                    em.op("dve", lambda e, tf=tf, to=to, hs=hs, pOa=pOa: e.tensor_tensor(out=to[hs, :], in0=pOa[hs, :], in1=tf[hs, :], op=ALU.mult), reads=[kO, kf], writes=[ko])
                    em.dma("sp", lambda e, to=to, hs=hs, c=c, b=b: e.dma_start(out=oT_d[c][hs, blkc(b)], in_=to[hs, :]), reads=[ko], writes=["oTd"])

        if stop_after == 'attn':
            return True
        for b in range(NB):
            em.dma("sp", lambda e, b=b: e.dma_start(out=blkin[:, :, 0:BW], in_=oT_d[:, :, blkc(b)].rearrange("c p n -> p c n")), reads=["oTd"], writes=["blkin"])
            for oc in range(8):
                kw, wv = wpiece_small(wo[:, oc * 128:(oc + 1) * 128])
                kp, pa = pA_rot.next()
                for kc in range(8):
                    em.op("pe", lambda e, wv=wv, kc=kc, pa=pa: e.matmul(pa, wv[:, kc, :], blkin[:, kc, 0:BW], start=(kc == 0), stop=(kc == 7)), reads=[kw, "blkin"], writes=[kp])
                em.op("act", lambda e, pa=pa, oc=oc: e.activation(out=yT[:, oc, :], in_=pa, func=AF.Identity), reads=[kp], writes=["yT"])
            postnorm_residual(b, 8)

        if stop_after == 'wo':
            return True
        for b in range(NB):
            rms_rstd(lambda c, b=b: xT[:, c, blkc(b)], ["xT"], 1.0 / D)
            modulate(b, 16, 24, lambda c: hT[:, c, :], "big2")
            em.dma("sp", lambda e, b=b: e.dma_start(out=h2_d[:, :, blkc(b)].rearrange("c p n -> p c n"), in_=hT), reads=["big2"], writes=["h2d"])
        for b in range(NB):
            lo = max(b * BW - 1, 0); hi = min((b + 1) * BW + 1, T)
            o0 = lo - (b * BW - 1)
            em.dma("sp", lambda e, lo=lo, hi=hi, o0=o0: e.dma_start(out=blkin[:, :, o0:o0 + hi - lo], in_=h2_d[:, :, lo:hi].rearrange("c p n -> p c n")), reads=["h2d"], writes=["blkin"])
            if b == 0:
                em.op("dve", lambda e: e.memset(blkin[:, :, 0:1], 0.0), writes=["blkin"])
            if b == NB - 1:
                em.op("dve", lambda e: e.memset(blkin[:, :, BW + 1:BW + 2], 0.0), writes=["blkin"])
            for j in range(NJ):
                accs = []
                for half in range(2):
                    jj = half * NJ + j
                    col0 = half * DFF + j * 128
                    kw, wv = wpiece_small(wup_d[l][:, col0:col0 + 128])
                    kp, pa = pF_rot.next()
                    kh, ph = pH_rot.next()
                    for kc in range(8):
                        em.op("pe", lambda e, wv=wv, kc=kc, pa=pa: e.matmul(pa, wv[:, kc, :], blkin[:, kc, 1:BW + 1], start=(kc == 0), stop=(kc == 7)), reads=[kw, "blkin"], writes=[kp])
                    for kc in range(8):
                        em.op("pe", lambda e, wv=wv, kc=kc, ph=ph: e.matmul(ph[:, 0:2], wv[:, kc, :], blkin[:, kc, 0:BW + 2:BW + 1], start=(kc == 0), stop=(kc == 7)), reads=[kw, "blkin"], writes=[kh])
                    ka, ac = acc_rot.next()
                    ku, us = u_rot.next()
                    em.op("act", lambda e, pa=pa, ac=ac, jj=jj: e.activation(out=ac, in_=pa, func=AF.Identity, bias=convp[:, jj * 4 + 3:jj * 4 + 4], scale=convp[:, jj * 4 + 1:jj * 4 + 2]), reads=[kp, "convp"], writes=[ka])
                    em.op("act", lambda e, pa=pa, us=us: e.activation(out=us[:, 1:BW + 1], in_=pa, func=AF.Identity), reads=[kp], writes=[ku])
                    em.op("dve", lambda e, us=us, ph=ph: e.tensor_copy(out=us[:, 0:BW + 2:BW + 1], in_=ph[:, 0:2]), reads=[kh], writes=[ku])
                    em.op("dve", lambda e, us=us, ac=ac, jj=jj: e.scalar_tensor_tensor(out=ac, in0=us[:, 0:BW], scalar=convp[:, jj * 4:jj * 4 + 1], in1=ac, op0=ALU.mult, op1=ALU.add), reads=[ku, ka, "convp"], writes=[ka])
                    em.op("dve", lambda e, us=us, ac=ac, jj=jj: e.scalar_tensor_tensor(out=ac, in0=us[:, 2:BW + 2], scalar=convp[:, jj * 4 + 2:jj * 4 + 3], in1=ac, op0=ALU.mult, op1=ALU.add), reads=[ku, ka, "convp"], writes=[ka])
                    em.op("dve", lambda e, us=us, ac=ac, jj=jj: e.scalar_tensor_tensor(out=ac[:, 0:BW:256], in0=us[:, 0:BW:256], scalar=wcor[:, jj * 2:jj * 2 + 1], in1=ac[:, 0:BW:256], op0=ALU.mult, op1=ALU.add), reads=[ku, ka, "wcor"], writes=[ka])
                    em.op("dve", lambda e, us=us, ac=ac, jj=jj: e.scalar_tensor_tensor(out=ac[:, 255:BW:256], in0=us[:, 257:BW + 2:256], scalar=wcor[:, jj * 2 + 1:jj * 2 + 2], in1=ac[:, 255:BW:256], op0=ALU.mult, op1=ALU.add), reads=[ku, ka, "wcor"], writes=[ka])
                    accs.append((ka, ac))
                (kaa, aa), (kag, ag) = accs
                em.op("act", lambda e, aa=aa: e.activation(out=aa, in_=aa, func=AF.Silu), reads=[kaa], writes=[kaa])
                em.op("dve", lambda e, aa=aa, ag=ag, j=j: e.tensor_tensor(out=big2[:, j, :], in0=aa, in1=ag, op=ALU.mult), reads=[kaa, kag], writes=["big2"])
            for oc in range(8):
                kw, wv = wpiece_big(wdn_d[l][:, oc * 128:(oc + 1) * 128], nk=NJ, ncol=128)
                kp, pa = pA_rot.next()
                for j in range(NJ):
                    em.op("pe", lambda e, wv=wv, j=j, pa=pa: e.matmul(pa, wv[:, j, :], big2[:, j, :], start=(j == 0), stop=(j == NJ - 1)), reads=[kw, "big2"], writes=[kp])
                em.op("act", lambda e, pa=pa, oc=oc: e.activation(out=yT[:, oc, :], in_=pa, func=AF.Identity), reads=[kp], writes=["yT"])
            postnorm_residual(b, 24)

    for l in range(n_layers):
        if do_layer(l):
            break

    for t in range(16):
        yo = xin[t % 2]
        for g4 in range(2):
            for c4 in range(4):
                c = g4 * 4 + c4
                em.op("pe", lambda e, c=c, c4=c4, t=t: e.transpose(pT[:, c4 * 128:(c4 + 1) * 128], xT[:, c, t * 128:(t + 1) * 128], idf), reads=["xT", "idf"], writes=["pT"])
            em.op("dve", lambda e, g4=g4, yo=yo: e.tensor_copy(out=yo[:, g4 * 512:(g4 + 1) * 512], in_=pT), reads=["pT"], writes=["yT"])
        em.dma("sp", lambda e, t=t, yo=yo: e.dma_start(out=y_d[t * 128:(t + 1) * 128, :], in_=yo), reads=["yT"], writes=["y_out"])

    em.finish()
    em.build()
    return nc


def _fm(v):
    return np.ascontiguousarray(np.asarray(v, np.float32).reshape(-1, 128).T)


def prepare(x_prompt, x_sample, cache_na_k, cache_na_v, cache_gqa_k, cache_gqa_v, c, c_ctx,
           ada_w, ada_b, norm_mix_pre, norm_mix_post, norm_ffn_pre, norm_ffn_post,
           na_w_qkv, na_w_o, na_rpb, gqa_w_qkv, gqa_w_o, gqa_q_norm, gqa_k_norm,
           ffn_w_up, ffn_conv_w, ffn_conv_b, ffn_w_down):
    f32 = np.float32
    bf = ml_dtypes.bfloat16
    A = lambda a: np.ascontiguousarray(np.asarray(a, f32))
    x_prompt = A(x_prompt); x_sample = A(x_sample)
    shared = {}
    shared["ada_w"] = A(ada_w)
    shared["ada_b"] = np.concatenate([_fm(np.asarray(ada_b)[l]) for l in range(DEPTH)], axis=1)
    gl = []
    for l in range(DEPTH):
        gl += [_fm(np.asarray(norm_mix_pre)[l]), _fm(np.asarray(norm_mix_post)[l]), _fm(np.asarray(norm_ffn_pre)[l]), _fm(np.asarray(norm_ffn_post)[l])]
    shared["gains"] = np.ascontiguousarray(np.concatenate(gl, axis=1))
    shared["wqkv_na"] = A(na_w_qkv); shared["wo_na"] = A(na_w_o); shared["wo_g"] = A(gqa_w_o)
    wg = np.asarray(gqa_w_qkv, f32)
    wq = wg[:, :, :1024]
    wk = wg[:, :, 1024:1280].reshape(2, 1024, 4, 1, 64)
    wv = wg[:, :, 1280:1536].reshape(2, 1024, 4, 1, 64)
    wkd = np.broadcast_to(wk, (2, 1024, 4, 2, 64)).reshape(2, 1024, 512)
    wvd = np.broadcast_to(wv, (2, 1024, 4, 2, 64)).reshape(2, 1024, 512)
    shared["wqkv_g"] = np.ascontiguousarray(np.concatenate([wq, wkd, wvd], axis=2))
    qn = np.asarray(gqa_q_norm, f32); kn = np.asarray(gqa_k_norm, f32)
    qkg = np.zeros((128, 4), f32)
    for i in range(2):
        qkg[:, 2 * i] = np.tile(qn[i], 2); qkg[:, 2 * i + 1] = np.tile(kn[i], 2)
    shared["qkg"] = qkg
    shared["w_up"] = A(ffn_w_up); shared["w_down"] = A(ffn_w_down)
    cw = np.asarray(ffn_conv_w, f32); cbias = np.asarray(ffn_conv_b, f32)
    convp = np.zeros((DEPTH, 128, 44, 4), f32)
    for l in range(DEPTH):
        for k in range(3):
            convp[l, :, :, k] = cw[l, k].reshape(44, 128).T
        convp[l, :, :, 3] = cbias[l].reshape(44, 128).T
    shared["convp"] = convp.reshape(DEPTH, 128, 176)
    shared["ident_f"] = np.eye(128, dtype=f32)
    blk = np.zeros((128, 128), f32); blk[:64, :64] = 1; blk[64:, 64:] = 1
    rot = np.zeros((128, 128), f32)
    for p in range(128):
        d = p % 32
        rot[p, p + 16 if d < 16 else p - 16] = 1.0
    shared["constb"] = np.ascontiguousarray(np.concatenate([np.eye(128, dtype=f32), np.ones((128, 128), f32), blk, rot], axis=1)).astype(bf)
    rpb = np.asarray(na_rpb, f32)
    p = np.arange(128); hi = (p >= 64).astype(int); kc = p % 64
    ep = np.arange(22); qc = np.arange(64)
    dr = 17 - ep[None, :] + hi[:, None]
    dc = np.clip(kc[:, None] - qc[None, :] + 15, 0, 30)
    cst = np.clip(qc - 8, 0, 48)
    colok = (kc[:, None] >= cst[None, :]) & (kc[:, None] < cst[None, :] + 16)
    drv_ok = (dr >= 0) & (dr <= 14)
    drc = np.clip(dr, 0, 14)
    tab = rpb[:, :, drc[:, :, None], dc[:, None, :]]
    tab = np.where(drv_ok[None, None, :, :, None], tab, 0.0)
    tab = np.where(colok[None, None, :, None, :], tab, NEG)
    bias_sample = np.ascontiguousarray(tab.reshape(2, 16, 128, 1408)).astype(bf)
    bias_prompt = np.zeros((2, 16, 128, 1408), bf)
    tok = np.arange(T)
    indA_p = np.zeros((2, 32, 2304), f32); indB_p = np.zeros((2, 32, 2048), f32)
    seq = tok // 256
    for j in range(8):
        indA_p[:, j, :T] = (seq == j)
        indA_p[:, j, T:] = 1.0
        indB_p[:, j, :] = np.where(seq == j, 0.0, NEG)
    indA_s = np.zeros((2, 32, 2304), f32); indB_s = np.zeros((2, 32, 2048), f32)
    row = tok // 64
    rs = np.clip(row - 4, 0, 24)
    for j in range(32):
        indA_s[0, j, :T] = (row == j)
        indB_s[0, j, :] = np.where((j >= rs) & (j < rs + 8), 0.0, NEG)
    d = np.arange(128) % 64
    fidx = d % 16
    freqs = (10000.0 ** (-(np.arange(16, dtype=f32)) / 16)).astype(f32)
    posr = (tok // 64).astype(f32); posc = (tok % 64).astype(f32)
    pos = np.where((d < 32)[:, None], posr[None, :], posc[None, :]).astype(f32)
    ang = (pos * freqs[fidx][:, None]).astype(f32)
    cos_s = np.cos(ang).astype(f32)
    sgn = np.where((d % 32) < 16, -1.0, 1.0).astype(f32)
    sin_s = (np.sin(ang).astype(f32) * sgn[:, None]).astype(f32)
    cos_p = np.ones((128, T), f32); sin_p = np.zeros((128, T), f32)
    cnk = np.asarray(cache_na_k, f32); cnv = np.asarray(cache_na_v, f32)
    cgk = np.asarray(cache_gqa_k, f32); cgv = np.asarray(cache_gqa_v, f32)

    def ctx_for(bb):
        out = {}
        k = cnk[bb].reshape(2, 256, 8, 128)
        out["ctxkT_na"] = np.ascontiguousarray(k.transpose(0, 3, 2, 1).reshape(2, 128, 2048))
        out["ctxv_na"] = np.ascontiguousarray(cnv[bb].reshape(2, 256, 1024))
        kg = cgk[bb]
        kgd = np.broadcast_to(kg[:, :, :, None, :], (2, 256, 4, 2, 64)).reshape(2, 256, 4, 128)
        out["ctxkT_g"] = np.ascontiguousarray(kgd.transpose(0, 3, 2, 1).reshape(2, 128, 1024))
        vg = cgv[bb]
        out["ctxv_g"] = np.ascontiguousarray(np.broadcast_to(vg[:, :, :, None, :], (2, 256, 4, 2, 64)).reshape(2, 256, 512))
        return out
    zero_ctx = {"ctxkT_na": np.zeros((2, 128, 2048), f32), "ctxv_na": np.zeros((2, 256, 1024), f32),
                "ctxkT_g": np.zeros((2, 128, 1024), f32), "ctxv_g": np.zeros((2, 256, 512), f32)}

    in_maps = []
    for core in range(NC8):
        m = dict(shared)
        role = core if core < 6 else core - 6
        if role < 4:
            m["x"] = np.ascontiguousarray(x_prompt[role * 8:(role + 1) * 8].reshape(T, D))
            m["cond"] = _fm(c_ctx)
            m["biasT"] = bias_prompt
            m["indA"] = indA_p.astype(bf); m["indB"] = indB_p.astype(bf)
            m["cosT"] = cos_p; m["sinT"] = sin_p
            m["cneg"] = np.full((128, 1), -1.0, f32)
            m.update(zero_ctx)
        else:
            bb = role - 4
            m["x"] = np.ascontiguousarray(x_sample[bb])
            m["cond"] = _fm(np.asarray(c)[bb])
            m["biasT"] = bias_sample
            m["indA"] = indA_s.astype(bf); m["indB"] = indB_s.astype(bf)
            m["cosT"] = cos_s; m["sinT"] = sin_s
            m["cneg"] = np.zeros((128, 1), f32)
            m.update(ctx_for(bb))
        in_maps.append(m)

    return in_maps


def assemble(R):
    f32 = np.float32
    y_prompt = np.concatenate([np.asarray(R[k]["y"], f32).reshape(8, 256, D) for k in range(4)], axis=0)
    y_sample = np.stack([np.asarray(R[4]["y"], f32), np.asarray(R[5]["y"], f32)], axis=0)

    def gather(name, feat):
        parts = [np.asarray(R[k][name], f32).reshape(2, 8, 256, feat).transpose(1, 0, 2, 3) for k in range(4)]
        return np.concatenate(parts, axis=0)
    na_k = gather("nk_na", 1024).reshape(32, 2, 256, 16, 64)
    na_v = gather("nv_na", 1024).reshape(32, 2, 256, 16, 64)
    g_k = gather("nk_g", 256).reshape(32, 2, 256, 4, 64)
    g_v = gather("nv_g", 256).reshape(32, 2, 256, 4, 64)
    return (y_prompt, y_sample, na_k, na_v, g_k, g_v)


def kernel(**inputs):
    in_maps = prepare(**inputs)
    nc = build_program()
    res = run_bass_kernel_spmd(nc, in_maps, core_ids=list(range(NC8)))
    return assemble(res.results)
```

```python
import numpy as np
import ml_dtypes
import concourse.bass as bass
import concourse.mybir as mybir
from concourse.bass_utils import run_bass_kernel_spmd

F32 = mybir.dt.float32
BF16 = mybir.dt.bfloat16
AF = mybir.ActivationFunctionType
ALU = mybir.AluOpType

D = 1024; NC8 = 8; T = 2048; NB = 4; BW = 512; DFF = 2816; NJ = 22; DEPTH = 4
EPS = 1e-6
NEG = -30000.0


class Em:
    def __init__(self, nc):
        self.nc = nc
        self.q = {e: [] for e in ("pe", "act", "dve", "sp", "pool")}
        self.cnt = {e: 0 for e in ("pe", "act", "dve")}
        self.sem = {e: nc.alloc_semaphore("s_" + e) for e in ("pe", "act", "dve")}
        self.P = 8
        self.dq = {}
        for qn in ("sp", "pool"):
            self.dq[qn] = {"sems": [nc.alloc_semaphore("d_%s%d" % (qn, i)) for i in range(self.P)], "k": 0}
        self.waited = {}
        self.lastw = {}
        self.readers = {}

    def _deps(self, reads, writes):
        d = []
        for r in reads:
            if r in self.lastw:
                d.append(self.lastw[r])
        for w in writes:
            if w in self.lastw:
                d.append(self.lastw[w])
            d.extend(self.readers.get(w, ()))
        return d

    def _waits(self, eng, deps):
        for (sem, val, name, src) in deps:
            if src == eng and eng == "pe":
                continue
            key = (eng, name)
            if self.waited.get(key, 0) >= val:
                continue
            self.waited[key] = val
            self.q[eng].append(lambda e, sem=sem, val=val: e.wait_ge(sem, val))

    def _record(self, tok, reads, writes):
        for w in writes:
            self.lastw[w] = tok
            self.readers[w] = []
        for r in reads:
            if r not in writes:
                self.readers.setdefault(r, []).append(tok)

    PSUM_KEYS = frozenset(['pA', 'pB', 'pS0', 'pS1', 'pO', 'pD', 'pN', 'pT'])

    def op(self, eng, fn, reads=(), writes=()):
        px = [r for r in reads if r in self.PSUM_KEYS and r not in writes]
        if px:
            writes = list(writes) + px
        self._waits(eng, self._deps(reads, writes))
        self.cnt[eng] += 1
        sem = self.sem[eng]
        self.q[eng].append(lambda e, fn=fn, sem=sem: fn(e).then_inc(sem, 1))
        self._record((sem, self.cnt[eng], "s_" + eng, eng), reads, writes)

    def dma(self, qn, fn, reads=(), writes=()):
        dq = self.dq[qn]
        k = dq["k"]; dq["k"] += 1
        slot = k % self.P
        sem = dq["sems"][slot]
        name = "d_%s%d" % (qn, slot)
        deps = self._deps(reads, writes)
        need = 16 * (k // self.P)
        if need > 0:
            deps.append((sem, need, name, "dma"))
        self._waits(qn, deps)
        self.q[qn].append(lambda e, fn=fn, sem=sem: fn(e).then_inc(sem, 16))
        self._record((sem, need + 16, name, "dma"), reads, writes)

    def finish(self):
        deps = []
        for qn in ("sp", "pool"):
            dq = self.dq[qn]
            for slot in range(self.P):
                n = (dq["k"] - slot + self.P - 1) // self.P if dq["k"] > slot else 0
                if n > 0:
                    deps.append((dq["sems"][slot], 16 * n, "d_%s%d" % (qn, slot), "dma"))
        for e in ("pe", "act", "dve"):
            if self.cnt[e]:
                deps.append((self.sem[e], self.cnt[e], "s_" + e, e))
        self._waits("sp", deps)

    def build(self):
        nc = self.nc
        with nc.Block() as block:
            @block.sync
            def _(e):
                for f in self.q["sp"]:
                    f(e)

            @block.gpsimd
            def _(e):
                for f in self.q["pool"]:
                    f(e)

            @block.tensor
            def _(e):
                for f in self.q["pe"]:
                    f(e)

            @block.scalar
            def _(e):
                for f in self.q["act"]:
                    f(e)

            @block.vector
            def _(e):
                for f in self.q["dve"]:
                    f(e)


class Rot:
    def __init__(self, items):
        self.items = items; self.i = 0

    def next(self):
        it = self.items[self.i % len(self.items)]; self.i += 1
        return it


def build_program(n_layers=DEPTH, stop_after=None):
    nc = bass.Bass("TRN2", target_bir_lowering=False)
    em = Em(nc)

    def din(name, shape, dt=F32):
        return nc.dram_tensor(name, list(shape), dt, kind="ExternalInput").ap()

    def dout(name, shape):
        return nc.dram_tensor(name, list(shape), F32, kind="ExternalOutput").ap()

    x_d = din("x", [T, D]); cond_d = din("cond", [128, 8])
    adaw_d = din("ada_w", [DEPTH, D, 6 * D]); adab_d = din("ada_b", [128, DEPTH * 48])
    gains_d = din("gains", [128, DEPTH * 32])
    wqkvn_d = din("wqkv_na", [2, D, 3072]); won_d = din("wo_na", [2, D, D])
    wqkvg_d = din("wqkv_g", [2, D, 2048]); wog_d = din("wo_g", [2, D, D])
    qkg_d = din("qkg", [128, 4])
    wup_d = din("w_up", [DEPTH, D, 2 * DFF]); wdn_d = din("w_down", [DEPTH, DFF, D])
    convp_d = din("convp", [DEPTH, 128, 44 * 4])
    bias_d = din("biasT", [2, 16, 128, 1408], BF16)
    indA_d = din("indA", [2, 32, 2304], BF16); indB_d = din("indB", [2, 32, 2048], BF16)
    ckn_d = din("ctxkT_na", [2, 128, 8 * 256]); cvn_d = din("ctxv_na", [2, 256, 1024])
    ckg_d = din("ctxkT_g", [2, 128, 4 * 256]); cvg_d = din("ctxv_g", [2, 256, 512])
    cos_d = din("cosT", [128, T]); sin_d = din("sinT", [128, T])
    cneg_d = din("cneg", [128, 1])
    idf_d = din("ident_f", [128, 128]); cb_d = din("constb", [128, 4 * 128], BF16)
    y_d = dout("y", [T, D])
    nkn_d = dout("nk_na", [2, T, D]); nvn_d = dout("nv_na", [2, T, D])
    nkg_d = dout("nk_g", [2, T, 256]); nvg_d = dout("nv_g", [2, T, 256])
    qT_d = nc.dram_tensor("qT_s", [8, 128, T], BF16, kind="ExternalOutput").ap()
    kT_d = nc.dram_tensor("kT_s", [8, 128, T], BF16, kind="ExternalOutput").ap()
    vS_d = nc.dram_tensor("vS_s", [T, D], BF16, kind="ExternalOutput").ap()
    oT_d = nc.dram_tensor("oT_s", [8, 128, T], BF16, kind="ExternalOutput").ap()
    h2_d = nc.dram_tensor("h2_s", [8, 128, T], BF16, kind="ExternalOutput").ap()

    def sb(name, shape, dt=F32):
        return nc.alloc_sbuf_tensor("sb_" + name, list(shape), dt).ap()

    xT = sb("xT", [128, 8, T])
    big2 = sb("big2", [128, NJ, BW], BF16)
    hT = big2[:, 0:8, :]
    blkin = sb("blkin", [128, 8, BW + 2], BF16)
    yT = sb("yT", [128, 8, BW])
    idf = sb("idf", [128, 128]); cb = sb("cb", [128, 4 * 128], BF16)
    idb = cb[:, 0:128]; onesb = cb[:, 128:256]; blkb = cb[:, 256:384]; rotb = cb[:, 384:512]
    cond_s = sb("cond_s", [128, 8]); silc = sb("silc", [128, 8], BF16)
    adab = sb("adab", [128, DEPTH * 48]); gains = sb("gains", [128, DEPTH * 32])
    qkg = sb("qkg", [128, 4]); cneg = sb("cneg", [128, 1])
    convp = sb("convp", [128, 44 * 4]); wcor = sb("wcor", [128, 44 * 2])
    mod = sb("mod", [128, 48]); drv = sb("drv", [128, 32])
    epsD = sb("epsD", [128, 1]); zerob = sb("zerob", [128, 8], BF16)
    kt2 = [sb("kt2_%d" % i, [128, 2304], BF16) for i in range(2)]; v_sb = sb("v_sb", [128, 18, 128], BF16)
    qe = [sb("qe%d" % i, [128, BW], BF16) for i in range(2)]
    qo = [sb("qo%d" % i, [128, BW], BF16) for i in range(2)]
    bias_sb = [sb("bias%d" % i, [128, 1408], BF16) for i in range(2)]
    p_rot = Rot([("p%d" % i, sb("p%d" % i, [128, BW], BF16)) for i in range(5)])
    wsm_rot = Rot([("wsm%d" % i, sb("wsm%d" % i, [128, 8, 128], BF16)) for i in range(4)])
    wbg_rot = Rot([("wbg%d" % i, sb("wbg%d" % i, [128, 8, 512], BF16)) for i in range(2)])
    f_rot = Rot([("f%d" % i, sb("f%d" % i, [128, BW])) for i in range(6)])
    b_rot = Rot([("b%d" % i, sb("b%d" % i, [128, BW], BF16)) for i in range(6)])
    u_rot = Rot([("u%d" % i, sb("u%d" % i, [128, BW + 2])) for i in range(3)])
    acc_rot = Rot([("acc%d" % i, sb("acc%d" % i, [128, BW])) for i in range(4)])
    cos_sb = sb("cos_sb", [128, BW]); sin_sb = sb("sin_sb", [128, BW])
    rstd = sb("rstd", [128, BW])
    ostg_rot = Rot([("ostg%d" % i, sb("ostg%d" % i, [128, BW], BF16)) for i in range(2)])
    nkst = sb("nkst", [128, 4, 64])

    def ps(name):
        return nc.alloc_psum_tensor("ps_" + name, [128, 512], F32).ap()
    pA_rot = Rot([("pA", ps("pA")), ("pB", ps("pB"))])
    pS_rot = Rot([("pS0", ps("pS0")), ("pS1", ps("pS1"))])
    pO = ps("pO"); pD = ps("pD"); pN = ps("pN"); pT = ps("pT")
    pF_rot = Rot(pA_rot.items + [("pO", pO), ("pD", pD)])
    pH_rot = Rot([("pT", pT)] + pS_rot.items)
    pS4_rot = Rot(pS_rot.items + pA_rot.items)
    pNT_rot = Rot([("pN", pN), ("pT", pT)])
    oacc_rot = Rot([("pO", pO, "pD", pD), ("pN", pN, "pT", pT)])

    def blkc(b):
        return slice(b * BW, (b + 1) * BW)

    def wpiece_small(src_ap):
        k, t = wsm_rot.next()
        em.dma("pool", lambda e: e.dma_start(out=t, in_=src_ap.rearrange("(kc p) n -> p kc n", p=128)), writes=[k])
        return k, t

    def wpiece_big(src_ap, nk=8, ncol=512):
        k, t = wbg_rot.next()
        if nk == 8 and ncol == 512:
            view = t
        else:
            view = nc_view(t, nk, ncol)
        em.dma("pool", lambda e: e.dma_start(out=view, in_=src_ap.rearrange("(kc p) n -> p kc n", p=128)), writes=[k])
        return k, view

    def nc_view(t, nk, ncol):
        flat = t.rearrange("p a b -> p (a b)")
        return flat[:, 0:nk * ncol].rearrange("p (a b) -> p a b", b=ncol)

    def rms_rstd(src_fn, src_keys, inv_n):
        for c in range(8):
            kq, sq = b_rot.next()
            em.op("act", lambda e, c=c, sq=sq: e.activation(out=sq, in_=src_fn(c), func=AF.Square), reads=src_keys, writes=[kq])
            em.op("pe", lambda e, c=c, sq=sq: e.matmul(pN, onesb, sq, start=(c == 0), stop=(c == 7)), reads=[kq, "cb"], writes=["pN"])
        kf, tf = f_rot.next()
        em.op("act", lambda e: e.activation(out=tf, in_=pN, func=AF.Ln, bias=epsD[:, 0:1], scale=inv_n), reads=["pN", "epsD"], writes=[kf])
        em.op("act", lambda e: e.activation(out=rstd, in_=tf, func=AF.Exp, scale=-0.5), reads=[kf], writes=["rstd"])

    def modulate(b, Acol, Bcol, dst_fn, dst_key):
        for c in range(8):
            kf, tf = f_rot.next()
            em.op("dve", lambda e, c=c, tf=tf: e.scalar_tensor_tensor(out=tf, in0=xT[:, c, blkc(b)], scalar=drv[:, Acol + c:Acol + c + 1], in1=rstd, op0=ALU.mult, op1=ALU.mult),
                  reads=["xT", "drv", "rstd"], writes=[kf])
            em.op("act", lambda e, c=c, tf=tf: e.activation(out=dst_fn(c), in_=tf, func=AF.Identity, bias=mod[:, Bcol + c:Bcol + c + 1], scale=1.0),
                  reads=[kf, "mod"], writes=[dst_key])

    def postnorm_residual(b, Gcol):
        rms_rstd(lambda c: yT[:, c, :], ["yT"], 1.0 / D)
        for c in range(8):
            kf, tf = f_rot.next()
            em.op("dve", lambda e, c=c, tf=tf: e.scalar_tensor_tensor(out=tf, in0=yT[:, c, :], scalar=drv[:, Gcol + c:Gcol + c + 1], in1=rstd, op0=ALU.mult, op1=ALU.mult),
                  reads=["yT", "drv", "rstd"], writes=[kf])
            em.op("dve", lambda e, c=c, tf=tf: e.tensor_tensor(out=xT[:, c, blkc(b)], in0=xT[:, c, blkc(b)], in1=tf, op=ALU.add),
                  reads=[kf, "xT"], writes=["xT"])

    for (dst, src, key) in [(idf, idf_d, "idf"), (cb, cb_d, "cb"), (cond_s, cond_d, "cond"), (adab, adab_d, "adab"),
                            (gains, gains_d, "gains"), (qkg, qkg_d, "qkg"), (cneg, cneg_d, "cneg")]:
        em.dma("sp", lambda e, dst=dst, src=src: e.dma_start(out=dst, in_=src), writes=[key])
    em.op("dve", lambda e: e.memset(epsD, EPS), writes=["epsD"])
    em.op("dve", lambda e: e.memset(zerob, 0.0), writes=["zerob"])
    for i in range(2):
        em.op("dve", lambda e, i=i: e.memset(qe[i], 0.0), writes=["qe%d" % i])
        em.op("dve", lambda e, i=i: e.memset(qo[i], 0.0), writes=["qo%d" % i])
    em.op("act", lambda e: e.activation(out=silc, in_=cond_s, func=AF.Silu), reads=["cond"], writes=["silc"])
    for i in range(2):
        em.op("dve", lambda e, i=i: e.memset(kt2[i], 0.0), writes=["kt2_%d" % i])
    yflat = yT.rearrange("p a b -> p (a b)")
    xin = [yflat[:, 0:1024], yflat[:, 1024:2048]]
    for t in range(16):
        xi = xin[t % 2]; kx = "xin%d" % (t % 2)
        em.dma("sp", lambda e, t=t, xi=xi: e.dma_start(out=xi, in_=x_d[t * 128:(t + 1) * 128, :]), writes=[kx, "yT"])
        for g4 in range(2):
            for c4 in range(4):
                c = g4 * 4 + c4
                em.op("pe", lambda e, c=c, c4=c4, xi=xi: e.transpose(pT[:, c4 * 128:(c4 + 1) * 128], xi[:, c * 128:(c + 1) * 128], idf), reads=[kx, "idf"], writes=["pT"])
            em.op("dve", lambda e, g4=g4, t=t: e.tensor_copy(out=xT[:, g4 * 4:(g4 + 1) * 4, t * 128:(t + 1) * 128], in_=pT.rearrange("p (a b) -> p a b", b=128)),
                  reads=["pT"], writes=["xT"])

    def do_layer(l):
        i = l // 2
        is_na = (l % 2 == 0)
        for pi in range(12):
            kw, wv = wpiece_big(adaw_d[l][:, pi * 512:(pi + 1) * 512])
            for oc4 in range(4):
                ch = pi * 4 + oc4
                for kc in range(8):
                    em.op("pe", lambda e, wv=wv, oc4=oc4, kc=kc, ch=ch: e.matmul(pT[:, ch:ch + 1], wv[:, kc, oc4 * 128:(oc4 + 1) * 128], silc[:, kc:kc + 1], start=(kc == 0), stop=(kc == 7)),
                          reads=[kw, "silc"], writes=["pT"])
        em.op("dve", lambda e, l=l: e.tensor_tensor(out=mod, in0=pT[:, 0:48], in1=adab[:, l * 48:(l + 1) * 48], op=ALU.add), reads=["pT", "adab"], writes=["mod"])
        g0 = l * 32
        em.op("dve", lambda e: e.scalar_tensor_tensor(out=drv[:, 0:8], in0=mod[:, 8:16], scalar=1.0, in1=gains[:, g0:g0 + 8], op0=ALU.add, op1=ALU.mult), reads=["mod", "gains"], writes=["drv"])
        em.op("dve", lambda e: e.tensor_tensor(out=drv[:, 8:16], in0=mod[:, 16:24], in1=gains[:, g0 + 8:g0 + 16], op=ALU.mult), reads=["mod", "gains"], writes=["drv"])
        em.op("dve", lambda e: e.scalar_tensor_tensor(out=drv[:, 16:24], in0=mod[:, 32:40], scalar=1.0, in1=gains[:, g0 + 16:g0 + 24], op0=ALU.add, op1=ALU.mult), reads=["mod", "gains"], writes=["drv"])
        em.op("dve", lambda e: e.tensor_tensor(out=drv[:, 24:32], in0=mod[:, 40:48], in1=gains[:, g0 + 24:g0 + 32], op=ALU.mult), reads=["mod", "gains"], writes=["drv"])
        em.dma("sp", lambda e, l=l: e.dma_start(out=convp, in_=convp_d[l]), writes=["convp"])
        ty = 0 if is_na else 1
        for p2 in range(2):
            em.dma("sp", lambda e, ty=ty, p2=p2: e.dma_start(out=kt2[p2][64:96, :], in_=indA_d[ty]), writes=["kt2_%d" % p2])
        cp3 = convp.rearrange("p (j f) -> p j f", f=4)
        wc3 = wcor.rearrange("p (j f) -> p j f", f=2)
        em.op("dve", lambda e: e.tensor_scalar(out=wc3[:, :, 0:1], in0=cp3[:, :, 0:1], scalar1=cneg[:, 0:1], scalar2=None, op0=ALU.mult), reads=["convp", "cneg"], writes=["wcor"])
        em.op("dve", lambda e: e.tensor_scalar(out=wc3[:, :, 1:2], in0=cp3[:, :, 2:3], scalar1=cneg[:, 0:1], scalar2=None, op0=ALU.mult), reads=["convp", "cneg"], writes=["wcor"])

        if stop_after == 'ada':
            return True
        wqkv = (wqkvn_d if is_na else wqkvg_d)[i]
        wo = (won_d if is_na else wog_d)[i]
        nqk = 16 if is_na else 12

        for b in range(NB):
            rms_rstd(lambda c, b=b: xT[:, c, blkc(b)], ["xT"], 1.0 / D)
            if stop_after == 'rms':
                break
            modulate(b, 0, 0, lambda c: hT[:, c, :], "big2")
            if stop_after == 'mod':
                break
            if not is_na:
                em.dma("sp", lambda e, b=b: e.dma_start(out=cos_sb, in_=cos_d[:, blkc(b)]), writes=["cos"])
                em.dma("sp", lambda e, b=b: e.dma_start(out=sin_sb, in_=sin_d[:, blkc(b)]), writes=["sin"])
            def qk_mm(oc, b=b):
                kw, wv = wpiece_small(wqkv[:, oc * 128:(oc + 1) * 128])
                kp, pa = pF_rot.next()
                for kc in range(8):
                    em.op("pe", lambda e, wv=wv, kc=kc, pa=pa: e.matmul(pa, wv[:, kc, :], hT[:, kc, :], start=(kc == 0), stop=(kc == 7)), reads=[kw, "big2"], writes=[kp])
                is_q = oc < 8
                st = dict(oc=oc, kp=kp, pa=pa, is_q=is_q, dst=(qT_d if is_q else kT_d)[oc if is_q else oc - 8][:, blkc(b)], dkey="qTd" if is_q else "kTd")
                return st

            def qk_A(st, b=b):
                kp, pa, is_q = st["kp"], st["pa"], st["is_q"]
                if is_na:
                    kb, tb = b_rot.next()
                    em.op("act", lambda e, pa=pa, tb=tb, is_q=is_q: e.activation(out=tb, in_=pa, func=AF.Identity, scale=(0.125 if is_q else 1.0)), reads=[kp], writes=[kb])
                    em.dma("sp", lambda e, dst=st["dst"], tb=tb: e.dma_start(out=dst, in_=tb), reads=[kb], writes=[st["dkey"]])
                    return
                gcol = 0 if is_q else 1
                kxf, xf = f_rot.next()
                em.op("act", lambda e, pa=pa, xf=xf: e.activation(out=xf, in_=pa, func=AF.Identity), reads=[kp], writes=[kxf])
                ksq, sq = b_rot.next()
                em.op("act", lambda e, pa=pa, sq=sq: e.activation(out=sq, in_=pa, func=AF.Square), reads=[kp], writes=[ksq])
                kn, pn = pNT_rot.next()
                em.op("pe", lambda e, sq=sq, pn=pn: e.matmul(pn, blkb, sq, start=True, stop=True), reads=[ksq, "cb"], writes=[kn])
                kr, rr = f_rot.next()
                em.op("act", lambda e, rr=rr, pn=pn: e.activation(out=rr, in_=pn, func=AF.Ln, bias=epsD[:, 0:1], scale=1.0 / 64.0), reads=[kn, "epsD"], writes=[kr])
                em.op("act", lambda e, rr=rr: e.activation(out=rr, in_=rr, func=AF.Exp, scale=-0.5), reads=[kr], writes=[kr])
                em.op("dve", lambda e, xf=xf, rr=rr, gcol=gcol: e.scalar_tensor_tensor(out=xf, in0=xf, scalar=qkg[:, 2 * i + gcol:2 * i + gcol + 1], in1=rr, op0=ALU.mult, op1=ALU.mult),
                      reads=[kxf, kr, "qkg"], writes=[kxf])
                kxb, xb = b_rot.next()
                em.op("act", lambda e, xf=xf, xb=xb: e.activation(out=xb, in_=xf, func=AF.Identity), reads=[kxf], writes=[kxb])
                st.update(kxf=kxf, xf=xf, kr=kr, rr=rr, kxb=kxb, xb=xb)

            def qk_B(st, b=b):
                if is_na:
                    return
                is_q, kxf, xf, kr, rr, kxb, xb = st["is_q"], st["kxf"], st["xf"], st["kr"], st["rr"], st["kxb"], st["xb"]
                kp2, pr = pS_rot.next()
                em.op("pe", lambda e, xb=xb, pr=pr: e.matmul(pr, rotb, xb, start=True, stop=True), reads=[kxb, "cb"], writes=[kp2])
                em.op("dve", lambda e, rr=rr, pr=pr: e.tensor_tensor(out=rr, in0=pr, in1=sin_sb, op=ALU.mult), reads=[kp2, "sin"], writes=[kr])
                em.op("dve", lambda e, xf=xf: e.tensor_tensor(out=xf, in0=xf, in1=cos_sb, op=ALU.mult), reads=[kxf, "cos"], writes=[kxf])
                em.op("dve", lambda e, xf=xf, rr=rr: e.tensor_tensor(out=xf, in0=xf, in1=rr, op=ALU.add), reads=[kxf, kr], writes=[kxf])
                kb, tb = b_rot.next()
                em.op("act", lambda e, xf=xf, tb=tb, is_q=is_q: e.activation(out=tb, in_=xf, func=AF.Identity, scale=(0.125 if is_q else 1.0)), reads=[kxf], writes=[kb])
                if not is_q:
                    g = st["oc"] - 8
                    for tt in range(4):
                        em.op("pe", lambda e, xf=xf, tt=tt: e.transpose(pT[:, tt * 128:(tt + 1) * 128], xf[:, tt * 128:(tt + 1) * 128], idf), reads=[kxf, "idf"], writes=["pT"])
                    em.op("dve", lambda e: e.tensor_copy(out=nkst, in_=pT.rearrange("p (a b) -> p a b", b=128)[:, :, 0:64]), reads=["pT"], writes=["nkst"])
                    em.dma("sp", lambda e, g=g, b=b: e.dma_start(out=nkg_d[i][b * BW:(b + 1) * BW, g * 64:(g + 1) * 64].rearrange("(a p) n -> p a n", p=128), in_=nkst),
                           reads=["nkst"], writes=["nkg_out"])
                em.dma("sp", lambda e, dst=st["dst"], tb=tb: e.dma_start(out=dst, in_=tb), reads=[kb], writes=[st["dkey"]])
            sts = []
            for step in range(nqk + 2):
                if step < nqk:
                    sts.append(qk_mm(step))
                if 0 <= step - 1 < nqk:
                    qk_A(sts[step - 1])
                if 0 <= step - 2 < nqk:
                    qk_B(sts[step - 2])
            if stop_after == 'qk':
                break
            if is_na:
                pieces = [(1024, nkn_d, 0, False), (1536, nkn_d, 512, False), (2048, nvn_d, 0, True), (2560, nvn_d, 512, True)]
            else:
                pieces = [(1536, None, 0, True)]
            for (col0, od, ocol, isv) in pieces:
                kw, wv = wpiece_big(wqkv[:, col0:col0 + 512])
                for tt in range(4):
                    kp, pa = pA_rot.next()
                    for kc in range(8):
                        em.op("pe", lambda e, wv=wv, kc=kc, pa=pa, tt=tt: e.matmul(pa, hT[:, kc, tt * 128:(tt + 1) * 128], wv[:, kc, :], start=(kc == 0), stop=(kc == 7)), reads=[kw, "big2"], writes=[kp])
                    r0 = b * BW + tt * 128
                    kf, tf = f_rot.next()
                    em.op("act", lambda e, pa=pa, tf=tf: e.activation(out=tf, in_=pa, func=AF.Identity), reads=[kp], writes=[kf])
                    if is_na:
                        em.dma("sp", lambda e, od=od, r0=r0, ocol=ocol, tf=tf: e.dma_start(out=od[i][r0:r0 + 128, ocol:ocol + 512], in_=tf), reads=[kf], writes=["nkv_out"])
                    else:
                        em.dma("sp", lambda e, r0=r0, tf=tf: e.dma_start(out=nvg_d[i][r0:r0 + 128, :].rearrange("p (g n) -> p g n", n=64), in_=tf.rearrange("p (g n) -> p g n", n=128)[:, :, 0:64]),
                               reads=[kf], writes=["nkv_out"])
                    if isv:
                        kb, tb = b_rot.next()
                        em.op("dve", lambda e, tf=tf, tb=tb: e.tensor_copy(out=tb, in_=tf), reads=[kf], writes=[kb])
                        em.dma("sp", lambda e, r0=r0, ocol=ocol, tb=tb: e.dma_start(out=vS_d[r0:r0 + 128, ocol:ocol + 512], in_=tb), reads=[kb], writes=["vSd"])
            if stop_after == 'kv0':
                break

        if stop_after in ('proj', 'rms', 'mod', 'qk', 'kv0'):
            return True
        for c in range(8):
            kvc = c if is_na else c // 2
            if is_na:
                for p2 in range(2):
                    em.dma("sp", lambda e, c=c, p2=p2: e.dma_start(out=kt2[p2][0:64, 0:T], in_=kT_d[c][p2 * 64:(p2 + 1) * 64, :]), reads=["kTd"], writes=["kt2_%d" % p2])
                    em.dma("pool", lambda e, c=c, p2=p2: e.dma_start(out=kt2[p2][0:64, T:T + 256], in_=ckn_d[i][p2 * 64:(p2 + 1) * 64, c * 256:(c + 1) * 256]), writes=["kt2_%d" % p2])
            else:
                em.dma("sp", lambda e, kvc=kvc: e.dma_start(out=kt2[0][0:64, 0:T], in_=kT_d[kvc][0:64, :]), reads=["kTd"], writes=["kt2_0"])
            if is_na:
                em.dma("pool", lambda e, c=c: e.dma_start(out=v_sb[:, 16:18, :], in_=cvn_d[i][:, c * 128:(c + 1) * 128].rearrange("(a p) n -> p a n", p=128)), writes=["v_sb"])
            else:
                em.dma("pool", lambda e, kvc=kvc: e.dma_start(out=kt2[0][0:64, T:T + 256], in_=ckg_d[i][0:64, kvc * 256:(kvc + 1) * 256]), writes=["kt2_0"])
                em.dma("pool", lambda e, kvc=kvc: e.dma_start(out=v_sb[:, 16:18, :], in_=cvg_d[i][:, kvc * 128:(kvc + 1) * 128].rearrange("(a p) n -> p a n", p=128)), writes=["v_sb"])
            em.dma("sp", lambda e, kvc=kvc: e.dma_start(out=v_sb[:, 0:16, :], in_=vS_d[:, kvc * 128:(kvc + 1) * 128].rearrange("(a p) n -> p a n", p=128)), reads=["vSd"], writes=["v_sb"])
            if is_na:
                for par in range(2):
                    em.dma("sp", lambda e, par=par, c=c: e.dma_start(out=bias_sb[par], in_=bias_d[i][2 * c + par]), writes=["bias%d" % par])
            LOOK = 3
            items = []
            for b in range(NB):
                if is_na:
                    tiles = [t for t in range(4 * b - 2, 4 * b + 6) if 0 <= t < 16] + [16, 17]
                else:
                    tiles = list(range(18))
                for par in range(2):
                    for ti, kt in enumerate(tiles):
                        items.append((b, par, ti, kt, len(tiles)))
            qloaded = set()

            def ensure_q(b, c=c):
                if b in qloaded:
                    return
                qloaded.add(b)
                qi = (c * NB + b) % 2
                em.dma("sp", lambda e, c=c, b=b, qi=qi: e.dma_start(out=qe[qi][0:64, :], in_=qT_d[c][0:64, blkc(b)]), reads=["qTd"], writes=["qe%d" % qi])
                em.dma("sp", lambda e, c=c, b=b, qi=qi: e.dma_start(out=qo[qi][0:64, :], in_=qT_d[c][64:128, blkc(b)]), reads=["qTd"], writes=["qo%d" % qi])
                em.dma("sp", lambda e, b=b, qi=qi: e.dma_start(out=qe[qi][64:96, :], in_=indB_d[ty][:, blkc(b)]), writes=["qe%d" % qi])
                em.dma("sp", lambda e, b=b, qi=qi: e.dma_start(out=qo[qi][64:96, :], in_=indB_d[ty][:, blkc(b)]), writes=["qo%d" % qi])

            def issue_S(n, c=c):
                (b, par, ti, kt, nt) = items[n]
                ensure_q(b)
                qi = (c * NB + b) % 2
                qm = (qe if par == 0 else qo)[qi]; kq = ("qe%d" if par == 0 else "qo%d") % qi
                ksn, psn = pS4_rot.next()
                has_bias = is_na and kt < 16
                ktile = kt2[par] if is_na else kt2[0]
                kkey = ("kt2_%d" % par) if is_na else "kt2_0"
                em.op("pe", lambda e, psn=psn, kt=kt, qm=qm, ktile=ktile, has_bias=has_bias: e.matmul(psn, ktile[:, kt * 128:(kt + 1) * 128], qm, start=True, stop=(not has_bias)), reads=[kkey, kq], writes=[ksn])
                if has_bias:
                    e0 = 10 - 2 * (kt - 4 * b)
                    em.op("pe", lambda e, psn=psn, e0=e0, par=par: e.matmul(psn, idb, bias_sb[par][:, e0 * 64:e0 * 64 + 512], start=False, stop=True), reads=["cb", "bias%d" % par], writes=[ksn])
                return ksn, psn
            issued = [issue_S(n) for n in range(min(LOOK, len(items)))]
            cur_acc = None
            for n, (b, par, ti, kt, nt) in enumerate(items):
                if n + LOOK < len(items):
                    issued.append(issue_S(n + LOOK))
                ksn, psn = issued[n]
                if ti == 0:
                    cur_acc = oacc_rot.next()
                (kO, pOa, kDn, pDa) = cur_acc
                kpb, pb = p_rot.next()
                em.op("act", lambda e, psn=psn, pb=pb: e.activation(out=pb, in_=psn, func=AF.Exp), reads=[ksn], writes=[kpb])
                first = (ti == 0); last = (ti == nt - 1)
                em.op("pe", lambda e, pb=pb, kt=kt, first=first, last=last, pOa=pOa: e.matmul(pOa, v_sb[:, kt, :], pb, start=first, stop=last), reads=[kpb, "v_sb"], writes=[kO])
                em.op("pe", lambda e, pb=pb, first=first, last=last, pDa=pDa: e.matmul(pDa, onesb, pb, start=first, stop=last), reads=[kpb, "cb"], writes=[kDn])
                if last:
                    hs = slice(par * 64, par * 64 + 64)
                    kf, tf = f_rot.next()
                    em.op("act", lambda e, tf=tf, hs=hs, pDa=pDa: e.activation(out=tf[hs, :], in_=pDa[hs, :], func=AF.Ln), reads=[kDn], writes=[kf])
                    em.op("act", lambda e, tf=tf, hs=hs: e.activation(out=tf[hs, :], in_=tf[hs, :], func=AF.Exp, scale=-1.0), reads=[kf], writes=[kf])
                    ko, to = ostg_rot.next()
                    em.op("dve", lambda e, tf=tf, to=to, hs=hs, pOa=pOa: e.tensor_tensor(out=to[hs, :], in0=pOa[hs, :], in1=tf[hs, :], op=ALU.mult), reads=[kO, kf], writes=[ko])
                    em.dma("sp", lambda e, to=to, hs=hs, c=c, b=b: e.dma_start(out=oT_d[c][hs, blkc(b)], in_=to[hs, :]), reads=[ko], writes=["oTd"])

        if stop_after == 'attn':
            return True
        for b in range(NB):
            em.dma("sp", lambda e, b=b: e.dma_start(out=blkin[:, :, 0:BW], in_=oT_d[:, :, blkc(b)].rearrange("c p n -> p c n")), reads=["oTd"], writes=["blkin"])
            for oc in range(8):
                kw, wv = wpiece_small(wo[:, oc * 128:(oc + 1) * 128])
                kp, pa = pA_rot.next()
                for kc in range(8):
                    em.op("pe", lambda e, wv=wv, kc=kc, pa=pa: e.matmul(pa, wv[:, kc, :], blkin[:, kc, 0:BW], start=(kc == 0), stop=(kc == 7)), reads=[kw, "blkin"], writes=[kp])
                em.op("act", lambda e, pa=pa, oc=oc: e.activation(out=yT[:, oc, :], in_=pa, func=AF.Identity), reads=[kp], writes=["yT"])
            postnorm_residual(b, 8)

        if stop_after == 'wo':
            return True
        for b in range(NB):
            rms_rstd(lambda c, b=b: xT[:, c, blkc(b)], ["xT"], 1.0 / D)
            modulate(b, 16, 24, lambda c: hT[:, c, :], "big2")
            em.dma("sp", lambda e, b=b: e.dma_start(out=h2_d[:, :, blkc(b)].rearrange("c p n -> p c n"), in_=hT), reads=["big2"], writes=["h2d"])
        for b in range(NB):
            lo = max(b * BW - 1, 0); hi = min((b + 1) * BW + 1, T)
            o0 = lo - (b * BW - 1)
            em.dma("sp", lambda e, lo=lo, hi=hi, o0=o0: e.dma_start(out=blkin[:, :, o0:o0 + hi - lo], in_=h2_d[:, :, lo:hi].rearrange("c p n -> p c n")), reads=["h2d"], writes=["blkin"])
            if b == 0:
                em.op("dve", lambda e: e.memset(blkin[:, :, 0:1], 0.0), writes=["blkin"])
            if b == NB - 1:
                em.op("dve", lambda e: e.memset(blkin[:, :, BW + 1:BW + 2], 0.0), writes=["blkin"])
            for j in range(NJ):
                accs = []
                for half in range(2):
                    jj = half * NJ + j
                    col0 = half * DFF + j * 128
                    kw, wv = wpiece_small(wup_d[l][:, col0:col0 + 128])
                    kp, pa = pF_rot.next()
                    kh, ph = pH_rot.next()
                    for kc in range(8):
                        em.op("pe", lambda e, wv=wv, kc=kc, pa=pa: e.matmul(pa, wv[:, kc, :], blkin[:, kc, 1:BW + 1], start=(kc == 0), stop=(kc == 7)), reads=[kw, "blkin"], writes=[kp])
                    for kc in range(8):
                        em.op("pe", lambda e, wv=wv, kc=kc, ph=ph: e.matmul(ph[:, 0:2], wv[:, kc, :], blkin[:, kc, 0:BW + 2:BW + 1], start=(kc == 0), stop=(kc == 7)), reads=[kw, "blkin"], writes=[kh])
                    ka, ac = acc_rot.next()
                    ku, us = u_rot.next()
                    em.op("act", lambda e, pa=pa, ac=ac, jj=jj: e.activation(out=ac, in_=pa, func=AF.Identity, bias=convp[:, jj * 4 + 3:jj * 4 + 4], scale=convp[:, jj * 4 + 1:jj * 4 + 2]), reads=[kp, "convp"], writes=[ka])
                    em.op("act", lambda e, pa=pa, us=us: e.activation(out=us[:, 1:BW + 1], in_=pa, func=AF.Identity), reads=[kp], writes=[ku])
                    em.op("dve", lambda e, us=us, ph=ph: e.tensor_copy(out=us[:, 0:BW + 2:BW + 1], in_=ph[:, 0:2]), reads=[kh], writes=[ku])
                    em.op("dve", lambda e, us=us, ac=ac, jj=jj: e.scalar_tensor_tensor(out=ac, in0=us[:, 0:BW], scalar=convp[:, jj * 4:jj * 4 + 1], in1=ac, op0=ALU.mult, op1=ALU.add), reads=[ku, ka, "convp"], writes=[ka])
                    em.op("dve", lambda e, us=us, ac=ac, jj=jj: e.scalar_tensor_tensor(out=ac, in0=us[:, 2:BW + 2], scalar=convp[:, jj * 4 + 2:jj * 4 + 3], in1=ac, op0=ALU.mult, op1=ALU.add), reads=[ku, ka, "convp"], writes=[ka])
                    em.op("dve", lambda e, us=us, ac=ac, jj=jj: e.scalar_tensor_tensor(out=ac[:, 0:BW:256], in0=us[:, 0:BW:256], scalar=wcor[:, jj * 2:jj * 2 + 1], in1=ac[:, 0:BW:256], op0=ALU.mult, op1=ALU.add), reads=[ku, ka, "wcor"], writes=[ka])
                    em.op("dve", lambda e, us=us, ac=ac, jj=jj: e.scalar_tensor_tensor(out=ac[:, 255:BW:256], in0=us[:, 257:BW + 2:256], scalar=wcor[:, jj * 2 + 1:jj * 2 + 2], in1=ac[:, 255:BW:256], op0=ALU.mult, op1=ALU.add), reads=[ku, ka, "wcor"], writes=[ka])
                    accs.append((ka, ac))
                (kaa, aa), (kag, ag) = accs
                em.op("act", lambda e, aa=aa: e.activation(out=aa, in_=aa, func=AF.Silu), reads=[kaa], writes=[kaa])
                em.op("dve", lambda e, aa=aa, ag=ag, j=j: e.tensor_tensor(out=big2[:, j, :], in0=aa, in1=ag, op=ALU.mult), reads=[kaa, kag], writes=["big2"])
            for oc in range(8):
                kw, wv = wpiece_big(wdn_d[l][:, oc * 128:(oc + 1) * 128], nk=NJ, ncol=128)
                kp, pa = pA_rot.next()
                for j in range(NJ):
                    em.op("pe", lambda e, wv=wv, j=j, pa=pa: e.matmul(pa, wv[:, j, :], big2[:, j, :], start=(j == 0), stop=(j == NJ - 1)), reads=[kw, "big2"], writes=[kp])
                em.op("act", lambda e, pa=pa, oc=oc: e.activation(out=yT[:, oc, :], in_=pa, func=AF.Identity), reads=[kp], writes=["yT"])
            postnorm_residual(b, 24)

    for l in range(n_layers):
        if do_layer(l):
            break

    for t in range(16):
        yo = xin[t % 2]
        for g4 in range(2):
            for c4 in range(4):
                c = g4 * 4 + c4
                em.op("pe", lambda e, c=c, c4=c4, t=t: e.transpose(pT[:, c4 * 128:(c4 + 1) * 128], xT[:, c, t * 128:(t + 1) * 128], idf), reads=["xT", "idf"], writes=["pT"])
            em.op("dve", lambda e, g4=g4, yo=yo: e.tensor_copy(out=yo[:, g4 * 512:(g4 + 1) * 512], in_=pT), reads=["pT"], writes=["yT"])
        em.dma("sp", lambda e, t=t, yo=yo: e.dma_start(out=y_d[t * 128:(t + 1) * 128, :], in_=yo), reads=["yT"], writes=["y_out"])

    em.finish()
    em.build()
    return nc


def _fm(v):
    return np.ascontiguousarray(np.asarray(v, np.float32).reshape(-1, 128).T)


def prepare(x_prompt, x_sample, cache_na_k, cache_na_v, cache_gqa_k, cache_gqa_v, c, c_ctx,
           ada_w, ada_b, norm_mix_pre, norm_mix_post, norm_ffn_pre, norm_ffn_post,
           na_w_qkv, na_w_o, na_rpb, gqa_w_qkv, gqa_w_o, gqa_q_norm, gqa_k_norm,
           ffn_w_up, ffn_conv_w, ffn_conv_b, ffn_w_down):
    f32 = np.float32
    bf = ml_dtypes.bfloat16
    A = lambda a: np.ascontiguousarray(np.asarray(a, f32))
    x_prompt = A(x_prompt); x_sample = A(x_sample)
    shared = {}
    shared["ada_w"] = A(ada_w)
    shared["ada_b"] = np.concatenate([_fm(np.asarray(ada_b)[l]) for l in range(DEPTH)], axis=1)
    gl = []
    for l in range(DEPTH):
        gl += [_fm(np.asarray(norm_mix_pre)[l]), _fm(np.asarray(norm_mix_post)[l]), _fm(np.asarray(norm_ffn_pre)[l]), _fm(np.asarray(norm_ffn_post)[l])]
    shared["gains"] = np.ascontiguousarray(np.concatenate(gl, axis=1))
    shared["wqkv_na"] = A(na_w_qkv); shared["wo_na"] = A(na_w_o); shared["wo_g"] = A(gqa_w_o)
    wg = np.asarray(gqa_w_qkv, f32)
    wq = wg[:, :, :1024]
    wk = wg[:, :, 1024:1280].reshape(2, 1024, 4, 1, 64)
    wv = wg[:, :, 1280:1536].reshape(2, 1024, 4, 1, 64)
    wkd = np.broadcast_to(wk, (2, 1024, 4, 2, 64)).reshape(2, 1024, 512)
    wvd = np.broadcast_to(wv, (2, 1024, 4, 2, 64)).reshape(2, 1024, 512)
    shared["wqkv_g"] = np.ascontiguousarray(np.concatenate([wq, wkd, wvd], axis=2))
    qn = np.asarray(gqa_q_norm, f32); kn = np.asarray(gqa_k_norm, f32)
    qkg = np.zeros((128, 4), f32)
    for i in range(2):
        qkg[:, 2 * i] = np.tile(qn[i], 2); qkg[:, 2 * i + 1] = np.tile(kn[i], 2)
    shared["qkg"] = qkg
    shared["w_up"] = A(ffn_w_up); shared["w_down"] = A(ffn_w_down)
    cw = np.asarray(ffn_conv_w, f32); cbias = np.asarray(ffn_conv_b, f32)
    convp = np.zeros((DEPTH, 128, 44, 4), f32)
    for l in range(DEPTH):
        for k in range(3):
            convp[l, :, :, k] = cw[l, k].reshape(44, 128).T
        convp[l, :, :, 3] = cbias[l].reshape(44, 128).T
    shared["convp"] = convp.reshape(DEPTH, 128, 176)
    shared["ident_f"] = np.eye(128, dtype=f32)
    blk = np.zeros((128, 128), f32); blk[:64, :64] = 1; blk[64:, 64:] = 1
    rot = np.zeros((128, 128), f32)
    for p in range(128):
        d = p % 32
        rot[p, p + 16 if d < 16 else p - 16] = 1.0
    shared["constb"] = np.ascontiguousarray(np.concatenate([np.eye(128, dtype=f32), np.ones((128, 128), f32), blk, rot], axis=1)).astype(bf)
    rpb = np.asarray(na_rpb, f32)
    p = np.arange(128); hi = (p >= 64).astype(int); kc = p % 64
    ep = np.arange(22); qc = np.arange(64)
    dr = 17 - ep[None, :] + hi[:, None]
    dc = np.clip(kc[:, None] - qc[None, :] + 15, 0, 30)
    cst = np.clip(qc - 8, 0, 48)
    colok = (kc[:, None] >= cst[None, :]) & (kc[:, None] < cst[None, :] + 16)
    drv_ok = (dr >= 0) & (dr <= 14)
    drc = np.clip(dr, 0, 14)
    tab = rpb[:, :, drc[:, :, None], dc[:, None, :]]
    tab = np.where(drv_ok[None, None, :, :, None], tab, 0.0)
    tab = np.where(colok[None, None, :, None, :], tab, NEG)
    bias_sample = np.ascontiguousarray(tab.reshape(2, 16, 128, 1408)).astype(bf)
    bias_prompt = np.zeros((2, 16, 128, 1408), bf)
    tok = np.arange(T)
    indA_p = np.zeros((2, 32, 2304), f32); indB_p = np.zeros((2, 32, 2048), f32)
    seq = tok // 256
    for j in range(8):
        indA_p[:, j, :T] = (seq == j)
        indA_p[:, j, T:] = 1.0
        indB_p[:, j, :] = np.where(seq == j, 0.0, NEG)
    indA_s = np.zeros((2, 32, 2304), f32); indB_s = np.zeros((2, 32, 2048), f32)
    row = tok // 64
    rs = np.clip(row - 4, 0, 24)
    for j in range(32):
        indA_s[0, j, :T] = (row == j)
        indB_s[0, j, :] = np.where((j >= rs) & (j < rs + 8), 0.0, NEG)
    d = np.arange(128) % 64
    fidx = d % 16
    freqs = (10000.0 ** (-(np.arange(16, dtype=f32)) / 16)).astype(f32)
    posr = (tok // 64).astype(f32); posc = (tok % 64).astype(f32)
    pos = np.where((d < 32)[:, None], posr[None, :], posc[None, :]).astype(f32)
    ang = (pos * freqs[fidx][:, None]).astype(f32)
    cos_s = np.cos(ang).astype(f32)
    sgn = np.where((d % 32) < 16, -1.0, 1.0).astype(f32)
    sin_s = (np.sin(ang).astype(f32) * sgn[:, None]).astype(f32)
    cos_p = np.ones((128, T), f32); sin_p = np.zeros((128, T), f32)
    cnk = np.asarray(cache_na_k, f32); cnv = np.asarray(cache_na_v, f32)
    cgk = np.asarray(cache_gqa_k, f32); cgv = np.asarray(cache_gqa_v, f32)

    def ctx_for(bb):
        out = {}
        k = cnk[bb].reshape(2, 256, 8, 128)
        out["ctxkT_na"] = np.ascontiguousarray(k.transpose(0, 3, 2, 1).reshape(2, 128, 2048))
        out["ctxv_na"] = np.ascontiguousarray(cnv[bb].reshape(2, 256, 1024))
        kg = cgk[bb]
        kgd = np.broadcast_to(kg[:, :, :, None, :], (2, 256, 4, 2, 64)).reshape(2, 256, 4, 128)
        out["ctxkT_g"] = np.ascontiguousarray(kgd.transpose(0, 3, 2, 1).reshape(2, 128, 1024))
        vg = cgv[bb]
        out["ctxv_g"] = np.ascontiguousarray(np.broadcast_to(vg[:, :, :, None, :], (2, 256, 4, 2, 64)).reshape(2, 256, 512))
        return out
    zero_ctx = {"ctxkT_na": np.zeros((2, 128, 2048), f32), "ctxv_na": np.zeros((2, 256, 1024), f32),
                "ctxkT_g": np.zeros((2, 128, 1024), f32), "ctxv_g": np.zeros((2, 256, 512), f32)}

    in_maps = []
    for core in range(NC8):
        m = dict(shared)
        role = core if core < 6 else core - 6
        if role < 4:
            m["x"] = np.ascontiguousarray(x_prompt[role * 8:(role + 1) * 8].reshape(T, D))
            m["cond"] = _fm(c_ctx)
            m["biasT"] = bias_prompt
            m["indA"] = indA_p.astype(bf); m["indB"] = indB_p.astype(bf)
            m["cosT"] = cos_p; m["sinT"] = sin_p
            m["cneg"] = np.full((128, 1), -1.0, f32)
            m.update(zero_ctx)
        else:
            bb = role - 4
            m["x"] = np.ascontiguousarray(x_sample[bb])
            m["cond"] = _fm(np.asarray(c)[bb])
            m["biasT"] = bias_sample
            m["indA"] = indA_s.astype(bf); m["indB"] = indB_s.astype(bf)
            m["cosT"] = cos_s; m["sinT"] = sin_s
            m["cneg"] = np.zeros((128, 1), f32)
            m.update(ctx_for(bb))
        in_maps.append(m)

    return in_maps


def assemble(R):
    f32 = np.float32
    y_prompt = np.concatenate([np.asarray(R[k]["y"], f32).reshape(8, 256, D) for k in range(4)], axis=0)
    y_sample = np.stack([np.asarray(R[4]["y"], f32), np.asarray(R[5]["y"], f32)], axis=0)

    def gather(name, feat):
        parts = [np.asarray(R[k][name], f32).reshape(2, 8, 256, feat).transpose(1, 0, 2, 3) for k in range(4)]
        return np.concatenate(parts, axis=0)
    na_k = gather("nk_na", 1024).reshape(32, 2, 256, 16, 64)
    na_v = gather("nv_na", 1024).reshape(32, 2, 256, 16, 64)
    g_k = gather("nk_g", 256).reshape(32, 2, 256, 4, 64)
    g_v = gather("nv_g", 256).reshape(32, 2, 256, 4, 64)
    return (y_prompt, y_sample, na_k, na_v, g_k, g_v)


def kernel(**inputs):
    in_maps = prepare(**inputs)
    nc = build_program()
    res = run_bass_kernel_spmd(nc, in_maps, core_ids=list(range(NC8)))
    return assemble(res.results)
```
